# Optimizing a Trainium2 kernel written in Bass

```python
import jax, jax.numpy as jnp
from jax import lax
import numpy as np

D_MODEL = 2048
BATCH = 32
SEQ = 256
DEPTH = 1
DEC_BATCH = 4
DEC_SEQ = 4096
PAST_LEN = 256

GRID_W = 64
HEAD_DIM = 128
ATT_HEADS = 8
ATT_KV_HEADS = 2
ATT_WIDTH = ATT_HEADS * HEAD_DIM
KV_WIDTH = ATT_KV_HEADS * HEAD_DIM
HG_HEADS = 8
HG_DK = 128
HG_DV = 128
HG_WIDTH = HG_HEADS * HG_DK
HG_VWIDTH = HG_HEADS * HG_DV
D_FF = -(-8 * D_MODEL // 768) * 256
ROPE_AXIS_DIM = HEAD_DIM // 2
ROPE_THETA = 10000.0
Q_BLOCK = 128
HG_CHUNK = 32
NORM_EPS = 1e-6
IN_WIDTH = ATT_WIDTH + 2 * KV_WIDTH + 3 * HG_WIDTH + 2 * HG_VWIDTH + 2 * D_MODEL

kernel_name = "hybrid_gqa_hgrn2_diffusion_step"


def rmsnorm(x, g):
    xf = x.astype(jnp.float32)
    y = xf * lax.rsqrt(jnp.mean(xf * xf, axis=-1, keepdims=True) + NORM_EPS)
    return (y * g.astype(jnp.float32)).astype(x.dtype)


def modulated_norm(x, g, shift, scale):
    return rmsnorm(x, g) * (1 + scale) + shift


def ada_mod(cond, w, b):
    m = jax.nn.silu(cond) @ w + b
    return jnp.split(m[:, None, :], 6, axis=-1)


def split_projection(h, w):
    sizes = (ATT_WIDTH, KV_WIDTH, KV_WIDTH, HG_WIDTH, HG_WIDTH, HG_WIDTH, HG_VWIDTH, HG_VWIDTH, D_MODEL, D_MODEL)
    offsets = np.cumsum(sizes)[:-1].tolist()
    return jnp.split(h @ w, offsets, axis=-1)


def axial_rope_tables(n_tokens):
    rows = n_tokens // GRID_W
    half = ROPE_AXIS_DIM // 2
    r = jnp.repeat(jnp.arange(rows), GRID_W).astype(jnp.float32)
    col = jnp.tile(jnp.arange(GRID_W), rows).astype(jnp.float32)
    inv = ROPE_THETA ** (-jnp.arange(half, dtype=jnp.float32) / half)
    ang = jnp.concatenate([r[:, None] * inv, col[:, None] * inv], axis=-1)
    return jnp.cos(ang), jnp.sin(ang)


def apply_axial_rope(x, cos, sin):
    half = ROPE_AXIS_DIM // 2
    xf = x.astype(jnp.float32)

    def rot(a, cs, sn):
        a1, a2 = a[..., :half], a[..., half:]
        cs = cs[None, :, None, :]
        sn = sn[None, :, None, :]
        return jnp.concatenate([a1 * cs - a2 * sn, a2 * cs + a1 * sn], axis=-1)

    xr = rot(xf[..., :ROPE_AXIS_DIM], cos[:, :half], sin[:, :half])
    xc = rot(xf[..., ROPE_AXIS_DIM:], cos[:, half:], sin[:, half:])
    return jnp.concatenate([xr, xc], axis=-1).astype(x.dtype)


def attention_qkv(qa, ka, va, q_norm, k_norm):
    B, T, _ = qa.shape
    q = rmsnorm(qa.reshape(B, T, ATT_HEADS, HEAD_DIM), q_norm)
    k = rmsnorm(ka.reshape(B, T, ATT_KV_HEADS, HEAD_DIM), k_norm)
    v = va.reshape(B, T, ATT_KV_HEADS, HEAD_DIM)
    return q, k, v


def block_attention(q, k, v):
    B, T, H, hd = q.shape
    G = H // ATT_KV_HEADS
    nb = T // Q_BLOCK
    qb = q.reshape(B, nb, Q_BLOCK, ATT_KV_HEADS, G, hd).transpose(1, 0, 2, 3, 4, 5).astype(jnp.float32)
    kf = k.astype(jnp.float32)
    vf = v.astype(jnp.float32)
    scale = HEAD_DIM ** -0.5

    def one_block(qblk):
        s = jnp.einsum("bqkgd,bskd->bkgqs", qblk, kf) * scale
        p = jax.nn.softmax(s, axis=-1)
        return jnp.einsum("bkgqs,bskd->bqkgd", p, vf)

    o = lax.map(one_block, qb)
    return o.transpose(1, 0, 2, 3, 4, 5).reshape(B, T, H * hd).astype(q.dtype)


def lower_bound(lb_param, l):
    return jnp.cumsum(jax.nn.softmax(lb_param.astype(jnp.float32), axis=0), axis=0)[l]


def forget_gate(z, lb):
    zf = z.astype(jnp.float32)
    logf = jnp.logaddexp(jnp.log(lb), jnp.log1p(-lb) + jax.nn.log_sigmoid(zf))
    k = (1 - lb) * jax.nn.sigmoid(-zf)
    return logf, k


def hgrn2_chunk_scan(q, k, logf, v, s0):
    B, T, H, dk = q.shape
    dv = v.shape[-1]
    n = T // HG_CHUNK

    def to_chunks(a):
        return a.reshape(B, n, HG_CHUNK, H, a.shape[-1]).transpose(1, 0, 3, 2, 4)

    mask = jnp.tril(jnp.ones((HG_CHUNK, HG_CHUNK), dtype=bool))[:, :, None]

    def step(S, inp):
        qc, kc, lfc, vc = inp
        bcum = jnp.cumsum(lfc, axis=2)
        diff = bcum[:, :, :, None, :] - bcum[:, :, None, :, :]
        decay = jnp.where(mask, jnp.exp(jnp.where(mask, diff, 0.0)), 0.0)
        a = jnp.einsum("bhtd,bhsd,bhtsd->bhts", qc, kc, decay)
        o = jnp.einsum("bhts,bhsv->bhtv", a, vc) + jnp.einsum("bhtd,bhdv->bhtv", qc * jnp.exp(bcum), S)
        btot = bcum[:, :, -1:, :]
        S_new = jnp.exp(btot[:, :, 0, :])[..., None] * S + jnp.einsum("bhsd,bhsv->bhdv", kc * jnp.exp(btot - bcum), vc)
        return S_new, o

    s_fin, o = lax.scan(step, s0.astype(jnp.float32), (to_chunks(q), to_chunks(k), to_chunks(logf), to_chunks(v)))
    o = o.transpose(1, 0, 3, 2, 4).reshape(B, T, H, dv)
    return o, s_fin


def hgrn2_branch(q_raw, zf_raw, zb_raw, v_raw, og_raw, lb_f, lb_b, hg_norm, s_f0, s_b0):
    B, T, _ = q_raw.shape
    q = jax.nn.silu(q_raw.astype(jnp.float32)).reshape(B, T, HG_HEADS, HG_DK)
    v = v_raw.astype(jnp.float32).reshape(B, T, HG_HEADS, HG_DV)
    logf_f, k_f = forget_gate(zf_raw, lb_f)
    logf_b, k_b = forget_gate(zb_raw, lb_b)
    shp = (B, T, HG_HEADS, HG_DK)
    o_f, s_f = hgrn2_chunk_scan(q, k_f.reshape(shp), logf_f.reshape(shp), v, s_f0)

    def flip(a):
        return jnp.flip(a, axis=1)

    o_b, s_b = hgrn2_chunk_scan(flip(q), flip(k_b.reshape(shp)), flip(logf_b.reshape(shp)), flip(v), s_b0)
    o = o_f + flip(o_b)
    o = rmsnorm(o, hg_norm) * jax.nn.silu(og_raw.astype(jnp.float32)).reshape(B, T, HG_HEADS, HG_DV)
    return o.reshape(B, T, HG_VWIDTH).astype(q_raw.dtype), s_f, s_b


def merge_branches(att, hg, ga, gb, w_br_att, w_br_hg, w_out):
    m = jax.nn.sigmoid(ga) * (att @ w_br_att) + jax.nn.sigmoid(gb) * (hg @ w_br_hg)
    return m @ w_out


def ffn_sublayer(x, sh2, sc2, g2, lp):
    h = modulated_norm(x, lp["norm_pre_ffn"], sh2, sc2)
    gate, up = jnp.split(h @ lp["w_ffn_in"], 2, axis=-1)
    f = (jax.nn.silu(gate) * up) @ lp["w_ffn_out"]
    return x + g2 * rmsnorm(f, lp["norm_post_ffn"])


def context_layer(x, c_ctx, lp, lb_f, lb_b):
    B, T, _ = x.shape
    sh1, sc1, g1, sh2, sc2, g2 = ada_mod(c_ctx[None, :], lp["w_ada"], lp["b_ada"])
    h = modulated_norm(x, lp["norm_pre_mix"], sh1, sc1)
    qa, ka, va, qh, zf, zb, vh, og, ga, gb = split_projection(h, lp["w_in"])
    q, k, v = attention_qkv(qa, ka, va, lp["q_norm"], lp["k_norm"])
    att = block_attention(q, k, v)
    zero = jnp.zeros((B, HG_HEADS, HG_DK, HG_DV), jnp.float32)
    hg, s_f, s_b = hgrn2_branch(qh, zf, zb, vh, og, lb_f, lb_b, lp["hg_norm"], zero, zero)
    mo = merge_branches(att, hg, ga, gb, lp["w_br_att"], lp["w_br_hg"], lp["w_out"])
    x = x + g1 * rmsnorm(mo, lp["norm_post_mix"])
    x = ffn_sublayer(x, sh2, sc2, g2, lp)
    return x, k, v, s_f, s_b


def latent_layer(x, c, k_ctx, v_ctx, s_f0, s_b0, lp, lb_f, lb_b, cos, sin):
    sh1, sc1, g1, sh2, sc2, g2 = ada_mod(c, lp["w_ada"], lp["b_ada"])
    h = modulated_norm(x, lp["norm_pre_mix"], sh1, sc1)
    qa, ka, va, qh, zf, zb, vh, og, ga, gb = split_projection(h, lp["w_in"])
    q, k, v = attention_qkv(qa, ka, va, lp["q_norm"], lp["k_norm"])
    q = apply_axial_rope(q, cos, sin)
    k = apply_axial_rope(k, cos, sin)
    k_all = jnp.concatenate([k, k_ctx.astype(k.dtype)], axis=1)
    v_all = jnp.concatenate([v, v_ctx.astype(v.dtype)], axis=1)
    att = block_attention(q, k_all, v_all)
    hg, _, _ = hgrn2_branch(qh, zf, zb, vh, og, lb_f, lb_b, lp["hg_norm"], s_f0, s_b0)
    mo = merge_branches(att, hg, ga, gb, lp["w_br_att"], lp["w_br_hg"], lp["w_out"])
    x = x + g1 * rmsnorm(mo, lp["norm_post_mix"])
    return ffn_sublayer(x, sh2, sc2, g2, lp)


def setup_inputs(seed: int = 0) -> dict:
    key = jax.random.key(seed)
    ks = jax.random.split(key, 26)

    def nrm(k, shape, scale):
        return jax.random.normal(k, shape, jnp.float32) * scale

    def gain(k, shape):
        return 1.0 + 0.05 * jax.random.normal(k, shape, jnp.float32)

    D = D_MODEL
    return {
        "x_prompt": nrm(ks[0], (BATCH, SEQ, D), 1.0),
        "x_sample": nrm(ks[1], (DEC_BATCH, DEC_SEQ, D), 1.0),
        "cache_k": nrm(ks[2], (DEC_BATCH, DEPTH, PAST_LEN, ATT_KV_HEADS, HEAD_DIM), 1.0),
        "cache_v": nrm(ks[3], (DEC_BATCH, DEPTH, PAST_LEN, ATT_KV_HEADS, HEAD_DIM), 1.0),
        "state_fwd": nrm(ks[4], (DEC_BATCH, DEPTH, HG_HEADS, HG_DK, HG_DV), 0.5),
        "state_bwd": nrm(ks[5], (DEC_BATCH, DEPTH, HG_HEADS, HG_DK, HG_DV), 0.5),
        "c": nrm(ks[6], (DEC_BATCH, D), 1.0),
        "c_ctx": nrm(ks[7], (D,), 1.0),
        "w_ada": nrm(ks[8], (DEPTH, D, 6 * D), D ** -0.5),
        "b_ada": nrm(ks[9], (DEPTH, 6 * D), 0.02),
        "norm_pre_mix": gain(ks[10], (DEPTH, D)),
        "norm_post_mix": gain(ks[11], (DEPTH, D)),
        "norm_pre_ffn": gain(ks[12], (DEPTH, D)),
        "norm_post_ffn": gain(ks[13], (DEPTH, D)),
        "w_in": nrm(ks[14], (DEPTH, D, IN_WIDTH), D ** -0.5),
        "q_norm": gain(ks[15], (DEPTH, HEAD_DIM)),
        "k_norm": gain(ks[16], (DEPTH, HEAD_DIM)),
        "lb_fwd": nrm(ks[17], (DEPTH + 1, HG_WIDTH), 0.5),
        "lb_bwd": nrm(ks[18], (DEPTH + 1, HG_WIDTH), 0.5),
        "hg_norm": gain(ks[19], (DEPTH, HG_DV)),
        "w_br_att": nrm(ks[20], (DEPTH, ATT_WIDTH, D), ATT_WIDTH ** -0.5),
        "w_br_hg": nrm(ks[21], (DEPTH, HG_VWIDTH, D), HG_VWIDTH ** -0.5),
        "w_out": nrm(ks[22], (DEPTH, D, D), D ** -0.5),
        "w_ffn_in": nrm(ks[23], (DEPTH, D, 2 * D_FF), D ** -0.5),
        "w_ffn_out": nrm(ks[24], (DEPTH, D_FF, D), D_FF ** -0.5),
    }


def reference(x_prompt, x_sample, cache_k, cache_v, state_fwd, state_bwd, c, c_ctx, w_ada, b_ada,
              norm_pre_mix, norm_post_mix, norm_pre_ffn, norm_post_ffn, w_in, q_norm, k_norm,
              lb_fwd, lb_bwd, hg_norm, w_br_att, w_br_hg, w_out, w_ffn_in, w_ffn_out):
    cos, sin = axial_rope_tables(x_sample.shape[1])
    xp = x_prompt
    xs = x_sample
    ks_, vs_, sf_, sb_ = [], [], [], []
    for l in range(DEPTH):
        lp = dict(w_ada=w_ada[l], b_ada=b_ada[l], norm_pre_mix=norm_pre_mix[l], norm_post_mix=norm_post_mix[l],
                  norm_pre_ffn=norm_pre_ffn[l], norm_post_ffn=norm_post_ffn[l], w_in=w_in[l],
                  q_norm=q_norm[l], k_norm=k_norm[l], hg_norm=hg_norm[l], w_br_att=w_br_att[l],
                  w_br_hg=w_br_hg[l], w_out=w_out[l], w_ffn_in=w_ffn_in[l], w_ffn_out=w_ffn_out[l])
        lb_f = lower_bound(lb_fwd, l)
        lb_b = lower_bound(lb_bwd, l)
        xp, k_c, v_c, s_f, s_b = context_layer(xp, c_ctx, lp, lb_f, lb_b)
        ks_.append(k_c)
        vs_.append(v_c)
        sf_.append(s_f)
        sb_.append(s_b)
        xs = latent_layer(xs, c, cache_k[:, l], cache_v[:, l], state_fwd[:, l], state_bwd[:, l],
                          lp, lb_f, lb_b, cos, sin)
    new_cache_k = jnp.stack(ks_, axis=1)
    new_cache_v = jnp.stack(vs_, axis=1)
    new_state_fwd = jnp.stack(sf_, axis=1)
    new_state_bwd = jnp.stack(sb_, axis=1)
    return (xp, xs, new_cache_k, new_cache_v, new_state_fwd, new_state_bwd)
```

```python
import numpy as np
from contextlib import ExitStack
import concourse.bass as bass
import concourse.mybir as mybir
from concourse.bass_utils import run_bass_kernel_spmd

F32 = mybir.dt.float32
BF16 = mybir.dt.bfloat16
AF = mybir.ActivationFunctionType
ALU = mybir.AluOpType
AX = mybir.AxisListType

D = 2048
DFF = 5632
NIN = 10752
EPS = 1e-6
C_QA, C_KA, C_VA, C_QH, C_ZA, C_ZB, C_VH, C_OG, C_GA, C_GB = 0, 1024, 1280, 1536, 2560, 3584, 4608, 5632, 6656, 8704
NPB = 2
NSB = 4
ROW_S = 1024

ENGS = ("pe", "act", "dve", "pool", "sp")
NRING = {"sp": 12, "pool": 6}


class T:
    def __init__(self, h, name=""):
        self.h = h
        self.name = name
        self.w = {}
        self.r = {}
        self.psum = False
        self.pw_open = False

    def __getitem__(self, idx):
        return self.h[idx]


class Prog:
    def __init__(self, nc):
        self.nc = nc
        self.q = {e: [] for e in ENGS}
        self.cnt = {e: 0 for e in ENGS}
        self.known = {e: {} for e in ENGS}
        self.dma_i = {"sp": 0, "pool": 0}
        self.final = {}
        self.stack = ExitStack()
        self.nalloc = 0

    def sb(self, shape, dt, name=None):
        self.nalloc += 1
        name = "b_" + (name or f"sb{self.nalloc}")
        h = self.stack.enter_context(self.nc.sbuf_tensor(name, list(shape), dt))
        return T(h, name)

    def ps(self, shape, dt, name=None):
        self.nalloc += 1
        name = "p_" + (name or f"ps{self.nalloc}")
        h = self.stack.enter_context(self.nc.psum_tensor(name, list(shape), dt))
        t = T(h, name)
        t.psum = True
        return t

    def view(self, t, name=""):
        return T(t.h, name)

    def _need(self, eng, reads, writes, pwrites=()):
        need = {}

        def add(ev):
            k, v = ev
            if need.get(k, 0) < v:
                need[k] = v

        for t in reads:
            for ev in t.w.items():
                add(ev)
            if t.psum:
                for k, v in t.r.items():
                    if k != eng:
                        add((k, v))
        for t in writes:
            for ev in t.w.items():
                add(ev)
            for ev in t.r.items():
                add(ev)
        for t in pwrites:
            if t.r or not t.pw_open:
                for ev in t.w.items():
                    add(ev)
            for ev in t.r.items():
                add(ev)
        out = []
        for k, v in need.items():
            if k == eng and eng == "pe":
                continue
            if self.known[eng].get(k, 0) >= v:
                continue
            self.known[eng][k] = v
            out.append((k, v))
        return out

    def _mark(self, ev, reads, writes, pwrites=()):
        k, v = ev
        for t in reads:
            if t.r.get(k, 0) < v:
                t.r[k] = v
        for t in writes:
            t.w = {k: v}
            t.r = {}
            t.pw_open = False
        for t in pwrites:
            if t.r or not t.pw_open:
                t.w = {}
                t.r = {}
                t.pw_open = True
            if t.w.get(k, 0) < v:
                t.w[k] = v

    def op(self, eng, fn, reads=(), writes=(), pwrites=()):
        for k, v in self._need(eng, reads, writes, pwrites):
            self.q[eng].append(("wait", k, v))
        self.cnt[eng] += 1
        self.q[eng].append(("op", fn))
        self._mark((eng, self.cnt[eng]), reads, writes, pwrites)

    def dma(self, queue, out_ap, in_ap, reads=(), writes=(), is_output=False):
        for k, v in self._need(queue, reads, writes):
            self.q[queue].append(("wait", k, v))
        i = self.dma_i[queue]
        self.dma_i[queue] += 1
        slot, gen = i % NRING[queue], i // NRING[queue]
        key = (queue, slot)
        if gen > 0 and self.known[queue].get(key, 0) < 16 * gen:
            self.q[queue].append(("wait", key, 16 * gen))
            self.known[queue][key] = 16 * gen
        self.q[queue].append(("dma", out_ap, in_ap, key))
        ev = (key, 16 * (gen + 1))
        self._mark(ev, reads, writes)
        if is_output:
            self.final[key] = max(self.final.get(key, 0), ev[1])

    def handoff(self, src, dst):
        for d in dst:
            for s_ in src:
                evs = list(s_.r.items()) + list(s_.w.items())
                for k, v in evs:
                    if d.r.get(k, 0) < v:
                        d.r[k] = v

    def emit(self):
        nc = self.nc
        for key, v in self.final.items():
            if self.known["sp"].get(key, 0) < v:
                self.q["sp"].append(("wait", key, v))
        keys = set()
        for e in ENGS:
            for it in self.q[e]:
                if it[0] == "wait":
                    keys.add(it[1])
                elif it[0] == "dma":
                    keys.add(it[3])
            keys.add(e)
        sems = {}
        for k in sorted(keys, key=str):
            nm = "s_" + (k if isinstance(k, str) else f"{k[0]}{k[1]}")
            sems[k] = self.stack.enter_context(nc.semaphore(nm))
        engobj = {"pe": "tensor", "act": "scalar", "dve": "vector", "pool": "gpsimd", "sp": "sync"}

        def run(e, eng):
            for it in self.q[e]:
                if it[0] == "wait":
                    eng.wait_ge(sems[it[1]], it[2])
                elif it[0] == "op":
                    it[1](eng).then_inc(sems[e], 1)
                else:
                    eng.dma_start(out=it[1], in_=it[2]).then_inc(sems[it[3]], 16)

        with nc.Block() as block:
            for e in ENGS:
                if not self.q[e]:
                    continue
                getattr(block, engobj[e])(lambda eng, e=e: run(e, eng))
        self.stack.close()


class Ring:
    def __init__(self, items):
        self.items = items
        self.i = 0

    def next(self):
        t = self.items[self.i % len(self.items)]
        self.i += 1
        return t


class _Stop(Exception):
    pass


def build(stop_after=None, debug=False):
    nc = bass.Bass("TRN2", target_bir_lowering=False)
    P = Prog(nc)

    def ck(name):
        if stop_after == name:
            raise _Stop()

    def din(name, shape, dt=F32):
        return nc.dram_tensor(name, list(shape), dt, kind="ExternalInput").ap()

    def dout(name, shape, dt=F32):
        return nc.dram_tensor(name, list(shape), dt, kind="ExternalOutput").ap()

    def dscr(name, shape, dt):
        return T(nc.dram_tensor(name, list(shape), dt, kind="Internal").ap(), name)

    x_all = din("x_all", [5120, D])
    rope_d = din("rope", [4096, 2, 2, 32])
    cache_k = din("cache_k", [256, 256])
    cache_v = din("cache_v", [256, 256])
    stA0 = din("stA0", [8, 128, 128])
    stB0 = din("stB0", [8, 128, 128])
    condT_d = din("condT", [128, 16, 2])
    vecT_d = din("vecT", [128, 192])
    rowb_d = din("rowb", [128, 3, 128])
    cF_d = din("cF", [128, 1412])
    w_ada = din("w_ada", [D, 6 * D])
    w_in = din("w_in", [D, NIN])
    w_ba = din("w_br_att", [1024, D])
    w_bh = din("w_br_hg", [1024, D])
    w_out = din("w_out", [D, D])
    w_fi = din("w_ffn_in", [D, 2 * DFF])
    w_fo = din("w_ffn_out", [DFF, D])

    y_own = dout("y_own", [3072, D])
    kc_out = dout("kc_out", [1024, 256])
    vc_out = dout("vc_out", [1024, 256])
    sA_out = dout("sA_out", [4, 8, 128, 128])
    sB_out = dout("sB_out", [4, 8, 128, 128])

    KT_scr = dscr("KT_scr", [2, 128, 4352], BF16)
    V_scr = dscr("V_scr", [4352, 256], BF16)
    SB_scr = dscr("SB_scr", [4, 8, 128, 128], F32)
    x1_scr = [dscr(f"x1_scr{i}", [512, D], F32) for i in range(2)]
    mo_scr = [dscr(f"mo_scr{i}", [512, D], F32) for i in range(2)]

    def dump(name, src_ap, shape, reads, dt=F32):
        if not debug:
            return
        d = nc.dram_tensor(name, list(shape), F32, kind="ExternalOutput").ap()
        P.dma("pool" if dt != F32 else "sp", d, src_ap, reads=reads, is_output=True)

    cF = P.sb([128, 1412], F32, "cF")
    identf = cF[:, 0:128]
    maskA = cF[:, 128:256]
    maskB = cF[:, 256:384]
    resetm = cF[:, 384:896]
    mask4 = cF[:, 896:900]
    ones512 = cF[:, 900:1412]
    identb = P.sb([128, 128], BF16, "identb")
    onesb = P.sb([128, 128], BF16, "onesb")
    onesf = P.sb([128, 128], F32, "onesf")
    rowb = P.sb([128, 3, 128], F32, "rowb")
    vecT = P.sb([128, 192], F32, "vecT")
    condT = P.sb([128, 32], F32, "condT")
    scT = P.sb([128, 16, 2], BF16, "scT")
    modT = P.sb([128, 96, 2], F32, "modT")
    AT1 = P.sb([128, 16, 2], F32, "AT1")
    AT2 = P.sb([128, 16, 2], F32, "AT2")
    GT1 = P.sb([128, 16, 2], F32, "GT1")
    GT2 = P.sb([128, 16, 2], F32, "GT2")
    oml = P.sb([128, 16], F32, "oml")
    noml = P.sb([128, 16], F32, "noml")
    C1b = P.sb([128, D], F32, "C1b")
    C2b = P.sb([128, D], F32, "C2b")
    dgt = Ring([P.sb([128, 128], F32, f"dgt{i}") for i in range(2)])

    slabs = Ring([P.sb([128, 16, 512], BF16, f"slab{i}") for i in range(2)])
    hT = P.sb([128, 16, 512], BF16, "hT")
    xtr = Ring([P.sb([128, D], F32, f"xt{i}") for i in range(2)])
    xsr = Ring([P.sb([128, D], BF16, f"xs{i}") for i in range(1)])
    sqj = P.sb([128, 512], BF16, "sqj")
    arena = P.sb([128, 22528], BF16, "arena")
    QT = P.view(arena, "QT")
    vtok = P.view(arena, "vtok")
    mT = P.view(arena, "mT")
    attT = P.view(arena, "attT")
    sog = P.view(arena, "sog")
    hgT = P.view(arena, "hgT")
    fT = P.view(arena, "fT")
    QTv = arena[:, 0:4096].rearrange("p (g i n) -> p g i n", g=2, i=4)
    vtokv = arena[:, 4096:8192].rearrange("p (i n) -> p i n", i=4)
    mTv = arena[:, 0:8192].rearrange("p (c n) -> p c n", c=16)
    attTv = arena[:, 8192:12288].rearrange("p (c n) -> p c n", c=8)
    sogv = arena[:, 12288:16384].rearrange("p (i n) -> p i n", i=4)
    hgTv = arena[:, 16384:20480].rearrange("p (c n) -> p c n", c=8)
    fTv = arena[:, 0:22528].rearrange("p (c n) -> p c n", c=44)

    tf = [P.sb([128, 512], F32, f"tf{i}") for i in range(6)]
    tfr = Ring(tf)
    qs = P.sb([128, 512], BF16, "qs")
    qt = [P.sb([128, 512], BF16, f"qt{i}") for i in range(2)]
    kt = [P.sb([128, 512], BF16, f"kt{i}") for i in range(2)]
    kh = [P.sb([128, 512], BF16, f"kh{i}") for i in range(2)]
    qb16 = P.sb([128, 512], BF16, "qb16")
    kTok = [P.sb([128, 4, 128], BF16, f"kTok{i}") for i in range(2)]
    ATm = [P.sb([128, 4, 128], BF16, f"ATm{i}") for i in range(2)]
    Vbd = P.sb([128, 4, 512], BF16, "Vbd")
    ebt = [P.sb([128, 16], F32, f"ebt{i}") for i in range(2)]
    ebt1 = [P.sb([128, 16], F32, f"ebtp{i}") for i in range(2)]
    Vbd1 = P.sb([128, 4, 512], BF16, "Vbd1")
    _r = lambda k: arena[:, k * 512:(k + 1) * 512]
    hqs1 = T(_r(0), "hqs1")
    hqt1 = [T(_r(1), "hqt10"), T(_r(2), "hqt11")]
    hkt1 = [T(_r(3), "hkt10"), T(_r(4), "hkt11")]
    hkh1 = [T(_r(5), "hkh10"), T(_r(6), "hkh11")]
    _r3 = lambda k: arena[:, 20480 + k * 512:20480 + (k + 1) * 512].rearrange("p (i n) -> p i n", i=4)
    hkTok1 = [T(_r3(0), "hkTok10"), T(_r3(1), "hkTok11")]
    hATm1 = [T(_r3(2), "hATm10"), T(_r3(3), "hATm11")]
    HGP_QT = [hqs1] + hqt1 + hkt1 + hkh1
    HGP_SP = hkTok1 + hATm1
    ptr = Ring([P.sb([128, 512], BF16, f"pt{i}") for i in range(6)])
    kqr = Ring([P.sb([128, 128], BF16, f"kq{i}") for i in range(8)])
    vqr = Ring([P.sb([128, 128], BF16, f"vq{i}") for i in range(10)])
    kchr = Ring([P.sb([128, 2, 128], BF16, f"kch{i}") for i in range(3)])
    vchr = Ring([P.sb([128, 256], BF16, f"vch{i}") for i in range(3)])
    KTp = P.sb([128, 2, 512], BF16, "KTp")
    Vp = P.sb([128, 4, 256], BF16, "Vp")
    SX = [[P.sb([128, 128], F32, f"S{x}{h}") for h in range(8)] for x in range(2)]
    SXb = [[P.sb([128, 128], BF16, f"Sb{x}{h}") for h in range(8)] for x in range(2)]
    ropet = P.sb([128, 4, 128], F32, "ropet")
    st = {n: P.sb([128, 16], F32, "st_" + n) for n in ("a", "b", "c", "d", "ssq", "e", "f", "g")}
    kvf = P.sb([128, 512], F32, "kvf")

    mm = [P.ps([128, 512], F32, f"mm{i}") for i in range(4)]
    mmr = Ring(mm)
    acc = [P.ps([128, 512], F32, f"acc{i}") for i in range(2)]
    tb = [P.ps([128, 1024], BF16, f"tb{i}") for i in range(2)]
    tbr = Ring(tb)
    HB = [dict(qs=qs, qt=qt, kt=kt, kh=kh, kTok=kTok, ATm=ATm, Vbd=Vbd, ebt=ebt, obank=acc[0]),
          dict(qs=hqs1, qt=hqt1, kt=hkt1, kh=hkh1, kTok=hkTok1, ATm=hATm1, Vbd=Vbd1, ebt=ebt1, obank=acc[1])]

    def act(out, in_, func, reads, writes, pw=(), **kw):
        P.op("act", lambda e: e.activation(out=out, in_=in_, func=func, **kw), reads, writes, pw)

    def tt(out, in0, in1, op, reads, writes, eng="dve", pw=()):
        P.op(eng, lambda e: e.tensor_tensor(out=out, in0=in0, in1=in1, op=op), reads, writes, pw)

    def ts(out, in0, s1, s2, op0, op1, reads, writes, eng="dve", pw=()):
        if s2 is None:
            P.op(eng, lambda e: e.tensor_scalar(out=out, in0=in0, scalar1=s1, scalar2=None, op0=op0), reads, writes, pw)
        else:
            P.op(eng, lambda e: e.tensor_scalar(out=out, in0=in0, scalar1=s1, scalar2=s2, op0=op0, op1=op1), reads, writes, pw)

    def stt(out, in0, sc, in1, op0, op1, reads, writes):
        P.op("dve", lambda e: e.scalar_tensor_tensor(out=out, in0=in0, scalar=sc, in1=in1, op0=op0, op1=op1), reads, writes)

    def rstd_of(ss_ap, n, out_ap, tmp_ap, tiles_r, tiles_w):
        act(tmp_ap, ss_ap, AF.Ln, tiles_r, tiles_w[:1], scale=1.0 / n, bias=EPS)
        act(out_ap, tmp_ap, AF.Exp, tiles_w[:1], tiles_w[1:], scale=-0.5)

    def sigmoid_inplace_from(dst_t, dst_ap, src_ap, src_tiles):
        act(dst_ap, src_ap, AF.Exp, src_tiles, [dst_t], scale=-1.0)
        act(dst_ap, dst_ap, AF.Ln, [dst_t], [dst_t], bias=1.0)
        act(dst_ap, dst_ap, AF.Exp, [dst_t], [dst_t], scale=-1.0)

    wcache = {}

    def wload_multi(parts, kcn, width, cache=True):
        key = tuple((p[0].name, p[1], p[2], p[3], p[4]) for p in parts) + (kcn,)
        slab = slabs.next()
        if cache and key in wcache:
            scr = wcache[key]
            P.dma("pool", slab[:, 0:kcn, 0:width], scr[:, :].rearrange("p (k n) -> p k n", k=kcn), reads=[scr], writes=[slab])
            return slab
        for (Wd, r0, c0, ncols, dst_c0) in parts:
            src = Wd[r0:r0 + kcn * 128, c0:c0 + ncols].rearrange("(k p) n -> p k n", p=128)
            P.dma("pool", slab[:, 0:kcn, dst_c0:dst_c0 + ncols], src, writes=[slab])
        if cache:
            scr = dscr(f"wc{len(wcache)}", [128, kcn * width], BF16)
            P.dma("sp", scr[:, :].rearrange("p (k n) -> p k n", k=kcn), slab[:, 0:kcn, 0:width], reads=[slab], writes=[scr])
            wcache[key] = scr
        return slab

    def wload(Wd, r0, kcn, c0, ncols, cache=True):
        return wload_multi([(Wd, r0, c0, ncols, 0)], kcn, ncols, cache)

    def gemm_a(slab, kcn, lhs, lhs_tiles, bank, ncols=512, k0=0, first=True, last=True):
        def f(e):
            for kc in range(kcn):
                ins = e.matmul(bank[:, 0:ncols], lhsT=lhs(k0 + kc), rhs=slab[:, kc, 0:ncols],
                               start=(first and kc == 0), stop=(last and kc == kcn - 1))
            return ins
        P.op("pe", f, [slab] + lhs_tiles, [bank])

    def gemm_b(slab, kcn, c0, rhs, rhs_tiles, bank):
        def f(e):
            for kc in range(kcn):
                ins = e.matmul(bank[:, 0:512], lhsT=slab[:, kc, c0:c0 + 128], rhs=rhs(kc),
                               start=(kc == 0), stop=(kc == kcn - 1))
            return ins
        P.op("pe", f, [slab] + rhs_tiles, [bank])

    def transposes(src_t, src_aps, n):
        bank = tbr.next()

        def f(e):
            for k in range(n):
                ins = e.transpose(bank[:, k * 128:(k + 1) * 128], src_aps(k), identb[:])
            return ins
        P.op("pe", f, [src_t, identb], [bank])
        return bank

    P.dma("sp", cF[:], cF_d[:, :], writes=[cF])
    P.dma("sp", rowb[:], rowb_d[:, :, :], writes=[rowb])
    P.dma("sp", vecT[:], vecT_d[:, :], writes=[vecT])
    P.dma("sp", condT[:], condT_d.rearrange("p k g -> p (k g)"), writes=[condT])
    P.op("dve", lambda e: e.tensor_copy(out=identb[:], in_=identf), [cF], [identb])
    P.op("dve", lambda e: e.memset(onesb[:], 1.0), [], [onesb])
    P.op("dve", lambda e: e.memset(onesf[:], 1.0), [], [onesf])
    act(rowb[:, 0, :], rowb[:, 0, :], AF.Copy, [rowb], [rowb], scale=float(128 ** -0.5))
    for h in range(8):
        P.dma("sp", SX[1][h][:], stB0[h, :, :], writes=[SX[1][h]])

    tt(oml[:, 0:8], vecT[:, 160:168], vecT[:, 168:176], ALU.subtract, [vecT], [oml])
    tt(oml[:, 8:16], vecT[:, 176:184], vecT[:, 184:192], ALU.subtract, [vecT], [oml])
    act(oml[:], oml[:], AF.Exp, [oml], [oml])
    act(oml[:], oml[:], AF.Ln, [oml], [oml], bias=1.0)
    act(oml[:], oml[:], AF.Exp, [oml], [oml], scale=-1.0)
    ts(noml[:], oml[:], -1.0, None, ALU.mult, None, [oml], [noml])

    t0 = tf[0]
    act(t0[:, 0:32], condT[:], AF.Exp, [condT], [t0], scale=-1.0)
    ts(t0[:, 0:32], t0[:, 0:32], 1.0, None, ALU.add, None, [t0], [t0])
    P.op("dve", lambda e: e.reciprocal(out=t0[:, 0:32], in_=t0[:, 0:32]), [t0], [t0])
    tt(scT[:].rearrange("p k g -> p (k g)"), condT[:], t0[:, 0:32], ALU.mult, [condT, t0], [scT])
    pada = acc[0]

    def ada_slabs(cbs):
        for cb in cbs:
            cur = wload(w_ada, 0, 16, cb * 512, 512, cache=False)

            def f(e, cb=cb, cur=cur):
                for m in range(4):
                    c = cb * 4 + m
                    for kc in range(16):
                        ins = e.matmul(pada[:, 2 * c:2 * c + 2], lhsT=cur[:, kc, m * 128:(m + 1) * 128], rhs=scT[:, kc, :],
                                       start=(kc == 0), stop=(kc == 15))
                return ins
            P.op("pe", f, [cur, scT], [pada])

    def ada_finish(c0, c1):
        tt(modT[:, c0:c1, :], pada[:, 2 * c0:2 * c1].rearrange("p (c g) -> p c g", g=2),
           vecT[:, c0:c1].unsqueeze(2).broadcast_to([128, c1 - c0, 2]), ALU.add, [pada, vecT], [modT])

    ada_slabs(range(8))
    ada_finish(0, 32)
    for g in range(2):
        stt(AT1[:, :, g], modT[:, 16:32, g], 1.0, vecT[:, 96:112], ALU.add, ALU.mult, [modT, vecT], [AT1])

    def ada_rest():
        ada_finish(32, 96)
        for g in range(2):
            stt(AT2[:, :, g], modT[:, 64:80, g], 1.0, vecT[:, 128:144], ALU.add, ALU.mult, [modT, vecT], [AT2])
            tt(GT1[:, :, g], modT[:, 32:48, g], vecT[:, 112:128], ALU.mult, [modT, vecT], [GT1])
            tt(GT2[:, :, g], modT[:, 80:96, g], vecT[:, 144:160], ALU.mult, [modT, vecT], [GT2])
    BT1 = lambda kc, g: modT[:, kc, g:g + 1]
    BT2 = lambda kc, g: modT[:, 48 + kc, g:g + 1]

    def build_Cb(g):
        for GT, Cb in ((GT1, C1b), (GT2, C2b)):
            for q4 in range(4):
                bank = mmr.next()
                for k in range(4):
                    kc = q4 * 4 + k
                    d_ = dgt.next()
                    ts(d_[:], identf, GT[:, kc, g:g + 1], None, ALU.mult, None, [cF, GT], [d_])
                    P.op("pe", lambda e, d_=d_, bank=bank, k=k: e.matmul(bank[:, k * 128:(k + 1) * 128], lhsT=onesf[:], rhs=d_[:],
                                                                       start=True, stop=True), [onesf, d_], [bank])
                act(Cb[:, q4 * 512:(q4 + 1) * 512], bank[:, 0:512], AF.Copy, [bank], [Cb])

    def norm_tile(xt, i, g, AT, BT):
        norm_B(norm_A(xt), i, g, AT, BT)

    def norm_A(xt):
        xs = xsr.next()
        act(xs[:], xt[:], AF.Square, [xt], [xs, st["a"]], accum_out=st["a"][:, 0:1])
        rstd_of(st["a"][:, 0:1], D, st["a"][:, 2:3], st["a"][:, 1:2], [st["a"]], [st["a"], st["a"]])
        ts(xs[:], xt[:], st["a"][:, 2:3], None, ALU.mult, None, [xt, st["a"]], [xs])
        return xs

    def norm_B(xs, i, g, AT, BT):
        for half in range(2):
            bank = transposes(xs, lambda k, half=half: xs[:, (half * 8 + k) * 128:(half * 8 + k + 1) * 128], 8)
            for k in range(8):
                kc = half * 8 + k
                o = hT[:, kc, i * 128:(i + 1) * 128]
                src = bank[:, k * 128:(k + 1) * 128]
                if half == 0:
                    act(o, src, AF.Identity, [bank, AT, modT], [], pw=[hT], scale=AT[:, kc, g:g + 1], bias=BT(kc, g))
                else:
                    ts(o, src, AT[:, kc, g:g + 1], BT(kc, g), ALU.mult, ALU.add, [bank, AT, modT], [], pw=[hT])

    def norm_one_A(row0, i):
        xt = xtr.next()
        P.dma("sp", xt[:], x_all[row0 + i * 128:row0 + (i + 1) * 128, :], writes=[xt])
        return norm_A(xt)

    def load_norm_block(row0, g):
        xts = []
        xt = xtr.next()
        P.dma("sp", xt[:], x_all[row0:row0 + 128, :], writes=[xt])
        for i in range(4):
            nxt = None
            if i < 3:
                nxt = xtr.next()
                P.dma("sp", nxt[:], x_all[row0 + (i + 1) * 128:row0 + (i + 2) * 128, :], writes=[nxt])
            norm_tile(xt, i, g, AT1, BT1)
            xt = nxt

    def headnorm(bank, c0, nh, nrow, out_ap, out_t):
        w = nh * 128
        t_sq = tfr.next()
        act(t_sq[:, 0:w], bank[:, c0:c0 + w], AF.Square, [bank], [t_sq])
        P.op("dve", lambda e: e.tensor_reduce(out=st["b"][:, 0:nh], in_=t_sq[:, 0:w].rearrange("p (h d) -> p h d", h=nh),
                                             axis=AX.X, op=ALU.add), [t_sq], [st["b"]])
        rstd_of(st["b"][:, 0:nh], 128, st["b"][:, 8:8 + nh], st["b"][:, 4:4 + nh], [st["b"]], [st["b"], st["b"]])
        tt(t_sq[:, 0:w].rearrange("p (h d) -> p h d", h=nh), bank[:, c0:c0 + w].rearrange("p (h d) -> p h d", h=nh),
           st["b"][:, 8:8 + nh].unsqueeze(2).broadcast_to([128, nh, 128]), ALU.mult, [bank, st["b"]], [t_sq])
        tt(out_ap.rearrange("p (h d) -> p h d", h=nh), t_sq[:, 0:w].rearrange("p (h d) -> p h d", h=nh),
           rowb[:, nrow, :].unsqueeze(1).broadcast_to([128, nh, 128]), ALU.mult, [t_sq, rowb], [out_t])

    def rope(src_t, src_ap, nh, i, out_t, out_ap):
        v = src_ap.rearrange("p (h a f d) -> p h a f d", h=nh, a=2, f=2)
        o = out_ap.rearrange("p (h a f d) -> p h a f d", h=nh, a=2, f=2)
        tbl = ropet[:, i, :].rearrange("p (c a d) -> p c a d", c=2, a=2)
        cos = tbl[:, 0, :, :].unsqueeze(1).broadcast_to([128, nh, 2, 32])
        sin = tbl[:, 1, :, :].unsqueeze(1).broadcast_to([128, nh, 2, 32])
        a1, a2 = v[:, :, :, 0, :], v[:, :, :, 1, :]
        t1, t2 = tfr.next(), tfr.next()
        w = nh * 64
        t1v = t1[:, 0:w].rearrange("p (h a d) -> p h a d", h=nh, a=2)
        t2v = t2[:, 0:w].rearrange("p (h a d) -> p h a d", h=nh, a=2)
        tt(t1v, a1, cos, ALU.mult, [src_t, ropet], [t1])
        tt(t2v, a2, sin, ALU.mult, [src_t, ropet], [t2])
        tt(o[:, :, :, 0, :], t1v, t2v, ALU.subtract, [t1, t2], [out_t])
        t3, t4 = tfr.next(), tfr.next()
        t3v = t3[:, 0:w].rearrange("p (h a d) -> p h a d", h=nh, a=2)
        t4v = t4[:, 0:w].rearrange("p (h a d) -> p h a d", h=nh, a=2)
        tt(t3v, a2, cos, ALU.mult, [src_t, ropet], [t3])
        tt(t4v, a1, sin, ALU.mult, [src_t, ropet], [t4])
        tt(o[:, :, :, 1, :], t3v, t4v, ALU.add, [t3, t4], [out_t])

    def q_stage(sample):
        P.handoff(HGP_QT, [QT])
        for s in range(2):
            slab = wload(w_in, 0, 16, C_QA + s * 512, 512)
            banks = {}

            def issue(i):
                banks[i] = mmr.next()
                gemm_a(slab, 16, lambda kc, i=i: hT[:, kc, i * 128:(i + 1) * 128], [hT], banks[i])
            issue(0)
            for i in range(4):
                if i + 1 < 4:
                    issue(i + 1)
                bank = banks[i]
                if sample:
                    qn = tfr.next()
                    headnorm(bank, 0, 4, 0, qn[:], qn)
                    rope(qn, qn[:], 4, i, qb16, qb16[:])
                else:
                    headnorm(bank, 0, 4, 0, qb16[:], qb16)
                tbk = transposes(qb16, lambda k: qb16[:, k * 128:(k + 1) * 128], 4)
                act(QTv[:, s, i, :], tbk[:, 0:512], AF.Copy, [tbk], [QT])

    def kv_stage(sample, blk_local_tok0, out_row0):
        slab = wload(w_in, 0, 16, C_KA, 512)
        banks = {}

        def issue(i):
            banks[i] = mmr.next()
            gemm_a(slab, 16, lambda kc, i=i: hT[:, kc, i * 128:(i + 1) * 128], [hT], banks[i])
        issue(0)
        for i in range(4):
            if i + 1 < 4:
                issue(i + 1)
            bank = banks[i]
            kb = ptr.next()
            if sample:
                kn = tfr.next()
                headnorm(bank, 0, 2, 1, kn[:, 0:256], kn)
                rope(kn, kn[:, 0:256], 2, i, kb, kb[:, 0:256])
            else:
                headnorm(bank, 0, 2, 1, kvf[:, 0:256], kvf)
                P.op("dve", lambda e, kb=kb: e.tensor_copy(out=kb[:, 0:256], in_=kvf[:, 0:256]), [kvf], [kb])
            act(kvf[:, 256:512], bank[:, 256:512], AF.Copy, [bank], [kvf])
            if not sample:
                P.dma("sp", kc_out[out_row0 + i * 128:out_row0 + (i + 1) * 128, :], kvf[:, 0:256], reads=[kvf], is_output=True)
                P.dma("sp", vc_out[out_row0 + i * 128:out_row0 + (i + 1) * 128, :], kvf[:, 256:512], reads=[kvf], is_output=True)
            tbk = transposes(kb, lambda k, kb=kb: kb[:, k * 128:(k + 1) * 128], 2)
            if sample:
                kch = kchr.next()
                act(kch[:].rearrange("p g n -> p (g n)"), tbk[:, 0:256], AF.Copy, [tbk], [kch])
                t0_ = blk_local_tok0 + i * 128
                P.dma("sp", KT_scr[:, :, t0_:t0_ + 128].rearrange("g d n -> d g n"), kch[:], reads=[kch], writes=[KT_scr])
                vch = vchr.next()
                P.op("dve", lambda e, vch=vch: e.tensor_copy(out=vch[:], in_=kvf[:, 256:512]), [kvf], [vch])
                P.dma("sp", V_scr[t0_:t0_ + 128, :], vch[:], reads=[vch], writes=[V_scr])
            else:
                act(KTp[:, :, i * 128:(i + 1) * 128], tbk[:, 0:256].rearrange("p (g n) -> p g n", g=2), AF.Copy, [tbk], [KTp])
                P.op("dve", lambda e, i=i: e.tensor_copy(out=Vp[:, i, :], in_=kvf[:, 256:512]), [kvf], [Vp])

    def attention(qtiles, nchunks, chunk_src):
        LOOK, PF = 3, 4
        Sbank = Ring(mm)
        ob, db = acc[0], acc[1]
        for i in qtiles:
            for g in range(2):
                srcs = {}

                def get(j):
                    if j not in srcs and j < nchunks:
                        srcs[j] = chunk_src(j, g)
                    return srcs.get(j)
                pend = []

                def flush_one():
                    j, pt, vv, tiles = pend.pop(0)

                    def f(e, pt=pt, vv=vv, j=j):
                        e.matmul(ob[:, 0:512], lhsT=vv, rhs=pt[:], start=(j == 0), stop=(j == nchunks - 1))
                        return e.matmul(db[:, 0:512], lhsT=onesb[:], rhs=pt[:], start=(j == 0), stop=(j == nchunks - 1))
                    P.op("pe", f, tiles + [pt, onesb], [ob, db])
                for j in range(nchunks):
                    for jj in range(j, min(j + PF, nchunks)):
                        get(jj)
                    kT, vv, tiles = get(j)
                    sb_ = Sbank.next()
                    P.op("pe", lambda e, sb_=sb_, kT=kT, g=g, i=i: e.matmul(sb_[:, 0:512], lhsT=kT, rhs=QTv[:, g, i, :], start=True, stop=True),
                         tiles + [QT], [sb_])
                    pt = ptr.next()
                    act(pt[:], sb_[:, 0:512], AF.Exp, [sb_], [pt])
                    pend.append((j, pt, vv, tiles))
                    if len(pend) > LOOK:
                        flush_one()
                while pend:
                    flush_one()
                rD = tfr.next()
                act(rD[:], db[:, 0:512], AF.Ln, [db], [rD])
                act(rD[:], rD[:], AF.Exp, [rD], [rD], scale=-1.0)
                tt(attTv[:, 4 * g:4 * g + 4, i * 128:(i + 1) * 128], ob[:, 0:512].rearrange("p (h n) -> p h n", h=4),
                   rD[:].rearrange("p (h n) -> p h n", h=4), ALU.mult, [ob, rD], [attT])

    def hg_gates(zbank, x, h, need_q, B):
        c = x * 8 + h
        e0, s1, lf, bc, t4, t5 = tf
        ebt_, qt_, kt_, kh_, kTok_, qs_ = B["ebt"], B["qt"], B["kt"], B["kh"], B["kTok"], B["qs"]
        act(e0[:], zbank[:, 0:512], AF.Exp, [zbank], [e0]); yield
        act(e0[:], e0[:], AF.Ln, [e0], [e0], bias=1.0); yield
        act(s1[:], e0[:], AF.Exp, [e0], [s1], scale=-1.0); yield
        act(lf[:], s1[:], AF.Ln, [s1, noml], [lf], scale=noml[:, c:c + 1], bias=1.0); yield
        P.op("dve", lambda e: e.tensor_tensor_scan(out=bc[:], data0=resetm, data1=lf[:], initial=0.0, op0=ALU.mult, op1=ALU.add),
             [cF, lf], [bc]); yield
        act(ebt_[x][:], bc[:, 31::32], AF.Exp, [bc], [ebt_[x]]); yield
        btot = bc[:, 31::32].unsqueeze(2).broadcast_to([128, 16, 32])
        v3 = lambda t: t[:].rearrange("p (c s) -> p c s", s=32)
        if x == 0:
            cum = bc
            tt(v3(t4), btot, v3(bc), ALU.subtract, [bc], [t4]); yield
            rem = t4
        else:
            tt(t4[:], bc[:], lf[:], ALU.subtract, [bc, lf], [t4]); yield
            rem = t4
            tt(v3(e0), btot, v3(t4), ALU.subtract, [bc, t4], [e0]); yield
            cum = e0
        if need_q:
            act(t5[:], cum[:], AF.Exp, [cum], [t5]); yield
            tt(qt_[x][:], qs_[:], t5[:], ALU.mult, [qs_, t5], [qt_[x]]); yield
            act(t5[:], cum[:], AF.Exp, [cum], [t5], scale=-1.0); yield
            stt(kt_[x][:], s1[:], oml[:, c:c + 1], t5[:], ALU.mult, ALU.mult, [s1, oml, t5], [kt_[x]]); yield
        act(t5[:], rem[:], AF.Exp, [rem], [t5]); yield
        stt(kh_[x][:], s1[:], oml[:, c:c + 1], t5[:], ALU.mult, ALU.mult, [s1, oml, t5], [kh_[x]]); yield
        tbk = transposes(kh_[x], lambda k: kh_[x][:, k * 128:(k + 1) * 128], 4); yield
        act(kTok_[x][:].rearrange("p i n -> p (i n)"), tbk[:, 0:512], AF.Copy, [tbk], [kTok_[x]]); yield

    def make_vbd(h, B):
        Vb = B["Vbd"]
        for i in range(4):
            tt(Vb[:, i, :].rearrange("p (c n) -> p c n", c=4),
               vtokv[:, i, h * 128:(h + 1) * 128].unsqueeze(1).broadcast_to([128, 4, 128]),
               mask4.unsqueeze(2).broadcast_to([128, 4, 128]), ALU.mult, [vtok, cF], [Vb])
            yield

    def hg_G(h, B, main, slab, c0):
        rhs = lambda kc: hT[:, kc, :]
        gbank = Ring([mm[0], mm[1]])
        if main:
            bq = gbank.next()
            gemm_b(slab, 16, c0, rhs, [hT], bq); yield
            t_ = tf[5]
            act(t_[:], bq[:, 0:512], AF.Exp, [bq], [t_], scale=-1.0); yield
            act(t_[:], t_[:], AF.Ln, [t_], [t_], bias=1.0); yield
            act(t_[:], t_[:], AF.Exp, [t_], [t_], scale=-1.0); yield
            tt(B["qs"][:], bq[:, 0:512], t_[:], ALU.mult, [bq, t_], [B["qs"]]); yield
            for x in range(2):
                bz = gbank.next()
                gemm_b(slab, 16, c0 + 128 + 128 * x, rhs, [hT], bz); yield
                yield from hg_gates(bz, x, h, True, B)
        else:
            bz = gbank.next()
            gemm_b(slab, 16, c0, rhs, [hT], bz); yield
            yield from hg_gates(bz, 1, h, False, B)
        yield from make_vbd(h, B)
        if main:
            obank = B["obank"]
            for x in range(2):
                bank = gbank.next()

                def f(e, x=x, bank=bank):
                    for i in range(4):
                        ins = e.matmul(bank[:, i * 128:(i + 1) * 128], lhsT=B["kt"][x][:, i * 128:(i + 1) * 128],
                                       rhs=B["qt"][x][:, i * 128:(i + 1) * 128], start=True, stop=True)
                    return ins
                P.op("pe", f, [B["kt"][x], B["qt"][x]], [bank]); yield
                msk = (maskA if x == 0 else maskB).unsqueeze(1).broadcast_to([128, 4, 128])
                tt(B["ATm"][x][:], bank[:, 0:512].rearrange("p (i n) -> p i n", n=128), msk, ALU.mult, [bank, cF], [B["ATm"][x]]); yield

            def f2(e):
                for i in range(4):
                    e.matmul(obank[:, i * 128:(i + 1) * 128], lhsT=B["ATm"][0][:, i, :], rhs=vtokv[:, i, h * 128:(h + 1) * 128],
                             start=(i == 0), stop=False, skip_group_check=True)
                    ins = e.matmul(obank[:, i * 128:(i + 1) * 128], lhsT=B["ATm"][1][:, i, :], rhs=vtokv[:, i, h * 128:(h + 1) * 128],
                                   start=False, stop=False, skip_group_check=True)
                return ins
            P.op("pe", f2, [B["ATm"][0], B["ATm"][1], vtok], [obank]); yield

    def hg_C(h, B, dirs, segs, with_o, pre_blk=None, sample_blk=None, seq0=None):
        obank = B["obank"]
        if pre_blk is not None and pre_blk <= 3:
            P.dma("sp", SB_scr[pre_blk, h, :, :], SX[1][h][:], reads=[SX[1][h]], writes=[SB_scr])
        if sample_blk is not None:
            P.dma("sp", SX[1][h][:], SB_scr[sample_blk, h, :, :], reads=[SB_scr], writes=[SX[1][h]])
            act(SXb[1][h][:], SX[1][h][:], AF.Copy, [SX[1][h]], [SXb[1][h]])
        pbank = {0: mm[2], 1: mm[3]}
        pring = Ring([mm[2], mm[3]])
        for si, seg in enumerate(segs):
            if seq0 is not None:
                for x in dirs:
                    P.op("dve", lambda e, x=x: e.memset(SX[x][h][:], 0.0), [], [SX[x][h]])
                    P.op("dve", lambda e, x=x: e.memset(SXb[x][h][:], 0.0), [], [SXb[x][h]])
            order = {0: [(i, c) for i in seg for c in range(4)], 1: [(i, c) for i in reversed(seg) for c in reversed(range(4))]}
            pb = {}
            nsteps = len(seg) * 4
            for k in range(nsteps):
                for x in dirs:
                    i, c = order[x][k]
                    if k % 4 == 0:
                        bank = pbank[x] if len(dirs) == 2 else pring.next()
                        P.op("pe", lambda e, bank=bank, x=x, i=i: e.matmul(bank[:, 0:512], lhsT=B["kTok"][x][:, i, :], rhs=B["Vbd"][:, i, :],
                                                                          start=True, stop=True), [B["kTok"][x], B["Vbd"]], [bank])
                        pb[x] = bank
                    S, Sb = SX[x][h], SXb[x][h]
                    if with_o:
                        P.op("pe", lambda e, x=x, i=i, c=c, Sb=Sb: e.matmul(obank[32 * c:32 * c + 32, i * 128:(i + 1) * 128],
                                                                          lhsT=B["qt"][x][:, i * 128 + 32 * c:i * 128 + 32 * c + 32], rhs=Sb[:],
                                                                          start=False, stop=True, tile_position=(0, 32 * c), skip_group_check=True),
                             [B["qt"][x], Sb], [obank])
                    bank = pb[x]
                    stt(S[:], S[:], B["ebt"][x][:, i * 4 + c:i * 4 + c + 1], bank[:, c * 128:(c + 1) * 128], ALU.mult, ALU.add,
                        [S, B["ebt"][x], bank], [S])
                    if with_o:
                        act(Sb[:], S[:], AF.Copy, [S], [Sb])
                yield
            if seq0 is not None:
                P.dma("sp", sA_out[seq0 + si, h, :, :], SX[0][h][:], reads=[SX[0][h]], is_output=True)
                P.dma("sp", sB_out[seq0 + si, h, :, :], SX[1][h][:], reads=[SX[1][h]], is_output=True)

    def hg_E(h, B):
        obank = B["obank"]
        t_sq = kvf
        act(t_sq[:], obank[:, 0:512], AF.Square, [obank], [t_sq]); yield
        P.op("dve", lambda e: e.tensor_reduce(out=st["c"][:, 0:4], in_=t_sq[:].rearrange("p (h d) -> p h d", h=4), axis=AX.X, op=ALU.add),
             [t_sq], [st["c"]]); yield
        rstd_of(st["c"][:, 0:4], 128, st["c"][:, 8:12], st["c"][:, 4:8], [st["c"]], [st["c"], st["c"]]); yield
        v4 = lambda ap: ap.rearrange("p (h d) -> p h d", h=4)
        tt(v4(t_sq[:]), v4(obank[:, 0:512]), st["c"][:, 8:12].unsqueeze(2).broadcast_to([128, 4, 128]), ALU.mult, [obank, st["c"]], [t_sq]); yield
        tt(v4(t_sq[:]), v4(t_sq[:]), rowb[:, 2, :].unsqueeze(1).broadcast_to([128, 4, 128]), ALU.mult, [t_sq, rowb], [t_sq]); yield
        tt(v4(qb16[:]), v4(t_sq[:]), sogv[:, :, h * 128:(h + 1) * 128], ALU.mult, [t_sq, sog], [qb16]); yield
        tbk = transposes(qb16, lambda k: qb16[:, k * 128:(k + 1) * 128], 4); yield
        act(hgTv[:, h, :], tbk[:, 0:512], AF.Copy, [tbk], [hgT]); yield

    def drain(g):
        if g is not None:
            for _ in g:
                pass

    def pipeline_heads(makeG, makeC, doE, ratio):
        g = makeG(0)
        drain(g)
        pe_ = None
        for h in range(8):
            c = makeC(h)
            gn = makeG(h + 1) if h < 7 else None
            for _ in c:
                if pe_ is not None and next(pe_, "end") == "end":
                    pe_ = None
                if gn is not None:
                    for _k in range(ratio):
                        if next(gn, "end") == "end":
                            gn = None
                            break
            drain(pe_)
            drain(gn)
            pe_ = doE(h) if doE is not None else None
        drain(pe_)

    def hgrn_main(segs, sample, blk, seq0):
        P.handoff([QT], HGP_QT)
        for s in range(2):
            slab = wload(w_in, 0, 16, C_VH + s * 512, 512)
            for i in range(4):
                bank = mmr.next()
                gemm_a(slab, 16, lambda kc, i=i: hT[:, kc, i * 128:(i + 1) * 128], [hT], bank)
                act(vtokv[:, i, s * 512:(s + 1) * 512], bank[:, 0:512], AF.Copy, [bank], [vtok])
        for s in range(2):
            slab = wload(w_in, 0, 16, C_OG + s * 512, 512)
            for i in range(4):
                bank = mmr.next()
                gemm_a(slab, 16, lambda kc, i=i: hT[:, kc, i * 128:(i + 1) * 128], [hT], bank)
                t_ = tfr.next()
                sigmoid_inplace_from(t_, t_[:], bank[:, 0:512], [bank])
                tt(sogv[:, i, s * 512:(s + 1) * 512], bank[:, 0:512], t_[:], ALU.mult, [bank, t_], [sog])

        def makeG(h):
            slab = wload_multi([(w_in, 0, C_QH + h * 128, 128, 0), (w_in, 0, C_ZA + h * 128, 128, 128),
                                (w_in, 0, C_ZB + h * 128, 128, 256)], 16, 384)
            return hg_G(h, HB[h % 2], True, slab, 0)

        def makeC(h):
            return hg_C(h, HB[h % 2], [0, 1], segs, True, sample_blk=(blk if sample else None), seq0=(None if sample else seq0))
        pipeline_heads(makeG, makeC, lambda h: hg_E(h, HB[h % 2]), 4)

    def hgrn_pre(blk):
        for s_ in range(2):
            slab = wload(w_in, 0, 16, C_VH + s_ * 512, 512)
            for i in range(4):
                bank = mmr.next()
                gemm_a(slab, 16, lambda kc, i=i: hT[:, kc, i * 128:(i + 1) * 128], [hT], bank)
                act(vtokv[:, i, s_ * 512:(s_ + 1) * 512], bank[:, 0:512], AF.Copy, [bank], [vtok])
        gbank = Ring([mm[0], mm[1]])
        zslab, zb = {}, {}

        def issue(h):
            hs = h // 4
            if hs not in zslab:
                zslab[hs] = wload(w_in, 0, 16, C_ZB + hs * 512, 512)
            zb[h] = gbank.next()
            gemm_b(zslab[hs], 16, (h % 4) * 128, lambda kc: hT[:, kc, :], [hT], zb[h])
        def head(h):
            B = HB[h % 2]
            c = 8 + h
            if blk <= 3:
                P.dma("sp", SB_scr[blk, h, :, :], SX[1][h][:], reads=[SX[1][h]], writes=[SB_scr])
            if h % 2 == 1:
                e0, s1, lf = tf[3], tf[4], tf[5]
                bc = tf[3]
            else:
                e0, s1, lf = tf[0], tf[1], tf[2]
                bc = tf[0]
            zbank = zb[h]
            act(e0[:], zbank[:, 0:512], AF.Exp, [zbank], [e0])
            if h + 1 < 8:
                issue(h + 1)
            yield
            act(e0[:], e0[:], AF.Ln, [e0], [e0], bias=1.0); yield
            act(s1[:], e0[:], AF.Exp, [e0], [s1], scale=-1.0); yield
            act(lf[:], s1[:], AF.Ln, [s1, noml], [lf], scale=noml[:, c:c + 1], bias=1.0); yield
            P.op("dve", lambda e, bc=bc, lf=lf: e.tensor_tensor_scan(out=bc[:], data0=ones512, data1=lf[:], initial=0.0, op0=ALU.mult, op1=ALU.add),
                 [cF, lf], [bc]); yield
            act(B["ebt"][1][:, 0:1], bc[:, 511:512], AF.Exp, [bc], [B["ebt"][1]]); yield
            tt(lf[:], bc[:], lf[:], ALU.subtract, [bc, lf], [lf]); yield
            act(lf[:], lf[:], AF.Exp, [lf], [lf]); yield
            kh_ = B["kh"][1]
            stt(kh_[:], s1[:], oml[:, c:c + 1], lf[:], ALU.mult, ALU.mult, [s1, oml, lf], [kh_]); yield
            tbk = transposes(kh_, lambda k, kh_=kh_: kh_[:, k * 128:(k + 1) * 128], 4); yield
            kT_ = B["kTok"][1]
            act(kT_[:].rearrange("p i n -> p (i n)"), tbk[:, 0:512], AF.Copy, [tbk], [kT_]); yield
            pbk = mm[2 + h % 2]

            def f(e, kT_=kT_, pbk=pbk, h=h):
                for i in range(4):
                    ins = e.matmul(pbk[:, 0:128], lhsT=kT_[:, i, :], rhs=vtokv[:, i, h * 128:(h + 1) * 128], start=(i == 0), stop=(i == 3))
                return ins
            P.op("pe", f, [kT_, vtok], [pbk]); yield
            S = SX[1][h]
            stt(S[:], S[:], B["ebt"][1][:, 0:1], pbk[:, 0:128], ALU.mult, ALU.add, [S, B["ebt"][1], pbk], [S]); yield

        issue(0)
        SKEW = 6
        gens = [head(h) for h in range(8)]
        cur, nxt_, started = gens[0], None, 1
        steps = 0
        while cur is not None:
            alive = next(cur, "end") != "end"
            steps += 1
            if nxt_ is None and started < 8 and steps >= SKEW:
                nxt_ = gens[started]
                started += 1
            if nxt_ is not None and next(nxt_, "end") == "end":
                nxt_ = None
            if not alive:
                cur, nxt_ = nxt_, None
                steps = SKEW
                if cur is None and started < 8:
                    cur = gens[started]
                    started += 1
                    steps = 0

    def merge_block():
        P.handoff([QT, vtok] + HGP_QT, [mT])
        for cb in range(4):
            Wga = wload(w_in, 0, 16, C_GA + cb * 512, 512)
            Wa = wload(w_ba, 0, 8, cb * 512, 512)
            for m in range(4):
                b3 = mmr.next()
                gemm_b(Wga, 16, m * 128, lambda kc: hT[:, kc, :], [hT], b3)
                sigmoid_inplace_from(tf[m], tf[m][:], b3[:, 0:512], [b3])
                b1 = mmr.next()
                gemm_b(Wa, 8, m * 128, lambda kc: attTv[:, kc, :], [attT], b1)
                tt(tf[m][:], b1[:, 0:512], tf[m][:], ALU.mult, [b1, tf[m]], [tf[m]])
            Wgb = wload(w_in, 0, 16, C_GB + cb * 512, 512)
            Wh = wload(w_bh, 0, 8, cb * 512, 512)
            for m in range(4):
                b4 = mmr.next()
                gemm_b(Wgb, 16, m * 128, lambda kc: hT[:, kc, :], [hT], b4)
                sigmoid_inplace_from(tf[4], tf[4][:], b4[:, 0:512], [b4])
                b2 = mmr.next()
                gemm_b(Wh, 8, m * 128, lambda kc: hgTv[:, kc, :], [hgT], b2)
                tt(tf[5][:], b2[:, 0:512], tf[4][:], ALU.mult, [b2, tf[4]], [tf[5]])
                tt(mTv[:, cb * 4 + m, :], tf[5][:], tf[m][:], ALU.add, [tf[5], tf[m]], [], pw=[mT])

    def proj_out_stage(Wd, kparts, lhs, lhs_tiles, scr, hook=None):
        for cb in range(4):
            if hook is not None:
                hook(cb)
            for pi, (k0, kcn) in enumerate(kparts):
                slab = wload(Wd, k0 * 128, kcn, cb * 512, 512)
                for i in range(4):
                    gemm_a(slab, kcn, lambda kc, i=i: lhs(kc, i), lhs_tiles, mm[i], k0=k0, first=(pi == 0), last=(pi == len(kparts) - 1))
            for i in range(4):
                t_ = tfr.next()
                act(t_[:], mm[i][:, 0:512], AF.Copy, [mm[i]], [t_])
                P.dma("sp", scr[i * 128:(i + 1) * 128, cb * 512:(cb + 1) * 512], t_[:], reads=[t_], writes=[scr])
                act(sqj[:], mm[i][:, 0:512], AF.Square, [mm[i]], [], pw=[st["ssq"]], accum_out=st["ssq"][:, i * 4 + cb:i * 4 + cb + 1])

    def tile_rstd(i):
        P.op("dve", lambda e: e.tensor_reduce(out=st["d"][:, 0:1], in_=st["ssq"][:, i * 4:(i + 1) * 4], axis=AX.X, op=ALU.add), [st["ssq"]], [st["d"]])
        rstd_of(st["d"][:, 0:1], D, st["d"][:, 2:3], st["d"][:, 1:2], [st["d"]], [st["d"], st["d"]])
        return st["d"][:, 2:3]

    def block_tail(row0, g, par, yrow0, nxt=None):
        proj_out_stage(w_out, [(0, 16)], lambda kc, i: mTv[:, kc, i * 128:(i + 1) * 128], [mT], mo_scr[par])
        for i in range(4):
            r = tile_rstd(i)
            mo_t, xt = xtr.next(), xtr.next()
            P.dma("sp", mo_t[:], mo_scr[par][i * 128:(i + 1) * 128, :], reads=[mo_scr[par]], writes=[mo_t])
            P.dma("sp", xt[:], x_all[row0 + i * 128:row0 + (i + 1) * 128, :], writes=[xt])
            stt(mo_t[:], mo_t[:], r, C1b[:], ALU.mult, ALU.mult, [mo_t, st["d"], C1b], [mo_t])
            tt(xt[:], xt[:], mo_t[:], ALU.add, [xt, mo_t], [xt])
            P.dma("sp", x1_scr[par][i * 128:(i + 1) * 128, :], xt[:], reads=[xt], writes=[x1_scr[par]])
            norm_tile(xt, i, g, AT2, BT2)
        P.handoff([QT, vtok, mT, attT, sog, hgT] + HGP_QT + HGP_SP, [fT])
        for jb in range(11):
            Wg = wload(w_fi, 0, 16, jb * 512, 512)
            Wu = wload(w_fi, 0, 16, DFF + jb * 512, 512)
            for m in range(4):
                bg, bu = mmr.next(), mmr.next()
                gemm_b(Wg, 16, m * 128, lambda kc: hT[:, kc, :], [hT], bg)
                gemm_b(Wu, 16, m * 128, lambda kc: hT[:, kc, :], [hT], bu)
                ta, tb_ = tfr.next(), tfr.next()
                sigmoid_inplace_from(ta, ta[:], bg[:, 0:512], [bg])
                tt(tb_[:], bg[:, 0:512], ta[:], ALU.mult, [bg, ta], [tb_])
                tt(fTv[:, jb * 4 + m, :], tb_[:], bu[:, 0:512], ALU.mult, [tb_, bu], [], pw=[fT])
        pend_xs = {}

        def hook(cb):
            if nxt is None:
                return
            if cb >= 1:
                norm_B(pend_xs[cb - 1], cb - 1, nxt[1], AT1, BT1)
            pend_xs[cb] = norm_one_A(nxt[0], cb)
        proj_out_stage(w_fo, [(0, 16), (16, 16), (32, 12)], lambda kc, i: fTv[:, kc, i * 128:(i + 1) * 128], [fT], mo_scr[par], hook)
        if nxt is not None:
            norm_B(pend_xs[3], 3, nxt[1], AT1, BT1)
        for i in range(4):
            r = tile_rstd(i)
            fo_t, x1t = xtr.next(), xtr.next()
            P.dma("sp", fo_t[:], mo_scr[par][i * 128:(i + 1) * 128, :], reads=[mo_scr[par]], writes=[fo_t])
            P.dma("sp", x1t[:], x1_scr[par][i * 128:(i + 1) * 128, :], reads=[x1_scr[par]], writes=[x1t])
            stt(fo_t[:], fo_t[:], r, C2b[:], ALU.mult, ALU.mult, [fo_t, st["d"], C2b], [fo_t])
            tt(x1t[:], x1t[:], fo_t[:], ALU.add, [x1t, fo_t], [x1t])
            P.dma("sp", y_own[yrow0 + i * 128:yrow0 + (i + 1) * 128, :], x1t[:], reads=[x1t], is_output=True)
        P.handoff([fT], [QT, vtok, attT, sog, hgT, mT] + HGP_QT + HGP_SP)

    def _program():
        ck("ada")
        for t in range(2):
            xt = xtr.next()
            P.dma("sp", xt[:, 0:256], cache_k[t * 128:(t + 1) * 128, :], writes=[xt])
            P.dma("sp", xt[:, 256:512], cache_v[t * 128:(t + 1) * 128, :], writes=[xt])
            kb = ptr.next()
            P.op("dve", lambda e, kb=kb, xt=xt: e.tensor_copy(out=kb[:], in_=xt[:, 0:512]), [xt], [kb])
            tbk = transposes(kb, lambda k, kb=kb: kb[:, k * 128:(k + 1) * 128], 2)
            kch = kchr.next()
            act(kch[:].rearrange("p g n -> p (g n)"), tbk[:, 0:256], AF.Copy, [tbk], [kch])
            t0_ = 4096 + t * 128
            P.dma("sp", KT_scr[:, :, t0_:t0_ + 128].rearrange("g d n -> d g n"), kch[:], reads=[kch], writes=[KT_scr])
            P.dma("sp", V_scr[t0_:t0_ + 128, :], kb[:, 256:512], reads=[kb], writes=[V_scr])

        ck("cache")
        import os
        _pb = [int(v) for v in os.environ.get("DBG_PRE_BLOCKS", "7,6,5,4,3,2,1,0").split(",")]
        _phg = os.environ.get("DBG_PRE_HG", "1") == "1"
        _pkv = os.environ.get("DBG_PRE_KV", "1") == "1"
        _ada_done = set()
        for blk in _pb:
            row0 = ROW_S + blk * 512
            load_norm_block(row0, 1)
            ck("pre_norm")
            P.dma("sp", ropet[:], rope_d[blk * 512:(blk + 1) * 512].rearrange("(i p) c a d -> p i (c a d)", p=128), writes=[ropet])
            if _pkv:
                kv_stage(True, blk * 512, 0)
            ck("pre_kv")
            if blk >= 1 and _phg:
                hgrn_pre(blk)
                ck("pre_hg")
            else:
                for h in range(8):
                    P.dma("sp", SB_scr[0, h, :, :], SX[1][h][:], reads=[SX[1][h]], writes=[SB_scr])
            _nx = [cb for cb in range(8, 24) if cb not in _ada_done][:2]
            _ada_done.update(_nx)
            ada_slabs(_nx)
            ck(f"pre_blk{blk}")

        ada_slabs([cb for cb in range(8, 24) if cb not in _ada_done])
        ada_rest()
        ck("pre")
        build_Cb(0)
        for pb_ in range(NPB):
            row0 = pb_ * 512
            if pb_ == 0 or stop_after is not None:
                load_norm_block(row0, 0)
            ck("p_norm")
            q_stage(False)
            ck("p_q")
            kv_stage(False, 0, row0)
            ck("p_kv")
            for si, seg in enumerate([[0, 1], [2, 3]]):
                attention(seg, 2, lambda j, g, si=si: (KTp[:, g, (si * 2 + j) * 128:(si * 2 + j + 1) * 128],
                                                        Vp[:, si * 2 + j, g * 128:(g + 1) * 128], [KTp, Vp]))
            if pb_ == 0:
                dump("d_attT", arena[:, 8192:12288], [128, 4096], [attT], BF16)
                dump("d_QT", arena[:, 0:4096], [128, 4096], [QT], BF16)
            ck("p_att")
            hgrn_main([[0, 1], [2, 3]], False, 0, pb_ * 2)
            if pb_ == 0:
                dump("d_hgT", arena[:, 16384:20480], [128, 4096], [hgT], BF16)
            ck("p_hg")
            merge_block()
            if pb_ == 0:
                dump("d_mT", arena[:, 0:8192], [128, 8192], [mT], BF16)
            ck("p_merge")
            block_tail(row0, 0, pb_ % 2, row0, None if stop_after is not None else ((row0 + 512, 0) if pb_ + 1 < NPB else (ROW_S, 1)))
            if pb_ == 0:
                dump("d_x1", x1_scr[0][:, :], [512, 2048], [x1_scr[0]])
                dump("d_fo", mo_scr[0][:, :], [512, 2048], [mo_scr[0]])
                dump("d_fT", arena[:, 0:22528], [128, 22528], [fT], BF16)
            ck("p_tail")

        ck("prompt")
        build_Cb(1)
        for h in range(8):
            P.dma("sp", SX[0][h][:], stA0[h, :, :], writes=[SX[0][h]])
            act(SXb[0][h][:], SX[0][h][:], AF.Copy, [SX[0][h]], [SXb[0][h]])
        for blk in range(NSB):
            row0 = ROW_S + blk * 512
            if stop_after is not None:
                load_norm_block(row0, 1)
            P.dma("sp", ropet[:], rope_d[blk * 512:(blk + 1) * 512].rearrange("(i p) c a d -> p i (c a d)", p=128), writes=[ropet])
            q_stage(True)
            ck("s_q")

            def chunk_src(j, g):
                kch, vch = kqr.next(), vqr.next()
                P.dma("sp", kch[:], KT_scr[g, :, j * 128:(j + 1) * 128], reads=[KT_scr], writes=[kch])
                P.dma("sp", vch[:], V_scr[j * 128:(j + 1) * 128, g * 128:(g + 1) * 128], reads=[V_scr], writes=[vch])
                return kch[:], vch[:], [kch, vch]
            attention([0, 1, 2, 3], 34, chunk_src)
            ck("s_att")
            hgrn_main([[0, 1, 2, 3]], True, blk, 0)
            ck("s_hg")
            merge_block()
            block_tail(row0, 1, blk % 2, 1024 + blk * 512, None if stop_after is not None else ((row0 + 512, 1) if blk + 1 < NSB else None))
            ck("s_tail")


    try:
        _program()
    except _Stop:
        pass
    P.emit()
    return nc


def _host_inputs(inp, cores=range(8)):
    f32 = np.float32
    x_prompt, x_sample = inp["x_prompt"], inp["x_sample"]
    w_in0 = inp["w_in"][0]
    half = 32
    inv = (10000.0 ** (-np.arange(half, dtype=f32) / half)).astype(f32)
    pos = np.arange(4096)
    r = (pos // 64).astype(f32)
    c = (pos % 64).astype(f32)
    ang = np.stack([r[:, None] * inv[None, :], c[:, None] * inv[None, :]], axis=1).astype(f32)
    rope_g = np.stack([np.cos(ang), np.sin(ang)], axis=1).astype(f32)
    s_idx = np.arange(128)
    same = (s_idx[:, None] // 32) == (s_idx[None, :] // 32)
    maskA = (same & (s_idx[:, None] <= s_idx[None, :])).astype(f32)
    maskB = (same & (s_idx[:, None] >= s_idx[None, :])).astype(f32)
    resetm = np.ones((128, 512), f32)
    resetm[:, ::32] = 0
    mask4 = (s_idx[:, None] // 32 == np.arange(4)[None, :]).astype(f32)
    cF = np.concatenate([np.eye(128, dtype=f32), maskA, maskB, resetm, mask4, np.ones((128, 512), f32)], axis=1)

    def fm(v, n):
        return np.ascontiguousarray(v.reshape(n, 128).T)

    cols = lambda a, b: w_in0[:, a:b]
    w_in_sw = np.ascontiguousarray(np.concatenate([cols(0, C_ZA), cols(C_ZB, C_VH), cols(C_ZA, C_ZB), cols(C_VH, NIN)], axis=1))
    rowb = np.ascontiguousarray(np.broadcast_to(np.stack([inp["q_norm"][0], inp["k_norm"][0], inp["hg_norm"][0]])[None], (128, 3, 128))).astype(f32)
    maps = []
    for j in cores:
        b, hf = j // 2, j % 2
        flip = hf == 1
        xp = x_prompt[4 * j:4 * j + 4]
        xs = x_sample[b]
        rp = rope_g
        if flip:
            xp = xp[:, ::-1]
            xs = xs[::-1]
            rp = rope_g[::-1]
        x_all = np.ascontiguousarray(np.concatenate([xp.reshape(1024, D), xs], axis=0))
        lbA, lbB = (inp["lb_bwd"], inp["lb_fwd"]) if flip else (inp["lb_fwd"], inp["lb_bwd"])
        vecT = np.concatenate([fm(inp["b_ada"][0], 96), fm(inp["norm_pre_mix"][0], 16), fm(inp["norm_post_mix"][0], 16),
                               fm(inp["norm_pre_ffn"][0], 16), fm(inp["norm_post_ffn"][0], 16),
                               fm(lbA[0], 8), fm(lbA[1], 8), fm(lbB[0], 8), fm(lbB[1], 8)], axis=1).astype(f32)
        condT = np.stack([fm(inp["c_ctx"], 16), fm(inp["c"][b], 16)], axis=2).astype(f32)
        stA, stB = (inp["state_bwd"], inp["state_fwd"]) if flip else (inp["state_fwd"], inp["state_bwd"])
        maps.append({
            "x_all": x_all, "rope": np.ascontiguousarray(rp),
            "cache_k": np.ascontiguousarray(inp["cache_k"][b, 0].reshape(256, 256)),
            "cache_v": np.ascontiguousarray(inp["cache_v"][b, 0].reshape(256, 256)),
            "stA0": np.ascontiguousarray(stA[b, 0]), "stB0": np.ascontiguousarray(stB[b, 0]),
            "condT": np.ascontiguousarray(condT), "vecT": np.ascontiguousarray(vecT), "rowb": rowb, "cF": cF,
            "w_ada": inp["w_ada"][0], "w_in": w_in_sw if flip else w_in0,
            "w_br_att": inp["w_br_att"][0], "w_br_hg": inp["w_br_hg"][0], "w_out": inp["w_out"][0],
            "w_ffn_in": inp["w_ffn_in"][0], "w_ffn_out": inp["w_ffn_out"][0],
        })
    return maps


def _assemble(res):
    f32 = np.float32
    y_prompt = np.zeros((32, 256, D), f32)
    y_sample = np.zeros((4, 4096, D), f32)
    nk = np.zeros((32, 1, 256, 2, 128), f32)
    nv = np.zeros((32, 1, 256, 2, 128), f32)
    sf = np.zeros((32, 1, 8, 128, 128), f32)
    sb = np.zeros((32, 1, 8, 128, 128), f32)
    for j in range(8):
        r = res[j]
        b, hf = j // 2, j % 2
        flip = hf == 1
        yp = r["y_own"][0:1024].reshape(4, 256, D)
        ys = r["y_own"][1024:3072]
        kc = r["kc_out"].reshape(4, 256, 2, 128)
        vc = r["vc_out"].reshape(4, 256, 2, 128)
        sA, sB = r["sA_out"], r["sB_out"]
        if flip:
            yp, ys, kc, vc = yp[:, ::-1], ys[::-1], kc[:, ::-1], vc[:, ::-1]
            sA, sB = sB, sA
        y_prompt[4 * j:4 * j + 4] = yp
        y_sample[b, hf * 2048:(hf + 1) * 2048] = ys
        nk[4 * j:4 * j + 4, 0] = kc
        nv[4 * j:4 * j + 4, 0] = vc
        sf[4 * j:4 * j + 4, 0] = sA
        sb[4 * j:4 * j + 4, 0] = sB
    return (y_prompt, y_sample, nk, nv, sf, sb)


def kernel(**inputs):
    inp = {k: np.asarray(v) for k, v in inputs.items()}
    nc = build()
    maps = _host_inputs(inp)
    res = run_bass_kernel_spmd(nc, maps, core_ids=list(range(8)))
    return _assemble(res.results)
```

```python
import numpy as np
from contextlib import ExitStack
import concourse.bass as bass
import concourse.mybir as mybir
from concourse.bass_utils import run_bass_kernel_spmd

F32 = mybir.dt.float32
BF16 = mybir.dt.bfloat16
AF = mybir.ActivationFunctionType
ALU = mybir.AluOpType
AX = mybir.AxisListType

D = 2048
DFF = 5632
NIN = 10752
EPS = 1e-6
C_QA, C_KA, C_VA, C_QH, C_ZA, C_ZB, C_VH, C_OG, C_GA, C_GB = 0, 1024, 1280, 1536, 2560, 3584, 4608, 5632, 6656, 8704
NPB = 2
NSB = 4
ROW_S = 1024

ENGS = ("pe", "act", "dve", "pool", "sp")
NRING = {"sp": 12, "pool": 6}


class T:
    def __init__(self, h, name=""):
        self.h = h
        self.name = name
        self.w = {}
        self.r = {}
        self.psum = False
        self.pw_open = False

    def __getitem__(self, idx):
        return self.h[idx]


class Prog:
    def __init__(self, nc):
        self.nc = nc
        self.q = {e: [] for e in ENGS}
        self.cnt = {e: 0 for e in ENGS}
        self.known = {e: {} for e in ENGS}
        self.dma_i = {"sp": 0, "pool": 0}
        self.final = {}
        self.stack = ExitStack()
        self.nalloc = 0

    def sb(self, shape, dt, name=None):
        self.nalloc += 1
        name = "b_" + (name or f"sb{self.nalloc}")
        h = self.stack.enter_context(self.nc.sbuf_tensor(name, list(shape), dt))
        return T(h, name)

    def ps(self, shape, dt, name=None):
        self.nalloc += 1
        name = "p_" + (name or f"ps{self.nalloc}")
        h = self.stack.enter_context(self.nc.psum_tensor(name, list(shape), dt))
        t = T(h, name)
        t.psum = True
        return t

    def view(self, t, name=""):
        return T(t.h, name)

    def _need(self, eng, reads, writes, pwrites=()):
        need = {}

        def add(ev):
            k, v = ev
            if need.get(k, 0) < v:
                need[k] = v

        for t in reads:
            for ev in t.w.items():
                add(ev)
            if t.psum:
                for k, v in t.r.items():
                    if k != eng:
                        add((k, v))
        for t in writes:
            for ev in t.w.items():
                add(ev)
            for ev in t.r.items():
                add(ev)
        for t in pwrites:
            if t.r or not t.pw_open:
                for ev in t.w.items():
                    add(ev)
            for ev in t.r.items():
                add(ev)
        out = []
        for k, v in need.items():
            if k == eng and eng == "pe":
                continue
            if self.known[eng].get(k, 0) >= v:
                continue
            self.known[eng][k] = v
            out.append((k, v))
        return out

    def _mark(self, ev, reads, writes, pwrites=()):
        k, v = ev
        for t in reads:
            if t.r.get(k, 0) < v:
                t.r[k] = v
        for t in writes:
            t.w = {k: v}
            t.r = {}
            t.pw_open = False
        for t in pwrites:
            if t.r or not t.pw_open:
                t.w = {}
                t.r = {}
                t.pw_open = True
            if t.w.get(k, 0) < v:
                t.w[k] = v

    def op(self, eng, fn, reads=(), writes=(), pwrites=()):
        for k, v in self._need(eng, reads, writes, pwrites):
            self.q[eng].append(("wait", k, v))
        self.cnt[eng] += 1
        self.q[eng].append(("op", fn))
        self._mark((eng, self.cnt[eng]), reads, writes, pwrites)

    def dma(self, queue, out_ap, in_ap, reads=(), writes=(), is_output=False):
        for k, v in self._need(queue, reads, writes):
            self.q[queue].append(("wait", k, v))
        i = self.dma_i[queue]
        self.dma_i[queue] += 1
        slot, gen = i % NRING[queue], i // NRING[queue]
        key = (queue, slot)
        if gen > 0 and self.known[queue].get(key, 0) < 16 * gen:
            self.q[queue].append(("wait", key, 16 * gen))
            self.known[queue][key] = 16 * gen
        self.q[queue].append(("dma", out_ap, in_ap, key))
        ev = (key, 16 * (gen + 1))
        self._mark(ev, reads, writes)
        if is_output:
            self.final[key] = max(self.final.get(key, 0), ev[1])

    def handoff(self, src, dst):
        for d in dst:
            for s_ in src:
                evs = list(s_.r.items()) + list(s_.w.items())
                for k, v in evs:
                    if d.r.get(k, 0) < v:
                        d.r[k] = v

    def emit(self):
        nc = self.nc
        for key, v in self.final.items():
            if self.known["sp"].get(key, 0) < v:
                self.q["sp"].append(("wait", key, v))
        keys = set()
        for e in ENGS:
            for it in self.q[e]:
                if it[0] == "wait":
                    keys.add(it[1])
                elif it[0] == "dma":
                    keys.add(it[3])
            keys.add(e)
        sems = {}
        for k in sorted(keys, key=str):
            nm = "s_" + (k if isinstance(k, str) else f"{k[0]}{k[1]}")
            sems[k] = self.stack.enter_context(nc.semaphore(nm))
        engobj = {"pe": "tensor", "act": "scalar", "dve": "vector", "pool": "gpsimd", "sp": "sync"}

        def run(e, eng):
            for it in self.q[e]:
                if it[0] == "wait":
                    eng.wait_ge(sems[it[1]], it[2])
                elif it[0] == "op":
                    it[1](eng).then_inc(sems[e], 1)
                else:
                    eng.dma_start(out=it[1], in_=it[2]).then_inc(sems[it[3]], 16)

        with nc.Block() as block:
            for e in ENGS:
                if not self.q[e]:
                    continue
                getattr(block, engobj[e])(lambda eng, e=e: run(e, eng))
        self.stack.close()


class Ring:
    def __init__(self, items):
        self.items = items
        self.i = 0

    def next(self):
        t = self.items[self.i % len(self.items)]
        self.i += 1
        return t


class _Stop(Exception):
    pass


def build(stop_after=None, debug=False):
    nc = bass.Bass("TRN2", target_bir_lowering=False)
    P = Prog(nc)

    def ck(name):
        if stop_after == name:
            raise _Stop()

    def din(name, shape, dt=F32):
        return nc.dram_tensor(name, list(shape), dt, kind="ExternalInput").ap()

    def dout(name, shape, dt=F32):
        return nc.dram_tensor(name, list(shape), dt, kind="ExternalOutput").ap()

    def dscr(name, shape, dt):
        return T(nc.dram_tensor(name, list(shape), dt, kind="Internal").ap(), name)

    x_all = din("x_all", [5120, D])
    rope_d = din("rope", [4096, 2, 2, 32])
    cache_k = din("cache_k", [256, 256])
    cache_v = din("cache_v", [256, 256])
    stA0 = din("stA0", [8, 128, 128])
    stB0 = din("stB0", [8, 128, 128])
    condT_d = din("condT", [128, 16, 2])
    vecT_d = din("vecT", [128, 192])
    rowb_d = din("rowb", [128, 3, 128])
    cF_d = din("cF", [128, 1412])
    w_ada = din("w_ada", [D, 6 * D])
    w_in = din("w_in", [D, NIN])
    w_ba = din("w_br_att", [1024, D])
    w_bh = din("w_br_hg", [1024, D])
    w_out = din("w_out", [D, D])
    w_fi = din("w_ffn_in", [D, 2 * DFF])
    w_fo = din("w_ffn_out", [DFF, D])

    y_own = dout("y_own", [3072, D])
    kc_out = dout("kc_out", [1024, 256])
    vc_out = dout("vc_out", [1024, 256])
    sA_out = dout("sA_out", [4, 8, 128, 128])
    sB_out = dout("sB_out", [4, 8, 128, 128])

    KT_scr = dscr("KT_scr", [2, 128, 4352], BF16)
    V_scr = dscr("V_scr", [4352, 256], BF16)
    SB_scr = dscr("SB_scr", [4, 8, 128, 128], F32)
    x1_scr = [dscr(f"x1_scr{i}", [512, D], F32) for i in range(2)]
    mo_scr = [dscr(f"mo_scr{i}", [512, D], F32) for i in range(2)]

    def dump(name, src_ap, shape, reads, dt=F32):
        if not debug:
            return
        d = nc.dram_tensor(name, list(shape), F32, kind="ExternalOutput").ap()
        P.dma("pool" if dt != F32 else "sp", d, src_ap, reads=reads, is_output=True)

    cF = P.sb([128, 1412], F32, "cF")
    identf = cF[:, 0:128]
    maskA = cF[:, 128:256]
    maskB = cF[:, 256:384]
    resetm = cF[:, 384:896]
    mask4 = cF[:, 896:900]
    ones512 = cF[:, 900:1412]
    identb = P.sb([128, 128], BF16, "identb")
    onesb = P.sb([128, 128], BF16, "onesb")
    onesf = P.sb([128, 128], F32, "onesf")
    rowb = P.sb([128, 3, 128], F32, "rowb")
    vecT = P.sb([128, 192], F32, "vecT")
    condT = P.sb([128, 32], F32, "condT")
    scT = P.sb([128, 16, 2], BF16, "scT")
    modT = P.sb([128, 96, 2], F32, "modT")
    AT1 = P.sb([128, 16, 2], F32, "AT1")
    AT2 = P.sb([128, 16, 2], F32, "AT2")
    GT1 = P.sb([128, 16, 2], F32, "GT1")
    GT2 = P.sb([128, 16, 2], F32, "GT2")
    oml = P.sb([128, 16], F32, "oml")
    noml = P.sb([128, 16], F32, "noml")
    C1b = P.sb([128, D], F32, "C1b")
    C2b = P.sb([128, D], F32, "C2b")
    dgt = Ring([P.sb([128, 128], F32, f"dgt{i}") for i in range(2)])

    slabs = Ring([P.sb([128, 16, 512], BF16, f"slab{i}") for i in range(2)])
    hT = P.sb([128, 16, 512], BF16, "hT")
    xtr = Ring([P.sb([128, D], F32, f"xt{i}") for i in range(2)])
    xsr = Ring([P.sb([128, D], BF16, f"xs{i}") for i in range(1)])
    sqj = P.sb([128, 512], BF16, "sqj")
    arena = P.sb([128, 22528], BF16, "arena")
    QT = P.view(arena, "QT")
    vtok = P.view(arena, "vtok")
    mT = P.view(arena, "mT")
    attT = P.view(arena, "attT")
    sog = P.view(arena, "sog")
    hgT = P.view(arena, "hgT")
    fT = P.view(arena, "fT")
    QTv = arena[:, 0:4096].rearrange("p (g i n) -> p g i n", g=2, i=4)
    vtokv = arena[:, 4096:8192].rearrange("p (i n) -> p i n", i=4)
    mTv = arena[:, 0:8192].rearrange("p (c n) -> p c n", c=16)
    attTv = arena[:, 8192:12288].rearrange("p (c n) -> p c n", c=8)
    sogv = arena[:, 12288:16384].rearrange("p (i n) -> p i n", i=4)
    hgTv = arena[:, 16384:20480].rearrange("p (c n) -> p c n", c=8)
    fTv = arena[:, 0:22528].rearrange("p (c n) -> p c n", c=44)

    tf = [P.sb([128, 512], F32, f"tf{i}") for i in range(6)]
    tfr = Ring(tf)
    qs = P.sb([128, 512], BF16, "qs")
    qt = [P.sb([128, 512], BF16, f"qt{i}") for i in range(2)]
    kt = [P.sb([128, 512], BF16, f"kt{i}") for i in range(2)]
    kh = [P.sb([128, 512], BF16, f"kh{i}") for i in range(2)]
    qb16 = P.sb([128, 512], BF16, "qb16")
    kTok = [P.sb([128, 4, 128], BF16, f"kTok{i}") for i in range(2)]
    ATm = [P.sb([128, 4, 128], BF16, f"ATm{i}") for i in range(2)]
    Vbd = P.sb([128, 4, 512], BF16, "Vbd")
    ebt = [P.sb([128, 16], F32, f"ebt{i}") for i in range(2)]
    ebt1 = [P.sb([128, 16], F32, f"ebtp{i}") for i in range(2)]
    Vbd1 = P.sb([128, 4, 512], BF16, "Vbd1")
    _r = lambda k: arena[:, k * 512:(k + 1) * 512]
    hqs1 = T(_r(0), "hqs1")
    hqt1 = [T(_r(1), "hqt10"), T(_r(2), "hqt11")]
    hkt1 = [T(_r(3), "hkt10"), T(_r(4), "hkt11")]
    hkh1 = [T(_r(5), "hkh10"), T(_r(6), "hkh11")]
    _r3 = lambda k: arena[:, 20480 + k * 512:20480 + (k + 1) * 512].rearrange("p (i n) -> p i n", i=4)
    hkTok1 = [T(_r3(0), "hkTok10"), T(_r3(1), "hkTok11")]
    hATm1 = [T(_r3(2), "hATm10"), T(_r3(3), "hATm11")]
    HGP_QT = [hqs1] + hqt1 + hkt1 + hkh1
    HGP_SP = hkTok1 + hATm1
    ptr = Ring([P.sb([128, 512], BF16, f"pt{i}") for i in range(6)])
    kqr = Ring([P.sb([128, 128], BF16, f"kq{i}") for i in range(8)])
    vqr = Ring([P.sb([128, 128], BF16, f"vq{i}") for i in range(10)])
    kchr = Ring([P.sb([128, 2, 128], BF16, f"kch{i}") for i in range(3)])
    vchr = Ring([P.sb([128, 256], BF16, f"vch{i}") for i in range(3)])
    KTp = P.sb([128, 2, 512], BF16, "KTp")
    Vp = P.sb([128, 4, 256], BF16, "Vp")
    SX = [[P.sb([128, 128], F32, f"S{x}{h}") for h in range(8)] for x in range(2)]
    SXb = [[P.sb([128, 128], BF16, f"Sb{x}{h}") for h in range(8)] for x in range(2)]
    ropet = P.sb([128, 4, 128], F32, "ropet")
    st = {n: P.sb([128, 16], F32, "st_" + n) for n in ("a", "b", "c", "d", "ssq", "e", "f", "g")}
    kvf = P.sb([128, 512], F32, "kvf")

    mm = [P.ps([128, 512], F32, f"mm{i}") for i in range(4)]
    mmr = Ring(mm)
    acc = [P.ps([128, 512], F32, f"acc{i}") for i in range(2)]
    tb = [P.ps([128, 1024], BF16, f"tb{i}") for i in range(2)]
    tbr = Ring(tb)
    HB = [dict(qs=qs, qt=qt, kt=kt, kh=kh, kTok=kTok, ATm=ATm, Vbd=Vbd, ebt=ebt, obank=acc[0]),
          dict(qs=hqs1, qt=hqt1, kt=hkt1, kh=hkh1, kTok=hkTok1, ATm=hATm1, Vbd=Vbd1, ebt=ebt1, obank=acc[1])]

    def act(out, in_, func, reads, writes, pw=(), **kw):
        P.op("act", lambda e: e.activation(out=out, in_=in_, func=func, **kw), reads, writes, pw)

    def tt(out, in0, in1, op, reads, writes, eng="dve", pw=()):
        P.op(eng, lambda e: e.tensor_tensor(out=out, in0=in0, in1=in1, op=op), reads, writes, pw)

    def ts(out, in0, s1, s2, op0, op1, reads, writes, eng="dve", pw=()):
        if s2 is None:
            P.op(eng, lambda e: e.tensor_scalar(out=out, in0=in0, scalar1=s1, scalar2=None, op0=op0), reads, writes, pw)
        else:
            P.op(eng, lambda e: e.tensor_scalar(out=out, in0=in0, scalar1=s1, scalar2=s2, op0=op0, op1=op1), reads, writes, pw)

    def stt(out, in0, sc, in1, op0, op1, reads, writes):
        P.op("dve", lambda e: e.scalar_tensor_tensor(out=out, in0=in0, scalar=sc, in1=in1, op0=op0, op1=op1), reads, writes)

    def rstd_of(ss_ap, n, out_ap, tmp_ap, tiles_r, tiles_w):
        act(tmp_ap, ss_ap, AF.Ln, tiles_r, tiles_w[:1], scale=1.0 / n, bias=EPS)
        act(out_ap, tmp_ap, AF.Exp, tiles_w[:1], tiles_w[1:], scale=-0.5)

    def sigmoid_inplace_from(dst_t, dst_ap, src_ap, src_tiles):
        act(dst_ap, src_ap, AF.Exp, src_tiles, [dst_t], scale=-1.0)
        act(dst_ap, dst_ap, AF.Ln, [dst_t], [dst_t], bias=1.0)
        act(dst_ap, dst_ap, AF.Exp, [dst_t], [dst_t], scale=-1.0)

    wcache = {}

    def wload_multi(parts, kcn, width, cache=True):
        key = tuple((p[0].name, p[1], p[2], p[3], p[4]) for p in parts) + (kcn,)
        slab = slabs.next()
        if cache and key in wcache:
            scr = wcache[key]
            P.dma("pool", slab[:, 0:kcn, 0:width], scr[:, :].rearrange("p (k n) -> p k n", k=kcn), reads=[scr], writes=[slab])
            return slab
        for (Wd, r0, c0, ncols, dst_c0) in parts:
            src = Wd[r0:r0 + kcn * 128, c0:c0 + ncols].rearrange("(k p) n -> p k n", p=128)
            P.dma("pool", slab[:, 0:kcn, dst_c0:dst_c0 + ncols], src, writes=[slab])
        if cache:
            scr = dscr(f"wc{len(wcache)}", [128, kcn * width], BF16)
            P.dma("sp", scr[:, :].rearrange("p (k n) -> p k n", k=kcn), slab[:, 0:kcn, 0:width], reads=[slab], writes=[scr])
            wcache[key] = scr
        return slab

    def wload(Wd, r0, kcn, c0, ncols, cache=True):
        return wload_multi([(Wd, r0, c0, ncols, 0)], kcn, ncols, cache)

    def gemm_a(slab, kcn, lhs, lhs_tiles, bank, ncols=512, k0=0, first=True, last=True):
        def f(e):
            for kc in range(kcn):
                ins = e.matmul(bank[:, 0:ncols], lhsT=lhs(k0 + kc), rhs=slab[:, kc, 0:ncols],
                               start=(first and kc == 0), stop=(last and kc == kcn - 1))
            return ins
        P.op("pe", f, [slab] + lhs_tiles, [bank])

    def gemm_b(slab, kcn, c0, rhs, rhs_tiles, bank):
        def f(e):
            for kc in range(kcn):
                ins = e.matmul(bank[:, 0:512], lhsT=slab[:, kc, c0:c0 + 128], rhs=rhs(kc),
                               start=(kc == 0), stop=(kc == kcn - 1))
            return ins
        P.op("pe", f, [slab] + rhs_tiles, [bank])

    def transposes(src_t, src_aps, n):
        bank = tbr.next()

        def f(e):
            for k in range(n):
                ins = e.transpose(bank[:, k * 128:(k + 1) * 128], src_aps(k), identb[:])
            return ins
        P.op("pe", f, [src_t, identb], [bank])
        return bank

    P.dma("sp", cF[:], cF_d[:, :], writes=[cF])
    P.dma("sp", rowb[:], rowb_d[:, :, :], writes=[rowb])
    P.dma("sp", vecT[:], vecT_d[:, :], writes=[vecT])
    P.dma("sp", condT[:], condT_d.rearrange("p k g -> p (k g)"), writes=[condT])
    P.op("dve", lambda e: e.tensor_copy(out=identb[:], in_=identf), [cF], [identb])
    P.op("dve", lambda e: e.memset(onesb[:], 1.0), [], [onesb])
    P.op("dve", lambda e: e.memset(onesf[:], 1.0), [], [onesf])
    act(rowb[:, 0, :], rowb[:, 0, :], AF.Copy, [rowb], [rowb], scale=float(128 ** -0.5))
    for h in range(8):
        P.dma("sp", SX[1][h][:], stB0[h, :, :], writes=[SX[1][h]])

    tt(oml[:, 0:8], vecT[:, 160:168], vecT[:, 168:176], ALU.subtract, [vecT], [oml])
    tt(oml[:, 8:16], vecT[:, 176:184], vecT[:, 184:192], ALU.subtract, [vecT], [oml])
    act(oml[:], oml[:], AF.Exp, [oml], [oml])
    act(oml[:], oml[:], AF.Ln, [oml], [oml], bias=1.0)
    act(oml[:], oml[:], AF.Exp, [oml], [oml], scale=-1.0)
    ts(noml[:], oml[:], -1.0, None, ALU.mult, None, [oml], [noml])

    t0 = tf[0]
    act(t0[:, 0:32], condT[:], AF.Exp, [condT], [t0], scale=-1.0)
    ts(t0[:, 0:32], t0[:, 0:32], 1.0, None, ALU.add, None, [t0], [t0])
    P.op("dve", lambda e: e.reciprocal(out=t0[:, 0:32], in_=t0[:, 0:32]), [t0], [t0])
    tt(scT[:].rearrange("p k g -> p (k g)"), condT[:], t0[:, 0:32], ALU.mult, [condT, t0], [scT])
    pada = acc[0]

    def ada_slabs(cbs):
        for cb in cbs:
            cur = wload(w_ada, 0, 16, cb * 512, 512, cache=False)

            def f(e, cb=cb, cur=cur):
                for m in range(4):
                    c = cb * 4 + m
                    for kc in range(16):
                        ins = e.matmul(pada[:, 2 * c:2 * c + 2], lhsT=cur[:, kc, m * 128:(m + 1) * 128], rhs=scT[:, kc, :],
                                       start=(kc == 0), stop=(kc == 15))
                return ins
            P.op("pe", f, [cur, scT], [pada])

    def ada_finish(c0, c1):
        tt(modT[:, c0:c1, :], pada[:, 2 * c0:2 * c1].rearrange("p (c g) -> p c g", g=2),
           vecT[:, c0:c1].unsqueeze(2).broadcast_to([128, c1 - c0, 2]), ALU.add, [pada, vecT], [modT])

    ada_slabs(range(8))
    ada_finish(0, 32)
    for g in range(2):
        stt(AT1[:, :, g], modT[:, 16:32, g], 1.0, vecT[:, 96:112], ALU.add, ALU.mult, [modT, vecT], [AT1])

    def ada_rest():
        ada_finish(32, 96)
        for g in range(2):
            stt(AT2[:, :, g], modT[:, 64:80, g], 1.0, vecT[:, 128:144], ALU.add, ALU.mult, [modT, vecT], [AT2])
            tt(GT1[:, :, g], modT[:, 32:48, g], vecT[:, 112:128], ALU.mult, [modT, vecT], [GT1])
            tt(GT2[:, :, g], modT[:, 80:96, g], vecT[:, 144:160], ALU.mult, [modT, vecT], [GT2])
    BT1 = lambda kc, g: modT[:, kc, g:g + 1]
    BT2 = lambda kc, g: modT[:, 48 + kc, g:g + 1]

    def build_Cb(g):
        for GT, Cb in ((GT1, C1b), (GT2, C2b)):
            for q4 in range(4):
                bank = mmr.next()
                for k in range(4):
                    kc = q4 * 4 + k
                    d_ = dgt.next()
                    ts(d_[:], identf, GT[:, kc, g:g + 1], None, ALU.mult, None, [cF, GT], [d_])
                    P.op("pe", lambda e, d_=d_, bank=bank, k=k: e.matmul(bank[:, k * 128:(k + 1) * 128], lhsT=onesf[:], rhs=d_[:],
                                                                       start=True, stop=True), [onesf, d_], [bank])
                act(Cb[:, q4 * 512:(q4 + 1) * 512], bank[:, 0:512], AF.Copy, [bank], [Cb])

    def norm_tile(xt, i, g, AT, BT):
        norm_B(norm_A(xt), i, g, AT, BT)

    def norm_A(xt):
        xs = xsr.next()
        act(xs[:], xt[:], AF.Square, [xt], [xs, st["a"]], accum_out=st["a"][:, 0:1])
        rstd_of(st["a"][:, 0:1], D, st["a"][:, 2:3], st["a"][:, 1:2], [st["a"]], [st["a"], st["a"]])
        ts(xs[:], xt[:], st["a"][:, 2:3], None, ALU.mult, None, [xt, st["a"]], [xs])
        return xs

    def norm_B(xs, i, g, AT, BT):
        for half in range(2):
            bank = transposes(xs, lambda k, half=half: xs[:, (half * 8 + k) * 128:(half * 8 + k + 1) * 128], 8)
            for k in range(8):
                kc = half * 8 + k
                o = hT[:, kc, i * 128:(i + 1) * 128]
                src = bank[:, k * 128:(k + 1) * 128]
                if half == 0:
                    act(o, src, AF.Identity, [bank, AT, modT], [], pw=[hT], scale=AT[:, kc, g:g + 1], bias=BT(kc, g))
                else:
                    ts(o, src, AT[:, kc, g:g + 1], BT(kc, g), ALU.mult, ALU.add, [bank, AT, modT], [], pw=[hT])

    def norm_one_A(row0, i):
        xt = xtr.next()
        P.dma("sp", xt[:], x_all[row0 + i * 128:row0 + (i + 1) * 128, :], writes=[xt])
        return norm_A(xt)

    def load_norm_block(row0, g):
        xts = []
        xt = xtr.next()
        P.dma("sp", xt[:], x_all[row0:row0 + 128, :], writes=[xt])
        for i in range(4):
            nxt = None
            if i < 3:
                nxt = xtr.next()
                P.dma("sp", nxt[:], x_all[row0 + (i + 1) * 128:row0 + (i + 2) * 128, :], writes=[nxt])
            norm_tile(xt, i, g, AT1, BT1)
            xt = nxt

    def headnorm(bank, c0, nh, nrow, out_ap, out_t):
        w = nh * 128
        t_sq = tfr.next()
        act(t_sq[:, 0:w], bank[:, c0:c0 + w], AF.Square, [bank], [t_sq])
        P.op("dve", lambda e: e.tensor_reduce(out=st["b"][:, 0:nh], in_=t_sq[:, 0:w].rearrange("p (h d) -> p h d", h=nh),
                                             axis=AX.X, op=ALU.add), [t_sq], [st["b"]])
        rstd_of(st["b"][:, 0:nh], 128, st["b"][:, 8:8 + nh], st["b"][:, 4:4 + nh], [st["b"]], [st["b"], st["b"]])
        tt(t_sq[:, 0:w].rearrange("p (h d) -> p h d", h=nh), bank[:, c0:c0 + w].rearrange("p (h d) -> p h d", h=nh),
           st["b"][:, 8:8 + nh].unsqueeze(2).broadcast_to([128, nh, 128]), ALU.mult, [bank, st["b"]], [t_sq])
        tt(out_ap.rearrange("p (h d) -> p h d", h=nh), t_sq[:, 0:w].rearrange("p (h d) -> p h d", h=nh),
           rowb[:, nrow, :].unsqueeze(1).broadcast_to([128, nh, 128]), ALU.mult, [t_sq, rowb], [out_t])

    def rope(src_t, src_ap, nh, i, out_t, out_ap):
        v = src_ap.rearrange("p (h a f d) -> p h a f d", h=nh, a=2, f=2)
        o = out_ap.rearrange("p (h a f d) -> p h a f d", h=nh, a=2, f=2)
        tbl = ropet[:, i, :].rearrange("p (c a d) -> p c a d", c=2, a=2)
        cos = tbl[:, 0, :, :].unsqueeze(1).broadcast_to([128, nh, 2, 32])
        sin = tbl[:, 1, :, :].unsqueeze(1).broadcast_to([128, nh, 2, 32])
        a1, a2 = v[:, :, :, 0, :], v[:, :, :, 1, :]
        t1, t2 = tfr.next(), tfr.next()
        w = nh * 64
        t1v = t1[:, 0:w].rearrange("p (h a d) -> p h a d", h=nh, a=2)
        t2v = t2[:, 0:w].rearrange("p (h a d) -> p h a d", h=nh, a=2)
        tt(t1v, a1, cos, ALU.mult, [src_t, ropet], [t1])
        tt(t2v, a2, sin, ALU.mult, [src_t, ropet], [t2])
        tt(o[:, :, :, 0, :], t1v, t2v, ALU.subtract, [t1, t2], [out_t])
        t3, t4 = tfr.next(), tfr.next()
        t3v = t3[:, 0:w].rearrange("p (h a d) -> p h a d", h=nh, a=2)
        t4v = t4[:, 0:w].rearrange("p (h a d) -> p h a d", h=nh, a=2)
        tt(t3v, a2, cos, ALU.mult, [src_t, ropet], [t3])
        tt(t4v, a1, sin, ALU.mult, [src_t, ropet], [t4])
        tt(o[:, :, :, 1, :], t3v, t4v, ALU.add, [t3, t4], [out_t])

    def q_stage(sample):
        P.handoff(HGP_QT, [QT])
        for s in range(2):
            slab = wload(w_in, 0, 16, C_QA + s * 512, 512)
            banks = {}

            def issue(i):
                banks[i] = mmr.next()
                gemm_a(slab, 16, lambda kc, i=i: hT[:, kc, i * 128:(i + 1) * 128], [hT], banks[i])
            issue(0)
            for i in range(4):
                if i + 1 < 4:
                    issue(i + 1)
                bank = banks[i]
                if sample:
                    qn = tfr.next()
                    headnorm(bank, 0, 4, 0, qn[:], qn)
                    rope(qn, qn[:], 4, i, qb16, qb16[:])
                else:
                    headnorm(bank, 0, 4, 0, qb16[:], qb16)
                tbk = transposes(qb16, lambda k: qb16[:, k * 128:(k + 1) * 128], 4)
                act(QTv[:, s, i, :], tbk[:, 0:512], AF.Copy, [tbk], [QT])

    def kv_stage(sample, blk_local_tok0, out_row0):
        slab = wload(w_in, 0, 16, C_KA, 512)
        banks = {}

        def issue(i):
            banks[i] = mmr.next()
            gemm_a(slab, 16, lambda kc, i=i: hT[:, kc, i * 128:(i + 1) * 128], [hT], banks[i])
        issue(0)
        for i in range(4):
            if i + 1 < 4:
                issue(i + 1)
            bank = banks[i]
            kb = ptr.next()
            if sample:
                kn = tfr.next()
                headnorm(bank, 0, 2, 1, kn[:, 0:256], kn)
                rope(kn, kn[:, 0:256], 2, i, kb, kb[:, 0:256])
            else:
                headnorm(bank, 0, 2, 1, kvf[:, 0:256], kvf)
                P.op("dve", lambda e, kb=kb: e.tensor_copy(out=kb[:, 0:256], in_=kvf[:, 0:256]), [kvf], [kb])
            act(kvf[:, 256:512], bank[:, 256:512], AF.Copy, [bank], [kvf])
            if not sample:
                P.dma("sp", kc_out[out_row0 + i * 128:out_row0 + (i + 1) * 128, :], kvf[:, 0:256], reads=[kvf], is_output=True)
                P.dma("sp", vc_out[out_row0 + i * 128:out_row0 + (i + 1) * 128, :], kvf[:, 256:512], reads=[kvf], is_output=True)
            tbk = transposes(kb, lambda k, kb=kb: kb[:, k * 128:(k + 1) * 128], 2)
            if sample:
                kch = kchr.next()
                act(kch[:].rearrange("p g n -> p (g n)"), tbk[:, 0:256], AF.Copy, [tbk], [kch])
                t0_ = blk_local_tok0 + i * 128
                P.dma("sp", KT_scr[:, :, t0_:t0_ + 128].rearrange("g d n -> d g n"), kch[:], reads=[kch], writes=[KT_scr])
                vch = vchr.next()
                P.op("dve", lambda e, vch=vch: e.tensor_copy(out=vch[:], in_=kvf[:, 256:512]), [kvf], [vch])
                P.dma("sp", V_scr[t0_:t0_ + 128, :], vch[:], reads=[vch], writes=[V_scr])
            else:
                act(KTp[:, :, i * 128:(i + 1) * 128], tbk[:, 0:256].rearrange("p (g n) -> p g n", g=2), AF.Copy, [tbk], [KTp])
                P.op("dve", lambda e, i=i: e.tensor_copy(out=Vp[:, i, :], in_=kvf[:, 256:512]), [kvf], [Vp])

    def attention(qtiles, nchunks, chunk_src):
        LOOK, PF = 3, 4
        Sbank = Ring(mm)
        ob, db = acc[0], acc[1]
        for i in qtiles:
            for g in range(2):
                srcs = {}

                def get(j):
                    if j not in srcs and j < nchunks:
                        srcs[j] = chunk_src(j, g)
                    return srcs.get(j)
                pend = []

                def flush_one():
                    j, pt, vv, tiles = pend.pop(0)

                    def f(e, pt=pt, vv=vv, j=j):
                        e.matmul(ob[:, 0:512], lhsT=vv, rhs=pt[:], start=(j == 0), stop=(j == nchunks - 1))
                        return e.matmul(db[:, 0:512], lhsT=onesb[:], rhs=pt[:], start=(j == 0), stop=(j == nchunks - 1))
                    P.op("pe", f, tiles + [pt, onesb], [ob, db])
                for j in range(nchunks):
                    for jj in range(j, min(j + PF, nchunks)):
                        get(jj)
                    kT, vv, tiles = get(j)
                    sb_ = Sbank.next()
                    P.op("pe", lambda e, sb_=sb_, kT=kT, g=g, i=i: e.matmul(sb_[:, 0:512], lhsT=kT, rhs=QTv[:, g, i, :], start=True, stop=True),
                         tiles + [QT], [sb_])
                    pt = ptr.next()
                    act(pt[:], sb_[:, 0:512], AF.Exp, [sb_], [pt])
                    pend.append((j, pt, vv, tiles))
                    if len(pend) > LOOK:
                        flush_one()
                while pend:
                    flush_one()
                rD = tfr.next()
                act(rD[:], db[:, 0:512], AF.Ln, [db], [rD])
                act(rD[:], rD[:], AF.Exp, [rD], [rD], scale=-1.0)
                tt(attTv[:, 4 * g:4 * g + 4, i * 128:(i + 1) * 128], ob[:, 0:512].rearrange("p (h n) -> p h n", h=4),
                   rD[:].rearrange("p (h n) -> p h n", h=4), ALU.mult, [ob, rD], [attT])

    def hg_gates(zbank, x, h, need_q, B):
        c = x * 8 + h
        e0, s1, lf, bc, t4, t5 = tf
        ebt_, qt_, kt_, kh_, kTok_, qs_ = B["ebt"], B["qt"], B["kt"], B["kh"], B["kTok"], B["qs"]
        act(e0[:], zbank[:, 0:512], AF.Exp, [zbank], [e0]); yield
        act(e0[:], e0[:], AF.Ln, [e0], [e0], bias=1.0); yield
        act(s1[:], e0[:], AF.Exp, [e0], [s1], scale=-1.0); yield
        act(lf[:], s1[:], AF.Ln, [s1, noml], [lf], scale=noml[:, c:c + 1], bias=1.0); yield
        P.op("dve", lambda e: e.tensor_tensor_scan(out=bc[:], data0=resetm, data1=lf[:], initial=0.0, op0=ALU.mult, op1=ALU.add),
             [cF, lf], [bc]); yield
        act(ebt_[x][:], bc[:, 31::32], AF.Exp, [bc], [ebt_[x]]); yield
        btot = bc[:, 31::32].unsqueeze(2).broadcast_to([128, 16, 32])
        v3 = lambda t: t[:].rearrange("p (c s) -> p c s", s=32)
        if x == 0:
            cum = bc
            tt(v3(t4), btot, v3(bc), ALU.subtract, [bc], [t4]); yield
            rem = t4
        else:
            tt(t4[:], bc[:], lf[:], ALU.subtract, [bc, lf], [t4]); yield
            rem = t4
            tt(v3(e0), btot, v3(t4), ALU.subtract, [bc, t4], [e0]); yield
            cum = e0
        if need_q:
            act(t5[:], cum[:], AF.Exp, [cum], [t5]); yield
            tt(qt_[x][:], qs_[:], t5[:], ALU.mult, [qs_, t5], [qt_[x]]); yield
            act(t5[:], cum[:], AF.Exp, [cum], [t5], scale=-1.0); yield
            stt(kt_[x][:], s1[:], oml[:, c:c + 1], t5[:], ALU.mult, ALU.mult, [s1, oml, t5], [kt_[x]]); yield
        act(t5[:], rem[:], AF.Exp, [rem], [t5]); yield
        stt(kh_[x][:], s1[:], oml[:, c:c + 1], t5[:], ALU.mult, ALU.mult, [s1, oml, t5], [kh_[x]]); yield
        tbk = transposes(kh_[x], lambda k: kh_[x][:, k * 128:(k + 1) * 128], 4); yield
        act(kTok_[x][:].rearrange("p i n -> p (i n)"), tbk[:, 0:512], AF.Copy, [tbk], [kTok_[x]]); yield

    def make_vbd(h, B):
        Vb = B["Vbd"]
        for i in range(4):
            tt(Vb[:, i, :].rearrange("p (c n) -> p c n", c=4),
               vtokv[:, i, h * 128:(h + 1) * 128].unsqueeze(1).broadcast_to([128, 4, 128]),
               mask4.unsqueeze(2).broadcast_to([128, 4, 128]), ALU.mult, [vtok, cF], [Vb])
            yield

    def hg_G(h, B, main, slab, c0):
        rhs = lambda kc: hT[:, kc, :]
        gbank = Ring([mm[0], mm[1]])
        if main:
            bq = gbank.next()
            gemm_b(slab, 16, c0, rhs, [hT], bq); yield
            t_ = tf[5]
            act(t_[:], bq[:, 0:512], AF.Exp, [bq], [t_], scale=-1.0); yield
            act(t_[:], t_[:], AF.Ln, [t_], [t_], bias=1.0); yield
            act(t_[:], t_[:], AF.Exp, [t_], [t_], scale=-1.0); yield
            tt(B["qs"][:], bq[:, 0:512], t_[:], ALU.mult, [bq, t_], [B["qs"]]); yield
            for x in range(2):
                bz = gbank.next()
                gemm_b(slab, 16, c0 + 128 + 128 * x, rhs, [hT], bz); yield
                yield from hg_gates(bz, x, h, True, B)
        else:
            bz = gbank.next()
            gemm_b(slab, 16, c0, rhs, [hT], bz); yield
            yield from hg_gates(bz, 1, h, False, B)
        yield from make_vbd(h, B)
        if main:
            obank = B["obank"]
            for x in range(2):
                bank = gbank.next()

                def f(e, x=x, bank=bank):
                    for i in range(4):
                        ins = e.matmul(bank[:, i * 128:(i + 1) * 128], lhsT=B["kt"][x][:, i * 128:(i + 1) * 128],
                                       rhs=B["qt"][x][:, i * 128:(i + 1) * 128], start=True, stop=True)
                    return ins
                P.op("pe", f, [B["kt"][x], B["qt"][x]], [bank]); yield
                msk = (maskA if x == 0 else maskB).unsqueeze(1).broadcast_to([128, 4, 128])
                tt(B["ATm"][x][:], bank[:, 0:512].rearrange("p (i n) -> p i n", n=128), msk, ALU.mult, [bank, cF], [B["ATm"][x]]); yield

            def f2(e):
                for i in range(4):
                    e.matmul(obank[:, i * 128:(i + 1) * 128], lhsT=B["ATm"][0][:, i, :], rhs=vtokv[:, i, h * 128:(h + 1) * 128],
                             start=(i == 0), stop=False, skip_group_check=True)
                    ins = e.matmul(obank[:, i * 128:(i + 1) * 128], lhsT=B["ATm"][1][:, i, :], rhs=vtokv[:, i, h * 128:(h + 1) * 128],
                                   start=False, stop=False, skip_group_check=True)
                return ins
            P.op("pe", f2, [B["ATm"][0], B["ATm"][1], vtok], [obank]); yield

    def hg_C(h, B, dirs, segs, with_o, pre_blk=None, sample_blk=None, seq0=None):
        obank = B["obank"]
        if pre_blk is not None and pre_blk <= 3:
            P.dma("sp", SB_scr[pre_blk, h, :, :], SX[1][h][:], reads=[SX[1][h]], writes=[SB_scr])
        if sample_blk is not None:
            P.dma("sp", SX[1][h][:], SB_scr[sample_blk, h, :, :], reads=[SB_scr], writes=[SX[1][h]])
            act(SXb[1][h][:], SX[1][h][:], AF.Copy, [SX[1][h]], [SXb[1][h]])
        pbank = {0: mm[2], 1: mm[3]}
        pring = Ring([mm[2], mm[3]])
        for si, seg in enumerate(segs):
            if seq0 is not None:
                for x in dirs:
                    P.op("dve", lambda e, x=x: e.memset(SX[x][h][:], 0.0), [], [SX[x][h]])
                    P.op("dve", lambda e, x=x: e.memset(SXb[x][h][:], 0.0), [], [SXb[x][h]])
            order = {0: [(i, c) for i in seg for c in range(4)], 1: [(i, c) for i in reversed(seg) for c in reversed(range(4))]}
            pb = {}
            nsteps = len(seg) * 4
            for k in range(nsteps):
                for x in dirs:
                    i, c = order[x][k]
                    if k % 4 == 0:
                        bank = pbank[x] if len(dirs) == 2 else pring.next()
                        P.op("pe", lambda e, bank=bank, x=x, i=i: e.matmul(bank[:, 0:512], lhsT=B["kTok"][x][:, i, :], rhs=B["Vbd"][:, i, :],
                                                                          start=True, stop=True), [B["kTok"][x], B["Vbd"]], [bank])
                        pb[x] = bank
                    S, Sb = SX[x][h], SXb[x][h]
                    if with_o:
                        P.op("pe", lambda e, x=x, i=i, c=c, Sb=Sb: e.matmul(obank[32 * c:32 * c + 32, i * 128:(i + 1) * 128],
                                                                          lhsT=B["qt"][x][:, i * 128 + 32 * c:i * 128 + 32 * c + 32], rhs=Sb[:],
                                                                          start=False, stop=True, tile_position=(0, 32 * c), skip_group_check=True),
                             [B["qt"][x], Sb], [obank])
                    bank = pb[x]
                    stt(S[:], S[:], B["ebt"][x][:, i * 4 + c:i * 4 + c + 1], bank[:, c * 128:(c + 1) * 128], ALU.mult, ALU.add,
                        [S, B["ebt"][x], bank], [S])
                    if with_o:
                        act(Sb[:], S[:], AF.Copy, [S], [Sb])
                yield
            if seq0 is not None:
                P.dma("sp", sA_out[seq0 + si, h, :, :], SX[0][h][:], reads=[SX[0][h]], is_output=True)
                P.dma("sp", sB_out[seq0 + si, h, :, :], SX[1][h][:], reads=[SX[1][h]], is_output=True)

    def hg_E(h, B):
        obank = B["obank"]
        t_sq = kvf
        act(t_sq[:], obank[:, 0:512], AF.Square, [obank], [t_sq]); yield
        P.op("dve", lambda e: e.tensor_reduce(out=st["c"][:, 0:4], in_=t_sq[:].rearrange("p (h d) -> p h d", h=4), axis=AX.X, op=ALU.add),
             [t_sq], [st["c"]]); yield
        rstd_of(st["c"][:, 0:4], 128, st["c"][:, 8:12], st["c"][:, 4:8], [st["c"]], [st["c"], st["c"]]); yield
        v4 = lambda ap: ap.rearrange("p (h d) -> p h d", h=4)
        tt(v4(t_sq[:]), v4(obank[:, 0:512]), st["c"][:, 8:12].unsqueeze(2).broadcast_to([128, 4, 128]), ALU.mult, [obank, st["c"]], [t_sq]); yield
        tt(v4(t_sq[:]), v4(t_sq[:]), rowb[:, 2, :].unsqueeze(1).broadcast_to([128, 4, 128]), ALU.mult, [t_sq, rowb], [t_sq]); yield
        tt(v4(qb16[:]), v4(t_sq[:]), sogv[:, :, h * 128:(h + 1) * 128], ALU.mult, [t_sq, sog], [qb16]); yield
        tbk = transposes(qb16, lambda k: qb16[:, k * 128:(k + 1) * 128], 4); yield
        act(hgTv[:, h, :], tbk[:, 0:512], AF.Copy, [tbk], [hgT]); yield

    def drain(g):
        if g is not None:
            for _ in g:
                pass

    def pipeline_heads(makeG, makeC, doE, ratio):
        g = makeG(0)
        drain(g)
        pe_ = None
        for h in range(8):
            c = makeC(h)
            gn = makeG(h + 1) if h < 7 else None
            for _ in c:
                if pe_ is not None and next(pe_, "end") == "end":
                    pe_ = None
                if gn is not None:
                    for _k in range(ratio):
                        if next(gn, "end") == "end":
                            gn = None
                            break
            drain(pe_)
            drain(gn)
            pe_ = doE(h) if doE is not None else None
        drain(pe_)

    def hgrn_main(segs, sample, blk, seq0):
        P.handoff([QT], HGP_QT)
        for s in range(2):
            slab = wload(w_in, 0, 16, C_VH + s * 512, 512)
            for i in range(4):
                bank = mmr.next()
                gemm_a(slab, 16, lambda kc, i=i: hT[:, kc, i * 128:(i + 1) * 128], [hT], bank)
                act(vtokv[:, i, s * 512:(s + 1) * 512], bank[:, 0:512], AF.Copy, [bank], [vtok])
        for s in range(2):
            slab = wload(w_in, 0, 16, C_OG + s * 512, 512)
            for i in range(4):
                bank = mmr.next()
                gemm_a(slab, 16, lambda kc, i=i: hT[:, kc, i * 128:(i + 1) * 128], [hT], bank)
                t_ = tfr.next()
                sigmoid_inplace_from(t_, t_[:], bank[:, 0:512], [bank])
                tt(sogv[:, i, s * 512:(s + 1) * 512], bank[:, 0:512], t_[:], ALU.mult, [bank, t_], [sog])

        def makeG(h):
            slab = wload_multi([(w_in, 0, C_QH + h * 128, 128, 0), (w_in, 0, C_ZA + h * 128, 128, 128),
                                (w_in, 0, C_ZB + h * 128, 128, 256)], 16, 384)
            return hg_G(h, HB[h % 2], True, slab, 0)

        def makeC(h):
            return hg_C(h, HB[h % 2], [0, 1], segs, True, sample_blk=(blk if sample else None), seq0=(None if sample else seq0))
        pipeline_heads(makeG, makeC, lambda h: hg_E(h, HB[h % 2]), 4)

    def hgrn_pre(blk):
        for s_ in range(2):
            slab = wload(w_in, 0, 16, C_VH + s_ * 512, 512)
            for i in range(4):
                bank = mmr.next()
                gemm_a(slab, 16, lambda kc, i=i: hT[:, kc, i * 128:(i + 1) * 128], [hT], bank)
                act(vtokv[:, i, s_ * 512:(s_ + 1) * 512], bank[:, 0:512], AF.Copy, [bank], [vtok])
        gbank = Ring([mm[0], mm[1]])
        zslab, zb = {}, {}

        def issue(h):
            hs = h // 4
            if hs not in zslab:
                zslab[hs] = wload(w_in, 0, 16, C_ZB + hs * 512, 512)
            zb[h] = gbank.next()
            gemm_b(zslab[hs], 16, (h % 4) * 128, lambda kc: hT[:, kc, :], [hT], zb[h])
        def head(h):
            B = HB[h % 2]
            c = 8 + h
            if blk <= 3:
                P.dma("sp", SB_scr[blk, h, :, :], SX[1][h][:], reads=[SX[1][h]], writes=[SB_scr])
            if h % 2 == 1:
                e0, s1, lf = tf[3], tf[4], tf[5]
                bc = tf[3]
            else:
                e0, s1, lf = tf[0], tf[1], tf[2]
                bc = tf[0]
            zbank = zb[h]
            act(e0[:], zbank[:, 0:512], AF.Exp, [zbank], [e0])
            if h + 1 < 8:
                issue(h + 1)
            yield
            act(e0[:], e0[:], AF.Ln, [e0], [e0], bias=1.0); yield
            act(s1[:], e0[:], AF.Exp, [e0], [s1], scale=-1.0); yield
            act(lf[:], s1[:], AF.Ln, [s1, noml], [lf], scale=noml[:, c:c + 1], bias=1.0); yield
            P.op("dve", lambda e, bc=bc, lf=lf: e.tensor_tensor_scan(out=bc[:], data0=ones512, data1=lf[:], initial=0.0, op0=ALU.mult, op1=ALU.add),
                 [cF, lf], [bc]); yield
            act(B["ebt"][1][:, 0:1], bc[:, 511:512], AF.Exp, [bc], [B["ebt"][1]]); yield
            tt(lf[:], bc[:], lf[:], ALU.subtract, [bc, lf], [lf]); yield
            act(lf[:], lf[:], AF.Exp, [lf], [lf]); yield
            kh_ = B["kh"][1]
            stt(kh_[:], s1[:], oml[:, c:c + 1], lf[:], ALU.mult, ALU.mult, [s1, oml, lf], [kh_]); yield
            tbk = transposes(kh_, lambda k, kh_=kh_: kh_[:, k * 128:(k + 1) * 128], 4); yield
            kT_ = B["kTok"][1]
            act(kT_[:].rearrange("p i n -> p (i n)"), tbk[:, 0:512], AF.Copy, [tbk], [kT_]); yield
            pbk = mm[2 + h % 2]

            def f(e, kT_=kT_, pbk=pbk, h=h):
                for i in range(4):
                    ins = e.matmul(pbk[:, 0:128], lhsT=kT_[:, i, :], rhs=vtokv[:, i, h * 128:(h + 1) * 128], start=(i == 0), stop=(i == 3))
                return ins
            P.op("pe", f, [kT_, vtok], [pbk]); yield
            S = SX[1][h]
            stt(S[:], S[:], B["ebt"][1][:, 0:1], pbk[:, 0:128], ALU.mult, ALU.add, [S, B["ebt"][1], pbk], [S]); yield

        issue(0)
        SKEW = 6
        gens = [head(h) for h in range(8)]
        cur, nxt_, started = gens[0], None, 1
        steps = 0
        while cur is not None:
            alive = next(cur, "end") != "end"
            steps += 1
            if nxt_ is None and started < 8 and steps >= SKEW:
                nxt_ = gens[started]
                started += 1
            if nxt_ is not None and next(nxt_, "end") == "end":
                nxt_ = None
            if not alive:
                cur, nxt_ = nxt_, None
                steps = SKEW
                if cur is None and started < 8:
                    cur = gens[started]
                    started += 1
                    steps = 0

    def merge_block():
        P.handoff([QT, vtok] + HGP_QT, [mT])
        Wga = wload(w_in, 0, 16, C_GA, 512)
        for cb in range(4):
            Wa = wload(w_ba, 0, 8, cb * 512, 512)
            for m in range(4):
                b3 = mmr.next()
                gemm_b(Wga, 16, m * 128, lambda kc: hT[:, kc, :], [hT], b3)
                sigmoid_inplace_from(tf[m], tf[m][:], b3[:, 0:512], [b3])
            Wgb = wload(w_in, 0, 16, C_GB + cb * 512, 512)
            for m in range(4):
                b1 = mmr.next()
                gemm_b(Wa, 8, m * 128, lambda kc: attTv[:, kc, :], [attT], b1)
                tt(tf[m][:], b1[:, 0:512], tf[m][:], ALU.mult, [b1, tf[m]], [tf[m]])
            Wh = wload(w_bh, 0, 8, cb * 512, 512)
            for m in range(4):
                b4 = mmr.next()
                gemm_b(Wgb, 16, m * 128, lambda kc: hT[:, kc, :], [hT], b4)
                sigmoid_inplace_from(tf[4], tf[4][:], b4[:, 0:512], [b4])
                b2 = mmr.next()
                gemm_b(Wh, 8, m * 128, lambda kc: hgTv[:, kc, :], [hgT], b2)
                tt(tf[5][:], b2[:, 0:512], tf[4][:], ALU.mult, [b2, tf[4]], [tf[5]])
                tt(mTv[:, cb * 4 + m, :], tf[5][:], tf[m][:], ALU.add, [tf[5], tf[m]], [], pw=[mT])
                if m == 1 and cb + 1 < 4:
                    pass
            if cb + 1 < 4:
                Wga = wload(w_in, 0, 16, C_GA + (cb + 1) * 512, 512)

    def proj_out_stage(Wd, kparts, lhs, lhs_tiles, scr, hook=None):
        for cb in range(4):
            if hook is not None:
                hook(cb)
            for pi, (k0, kcn) in enumerate(kparts):
                slab = wload(Wd, k0 * 128, kcn, cb * 512, 512)
                for i in range(4):
                    gemm_a(slab, kcn, lambda kc, i=i: lhs(kc, i), lhs_tiles, mm[i], k0=k0, first=(pi == 0), last=(pi == len(kparts) - 1))
            for i in range(4):
                t_ = tfr.next()
                act(t_[:], mm[i][:, 0:512], AF.Copy, [mm[i]], [t_])
                P.dma("sp", scr[i * 128:(i + 1) * 128, cb * 512:(cb + 1) * 512], t_[:], reads=[t_], writes=[scr])
                act(sqj[:], mm[i][:, 0:512], AF.Square, [mm[i]], [], pw=[st["ssq"]], accum_out=st["ssq"][:, i * 4 + cb:i * 4 + cb + 1])

    def tile_rstd(i):
        P.op("dve", lambda e: e.tensor_reduce(out=st["d"][:, 0:1], in_=st["ssq"][:, i * 4:(i + 1) * 4], axis=AX.X, op=ALU.add), [st["ssq"]], [st["d"]])
        rstd_of(st["d"][:, 0:1], D, st["d"][:, 2:3], st["d"][:, 1:2], [st["d"]], [st["d"], st["d"]])
        return st["d"][:, 2:3]

    def block_tail(row0, g, par, yrow0, nxt=None):
        proj_out_stage(w_out, [(0, 16)], lambda kc, i: mTv[:, kc, i * 128:(i + 1) * 128], [mT], mo_scr[par])
        for i in range(4):
            r = tile_rstd(i)
            mo_t, xt = xtr.next(), xtr.next()
            P.dma("sp", mo_t[:], mo_scr[par][i * 128:(i + 1) * 128, :], reads=[mo_scr[par]], writes=[mo_t])
            P.dma("sp", xt[:], x_all[row0 + i * 128:row0 + (i + 1) * 128, :], writes=[xt])
            stt(mo_t[:], mo_t[:], r, C1b[:], ALU.mult, ALU.mult, [mo_t, st["d"], C1b], [mo_t])
            tt(xt[:], xt[:], mo_t[:], ALU.add, [xt, mo_t], [xt])
            P.dma("sp", x1_scr[par][i * 128:(i + 1) * 128, :], xt[:], reads=[xt], writes=[x1_scr[par]])
            norm_tile(xt, i, g, AT2, BT2)
        P.handoff([QT, vtok, mT, attT, sog, hgT] + HGP_QT + HGP_SP, [fT])
        Wg = wload(w_fi, 0, 16, 0, 512)
        for jb in range(11):
            Wu = wload(w_fi, 0, 16, DFF + jb * 512, 512)
            for m in range(4):
                bg = mmr.next()
                gemm_b(Wg, 16, m * 128, lambda kc: hT[:, kc, :], [hT], bg)
                sigmoid_inplace_from(tf[m], tf[m][:], bg[:, 0:512], [bg])
                tt(tf[m][:], bg[:, 0:512], tf[m][:], ALU.mult, [bg, tf[m]], [tf[m]])
            if jb + 1 < 11:
                Wg = wload(w_fi, 0, 16, (jb + 1) * 512, 512)
            for m in range(4):
                bu = mmr.next()
                gemm_b(Wu, 16, m * 128, lambda kc: hT[:, kc, :], [hT], bu)
                tt(fTv[:, jb * 4 + m, :], tf[m][:], bu[:, 0:512], ALU.mult, [tf[m], bu], [], pw=[fT])
        pend_xs = {}

        def hook(cb):
            if nxt is None:
                return
            if cb >= 1:
                norm_B(pend_xs[cb - 1], cb - 1, nxt[1], AT1, BT1)
            pend_xs[cb] = norm_one_A(nxt[0], cb)
        proj_out_stage(w_fo, [(0, 16), (16, 16), (32, 12)], lambda kc, i: fTv[:, kc, i * 128:(i + 1) * 128], [fT], mo_scr[par], hook)
        if nxt is not None:
            norm_B(pend_xs[3], 3, nxt[1], AT1, BT1)
        for i in range(4):
            r = tile_rstd(i)
            fo_t, x1t = xtr.next(), xtr.next()
            P.dma("sp", fo_t[:], mo_scr[par][i * 128:(i + 1) * 128, :], reads=[mo_scr[par]], writes=[fo_t])
            P.dma("sp", x1t[:], x1_scr[par][i * 128:(i + 1) * 128, :], reads=[x1_scr[par]], writes=[x1t])
            stt(fo_t[:], fo_t[:], r, C2b[:], ALU.mult, ALU.mult, [fo_t, st["d"], C2b], [fo_t])
            tt(x1t[:], x1t[:], fo_t[:], ALU.add, [x1t, fo_t], [x1t])
            P.dma("sp", y_own[yrow0 + i * 128:yrow0 + (i + 1) * 128, :], x1t[:], reads=[x1t], is_output=True)
        P.handoff([fT], [QT, vtok, attT, sog, hgT, mT] + HGP_QT + HGP_SP)

    def _program():
        ck("ada")
        for t in range(2):
            xt = xtr.next()
            P.dma("sp", xt[:, 0:256], cache_k[t * 128:(t + 1) * 128, :], writes=[xt])
            P.dma("sp", xt[:, 256:512], cache_v[t * 128:(t + 1) * 128, :], writes=[xt])
            kb = ptr.next()
            P.op("dve", lambda e, kb=kb, xt=xt: e.tensor_copy(out=kb[:], in_=xt[:, 0:512]), [xt], [kb])
            tbk = transposes(kb, lambda k, kb=kb: kb[:, k * 128:(k + 1) * 128], 2)
            kch = kchr.next()
            act(kch[:].rearrange("p g n -> p (g n)"), tbk[:, 0:256], AF.Copy, [tbk], [kch])
            t0_ = 4096 + t * 128
            P.dma("sp", KT_scr[:, :, t0_:t0_ + 128].rearrange("g d n -> d g n"), kch[:], reads=[kch], writes=[KT_scr])
            P.dma("sp", V_scr[t0_:t0_ + 128, :], kb[:, 256:512], reads=[kb], writes=[V_scr])

        ck("cache")
        import os
        _pb = [int(v) for v in os.environ.get("DBG_PRE_BLOCKS", "7,6,5,4,3,2,1,0").split(",")]
        _phg = os.environ.get("DBG_PRE_HG", "1") == "1"
        _pkv = os.environ.get("DBG_PRE_KV", "1") == "1"
        _ada_done = set()
        for blk in _pb:
            row0 = ROW_S + blk * 512
            load_norm_block(row0, 1)
            ck("pre_norm")
            P.dma("sp", ropet[:], rope_d[blk * 512:(blk + 1) * 512].rearrange("(i p) c a d -> p i (c a d)", p=128), writes=[ropet])
            if _pkv:
                kv_stage(True, blk * 512, 0)
            ck("pre_kv")
            if blk >= 1 and _phg:
                hgrn_pre(blk)
                ck("pre_hg")
            else:
                for h in range(8):
                    P.dma("sp", SB_scr[0, h, :, :], SX[1][h][:], reads=[SX[1][h]], writes=[SB_scr])
            _nx = [cb for cb in range(8, 24) if cb not in _ada_done][:2]
            _ada_done.update(_nx)
            ada_slabs(_nx)
            ck(f"pre_blk{blk}")

        ada_slabs([cb for cb in range(8, 24) if cb not in _ada_done])
        ada_rest()
        ck("pre")
        build_Cb(0)
        for pb_ in range(NPB):
            row0 = pb_ * 512
            if pb_ == 0 or stop_after is not None:
                load_norm_block(row0, 0)
            ck("p_norm")
            q_stage(False)
            ck("p_q")
            kv_stage(False, 0, row0)
            ck("p_kv")
            for si, seg in enumerate([[0, 1], [2, 3]]):
                attention(seg, 2, lambda j, g, si=si: (KTp[:, g, (si * 2 + j) * 128:(si * 2 + j + 1) * 128],
                                                        Vp[:, si * 2 + j, g * 128:(g + 1) * 128], [KTp, Vp]))
            if pb_ == 0:
                dump("d_attT", arena[:, 8192:12288], [128, 4096], [attT], BF16)
                dump("d_QT", arena[:, 0:4096], [128, 4096], [QT], BF16)
            ck("p_att")
            hgrn_main([[0, 1], [2, 3]], False, 0, pb_ * 2)
            if pb_ == 0:
                dump("d_hgT", arena[:, 16384:20480], [128, 4096], [hgT], BF16)
            ck("p_hg")
            merge_block()
            if pb_ == 0:
                dump("d_mT", arena[:, 0:8192], [128, 8192], [mT], BF16)
            ck("p_merge")
            block_tail(row0, 0, pb_ % 2, row0, None if stop_after is not None else ((row0 + 512, 0) if pb_ + 1 < NPB else (ROW_S, 1)))
            if pb_ == 0:
                dump("d_x1", x1_scr[0][:, :], [512, 2048], [x1_scr[0]])
                dump("d_fo", mo_scr[0][:, :], [512, 2048], [mo_scr[0]])
                dump("d_fT", arena[:, 0:22528], [128, 22528], [fT], BF16)
            ck("p_tail")

        ck("prompt")
        build_Cb(1)
        for h in range(8):
            P.dma("sp", SX[0][h][:], stA0[h, :, :], writes=[SX[0][h]])
            act(SXb[0][h][:], SX[0][h][:], AF.Copy, [SX[0][h]], [SXb[0][h]])
        for blk in range(NSB):
            row0 = ROW_S + blk * 512
            if stop_after is not None:
                load_norm_block(row0, 1)
            P.dma("sp", ropet[:], rope_d[blk * 512:(blk + 1) * 512].rearrange("(i p) c a d -> p i (c a d)", p=128), writes=[ropet])
            q_stage(True)
            ck("s_q")

            def chunk_src(j, g):
                kch, vch = kqr.next(), vqr.next()
                P.dma("sp", kch[:], KT_scr[g, :, j * 128:(j + 1) * 128], reads=[KT_scr], writes=[kch])
                P.dma("sp", vch[:], V_scr[j * 128:(j + 1) * 128, g * 128:(g + 1) * 128], reads=[V_scr], writes=[vch])
                return kch[:], vch[:], [kch, vch]
            attention([0, 1, 2, 3], 34, chunk_src)
            ck("s_att")
            hgrn_main([[0, 1, 2, 3]], True, blk, 0)
            ck("s_hg")
            merge_block()
            block_tail(row0, 1, blk % 2, 1024 + blk * 512, None if stop_after is not None else ((row0 + 512, 1) if blk + 1 < NSB else None))
            ck("s_tail")


    try:
        _program()
    except _Stop:
        pass
    P.emit()
    return nc


def _host_inputs(inp, cores=range(8)):
    f32 = np.float32
    x_prompt, x_sample = inp["x_prompt"], inp["x_sample"]
    w_in0 = inp["w_in"][0]
    half = 32
    inv = (10000.0 ** (-np.arange(half, dtype=f32) / half)).astype(f32)
    pos = np.arange(4096)
    r = (pos // 64).astype(f32)
    c = (pos % 64).astype(f32)
    ang = np.stack([r[:, None] * inv[None, :], c[:, None] * inv[None, :]], axis=1).astype(f32)
    rope_g = np.stack([np.cos(ang), np.sin(ang)], axis=1).astype(f32)
    s_idx = np.arange(128)
    same = (s_idx[:, None] // 32) == (s_idx[None, :] // 32)
    maskA = (same & (s_idx[:, None] <= s_idx[None, :])).astype(f32)
    maskB = (same & (s_idx[:, None] >= s_idx[None, :])).astype(f32)
    resetm = np.ones((128, 512), f32)
    resetm[:, ::32] = 0
    mask4 = (s_idx[:, None] // 32 == np.arange(4)[None, :]).astype(f32)
    cF = np.concatenate([np.eye(128, dtype=f32), maskA, maskB, resetm, mask4, np.ones((128, 512), f32)], axis=1)

    def fm(v, n):
        return np.ascontiguousarray(v.reshape(n, 128).T)

    cols = lambda a, b: w_in0[:, a:b]
    w_in_sw = np.ascontiguousarray(np.concatenate([cols(0, C_ZA), cols(C_ZB, C_VH), cols(C_ZA, C_ZB), cols(C_VH, NIN)], axis=1))
    rowb = np.ascontiguousarray(np.broadcast_to(np.stack([inp["q_norm"][0], inp["k_norm"][0], inp["hg_norm"][0]])[None], (128, 3, 128))).astype(f32)
    maps = []
    for j in cores:
        b, hf = j // 2, j % 2
        flip = hf == 1
        xp = x_prompt[4 * j:4 * j + 4]
        xs = x_sample[b]
        rp = rope_g
        if flip:
            xp = xp[:, ::-1]
            xs = xs[::-1]
            rp = rope_g[::-1]
        x_all = np.ascontiguousarray(np.concatenate([xp.reshape(1024, D), xs], axis=0))
        lbA, lbB = (inp["lb_bwd"], inp["lb_fwd"]) if flip else (inp["lb_fwd"], inp["lb_bwd"])
        vecT = np.concatenate([fm(inp["b_ada"][0], 96), fm(inp["norm_pre_mix"][0], 16), fm(inp["norm_post_mix"][0], 16),
                               fm(inp["norm_pre_ffn"][0], 16), fm(inp["norm_post_ffn"][0], 16),
                               fm(lbA[0], 8), fm(lbA[1], 8), fm(lbB[0], 8), fm(lbB[1], 8)], axis=1).astype(f32)
        condT = np.stack([fm(inp["c_ctx"], 16), fm(inp["c"][b], 16)], axis=2).astype(f32)
        stA, stB = (inp["state_bwd"], inp["state_fwd"]) if flip else (inp["state_fwd"], inp["state_bwd"])
        maps.append({
            "x_all": x_all, "rope": np.ascontiguousarray(rp),
            "cache_k": np.ascontiguousarray(inp["cache_k"][b, 0].reshape(256, 256)),
            "cache_v": np.ascontiguousarray(inp["cache_v"][b, 0].reshape(256, 256)),
            "stA0": np.ascontiguousarray(stA[b, 0]), "stB0": np.ascontiguousarray(stB[b, 0]),
            "condT": np.ascontiguousarray(condT), "vecT": np.ascontiguousarray(vecT), "rowb": rowb, "cF": cF,
            "w_ada": inp["w_ada"][0], "w_in": w_in_sw if flip else w_in0,
            "w_br_att": inp["w_br_att"][0], "w_br_hg": inp["w_br_hg"][0], "w_out": inp["w_out"][0],
            "w_ffn_in": inp["w_ffn_in"][0], "w_ffn_out": inp["w_ffn_out"][0],
        })
    return maps


def _assemble(res):
    f32 = np.float32
    y_prompt = np.zeros((32, 256, D), f32)
    y_sample = np.zeros((4, 4096, D), f32)
    nk = np.zeros((32, 1, 256, 2, 128), f32)
    nv = np.zeros((32, 1, 256, 2, 128), f32)
    sf = np.zeros((32, 1, 8, 128, 128), f32)
    sb = np.zeros((32, 1, 8, 128, 128), f32)
    for j in range(8):
        r = res[j]
        b, hf = j // 2, j % 2
        flip = hf == 1
        yp = r["y_own"][0:1024].reshape(4, 256, D)
        ys = r["y_own"][1024:3072]
        kc = r["kc_out"].reshape(4, 256, 2, 128)
        vc = r["vc_out"].reshape(4, 256, 2, 128)
        sA, sB = r["sA_out"], r["sB_out"]
        if flip:
            yp, ys, kc, vc = yp[:, ::-1], ys[::-1], kc[:, ::-1], vc[:, ::-1]
            sA, sB = sB, sA
        y_prompt[4 * j:4 * j + 4] = yp
        y_sample[b, hf * 2048:(hf + 1) * 2048] = ys
        nk[4 * j:4 * j + 4, 0] = kc
        nv[4 * j:4 * j + 4, 0] = vc
        sf[4 * j:4 * j + 4, 0] = sA
        sb[4 * j:4 * j + 4, 0] = sB
    return (y_prompt, y_sample, nk, nv, sf, sb)


def kernel(**inputs):
    inp = {k: np.asarray(v) for k, v in inputs.items()}
    nc = build()
    maps = _host_inputs(inp)
    res = run_bass_kernel_spmd(nc, maps, core_ids=list(range(8)))
    return _assemble(res.results)
```

```python
import numpy as np
from contextlib import ExitStack
import concourse.bass as bass
import concourse.mybir as mybir
from concourse.bass_utils import run_bass_kernel_spmd

F32 = mybir.dt.float32
BF16 = mybir.dt.bfloat16
AF = mybir.ActivationFunctionType
ALU = mybir.AluOpType
AX = mybir.AxisListType

D = 2048
DFF = 5632
NIN = 10752
EPS = 1e-6
C_QA, C_KA, C_VA, C_QH, C_ZA, C_ZB, C_VH, C_OG, C_GA, C_GB = 0, 1024, 1280, 1536, 2560, 3584, 4608, 5632, 6656, 8704
NPB = 2
NSB = 4
ROW_S = 1024

ENGS = ("pe", "act", "dve", "pool", "sp")
NRING = {"sp": 12, "pool": 6}


class T:
    def __init__(self, h, name=""):
        self.h = h
        self.name = name
        self.w = {}
        self.r = {}
        self.psum = False
        self.pw_open = False

    def __getitem__(self, idx):
        return self.h[idx]


class Prog:
    def __init__(self, nc):
        self.nc = nc
        self.q = {e: [] for e in ENGS}
        self.cnt = {e: 0 for e in ENGS}
        self.known = {e: {} for e in ENGS}
        self.dma_i = {"sp": 0, "pool": 0}
        self.final = {}
        self.stack = ExitStack()
        self.nalloc = 0

    def sb(self, shape, dt, name=None):
        self.nalloc += 1
        name = "b_" + (name or f"sb{self.nalloc}")
        h = self.stack.enter_context(self.nc.sbuf_tensor(name, list(shape), dt))
        return T(h, name)

    def ps(self, shape, dt, name=None):
        self.nalloc += 1
        name = "p_" + (name or f"ps{self.nalloc}")
        h = self.stack.enter_context(self.nc.psum_tensor(name, list(shape), dt))
        t = T(h, name)
        t.psum = True
        return t

    def view(self, t, name=""):
        return T(t.h, name)

    def _need(self, eng, reads, writes, pwrites=()):
        need = {}

        def add(ev):
            k, v = ev
            if need.get(k, 0) < v:
                need[k] = v

        for t in reads:
            for ev in t.w.items():
                add(ev)
            if t.psum:
                for k, v in t.r.items():
                    if k != eng:
                        add((k, v))
        for t in writes:
            for ev in t.w.items():
                add(ev)
            for ev in t.r.items():
                add(ev)
        for t in pwrites:
            if t.r or not t.pw_open:
                for ev in t.w.items():
                    add(ev)
            for ev in t.r.items():
                add(ev)
        out = []
        for k, v in need.items():
            if k == eng and eng == "pe":
                continue
            if self.known[eng].get(k, 0) >= v:
                continue
            self.known[eng][k] = v
            out.append((k, v))
        return out

    def _mark(self, ev, reads, writes, pwrites=()):
        k, v = ev
        for t in reads:
            if t.r.get(k, 0) < v:
                t.r[k] = v
        for t in writes:
            t.w = {k: v}
            t.r = {}
            t.pw_open = False
        for t in pwrites:
            if t.r or not t.pw_open:
                t.w = {}
                t.r = {}
                t.pw_open = True
            if t.w.get(k, 0) < v:
                t.w[k] = v

    def op(self, eng, fn, reads=(), writes=(), pwrites=()):
        for k, v in self._need(eng, reads, writes, pwrites):
            self.q[eng].append(("wait", k, v))
        self.cnt[eng] += 1
        self.q[eng].append(("op", fn))
        self._mark((eng, self.cnt[eng]), reads, writes, pwrites)

    def dma(self, queue, out_ap, in_ap, reads=(), writes=(), is_output=False):
        for k, v in self._need(queue, reads, writes):
            self.q[queue].append(("wait", k, v))
        i = self.dma_i[queue]
        self.dma_i[queue] += 1
        slot, gen = i % NRING[queue], i // NRING[queue]
        key = (queue, slot)
        if gen > 0 and self.known[queue].get(key, 0) < 16 * gen:
            self.q[queue].append(("wait", key, 16 * gen))
            self.known[queue][key] = 16 * gen
        self.q[queue].append(("dma", out_ap, in_ap, key))
        ev = (key, 16 * (gen + 1))
        self._mark(ev, reads, writes)
        if is_output:
            self.final[key] = max(self.final.get(key, 0), ev[1])

    def handoff(self, src, dst):
        for d in dst:
            for s_ in src:
                evs = list(s_.r.items()) + list(s_.w.items())
                for k, v in evs:
                    if d.r.get(k, 0) < v:
                        d.r[k] = v

    def emit(self):
        nc = self.nc
        for key, v in self.final.items():
            if self.known["sp"].get(key, 0) < v:
                self.q["sp"].append(("wait", key, v))
        keys = set()
        for e in ENGS:
            for it in self.q[e]:
                if it[0] == "wait":
                    keys.add(it[1])
                elif it[0] == "dma":
                    keys.add(it[3])
            keys.add(e)
        sems = {}
        for k in sorted(keys, key=str):
            nm = "s_" + (k if isinstance(k, str) else f"{k[0]}{k[1]}")
            sems[k] = self.stack.enter_context(nc.semaphore(nm))
        engobj = {"pe": "tensor", "act": "scalar", "dve": "vector", "pool": "gpsimd", "sp": "sync"}

        def run(e, eng):
            for it in self.q[e]:
                if it[0] == "wait":
                    eng.wait_ge(sems[it[1]], it[2])
                elif it[0] == "op":
                    it[1](eng).then_inc(sems[e], 1)
                else:
                    eng.dma_start(out=it[1], in_=it[2]).then_inc(sems[it[3]], 16)

        with nc.Block() as block:
            for e in ENGS:
                if not self.q[e]:
                    continue
                getattr(block, engobj[e])(lambda eng, e=e: run(e, eng))
        self.stack.close()


class Ring:
    def __init__(self, items):
        self.items = items
        self.i = 0

    def next(self):
        t = self.items[self.i % len(self.items)]
        self.i += 1
        return t


class _Stop(Exception):
    pass


def build(stop_after=None, debug=False):
    nc = bass.Bass("TRN2", target_bir_lowering=False)
    P = Prog(nc)

    def ck(name):
        if stop_after == name:
            raise _Stop()

    def din(name, shape, dt=F32):
        return nc.dram_tensor(name, list(shape), dt, kind="ExternalInput").ap()

    def dout(name, shape, dt=F32):
        return nc.dram_tensor(name, list(shape), dt, kind="ExternalOutput").ap()

    def dscr(name, shape, dt):
        return T(nc.dram_tensor(name, list(shape), dt, kind="Internal").ap(), name)

    x_all = din("x_all", [5120, D])
    rope_d = din("rope", [4096, 2, 2, 32])
    cache_k = din("cache_k", [256, 256])
    cache_v = din("cache_v", [256, 256])
    stA0 = din("stA0", [8, 128, 128])
    stB0 = din("stB0", [8, 128, 128])
    condT_d = din("condT", [128, 16, 2])
    vecT_d = din("vecT", [128, 192])
    rowb_d = din("rowb", [128, 3, 128])
    cF_d = din("cF", [128, 1412])
    w_ada = din("w_ada", [D, 6 * D])
    w_in = din("w_in", [D, NIN])
    w_ba = din("w_br_att", [1024, D])
    w_bh = din("w_br_hg", [1024, D])
    w_out = din("w_out", [D, D])
    w_fi = din("w_ffn_in", [D, 2 * DFF])
    w_fo = din("w_ffn_out", [DFF, D])

    y_own = dout("y_own", [3072, D])
    kc_out = dout("kc_out", [1024, 256])
    vc_out = dout("vc_out", [1024, 256])
    sA_out = dout("sA_out", [4, 8, 128, 128])
    sB_out = dout("sB_out", [4, 8, 128, 128])

    KT_scr = dscr("KT_scr", [2, 128, 4352], BF16)
    V_scr = dscr("V_scr", [4352, 256], BF16)
    SB_scr = dscr("SB_scr", [4, 8, 128, 128], F32)
    x1_scr = [dscr(f"x1_scr{i}", [512, D], F32) for i in range(2)]
    mo_scr = [dscr(f"mo_scr{i}", [512, D], F32) for i in range(2)]

    def dump(name, src_ap, shape, reads, dt=F32):
        if not debug:
            return
        d = nc.dram_tensor(name, list(shape), F32, kind="ExternalOutput").ap()
        P.dma("pool" if dt != F32 else "sp", d, src_ap, reads=reads, is_output=True)

    cF = P.sb([128, 1412], F32, "cF")
    identf = cF[:, 0:128]
    maskA = cF[:, 128:256]
    maskB = cF[:, 256:384]
    resetm = cF[:, 384:896]
    mask4 = cF[:, 896:900]
    ones512 = cF[:, 900:1412]
    identb = P.sb([128, 128], BF16, "identb")
    onesb = P.sb([128, 128], BF16, "onesb")
    onesf = P.sb([128, 128], F32, "onesf")
    rowb = P.sb([128, 3, 128], F32, "rowb")
    vecT = P.sb([128, 192], F32, "vecT")
    condT = P.sb([128, 32], F32, "condT")
    scT = P.sb([128, 16, 2], BF16, "scT")
    modT = P.sb([128, 96, 2], F32, "modT")
    AT1 = P.sb([128, 16, 2], F32, "AT1")
    AT2 = P.sb([128, 16, 2], F32, "AT2")
    GT1 = P.sb([128, 16, 2], F32, "GT1")
    GT2 = P.sb([128, 16, 2], F32, "GT2")
    oml = P.sb([128, 16], F32, "oml")
    noml = P.sb([128, 16], F32, "noml")
    C1b = P.sb([128, D], F32, "C1b")
    C2b = P.sb([128, D], F32, "C2b")
    dgt = Ring([P.sb([128, 128], F32, f"dgt{i}") for i in range(2)])

    slabs = Ring([P.sb([128, 16, 512], BF16, f"slab{i}") for i in range(2)])
    hT = P.sb([128, 16, 512], BF16, "hT")
    xtr = Ring([P.sb([128, D], F32, f"xt{i}") for i in range(2)])
    xsr = Ring([P.sb([128, D], BF16, f"xs{i}") for i in range(1)])
    sqj = P.sb([128, 512], BF16, "sqj")
    arena = P.sb([128, 22528], BF16, "arena")
    QT = P.view(arena, "QT")
    vtok = P.view(arena, "vtok")
    mT = P.view(arena, "mT")
    attT = P.view(arena, "attT")
    sog = P.view(arena, "sog")
    hgT = P.view(arena, "hgT")
    fT = P.view(arena, "fT")
    QTv = arena[:, 0:4096].rearrange("p (g i n) -> p g i n", g=2, i=4)
    vtokv = arena[:, 4096:8192].rearrange("p (i n) -> p i n", i=4)
    mTv = arena[:, 0:8192].rearrange("p (c n) -> p c n", c=16)
    attTv = arena[:, 8192:12288].rearrange("p (c n) -> p c n", c=8)
    sogv = arena[:, 12288:16384].rearrange("p (i n) -> p i n", i=4)
    hgTv = arena[:, 16384:20480].rearrange("p (c n) -> p c n", c=8)
    fTv = arena[:, 0:22528].rearrange("p (c n) -> p c n", c=44)

    tf = [P.sb([128, 512], F32, f"tf{i}") for i in range(6)]
    tfr = Ring(tf)
    qs = P.sb([128, 512], BF16, "qs")
    qt = [P.sb([128, 512], BF16, f"qt{i}") for i in range(2)]
    kt = [P.sb([128, 512], BF16, f"kt{i}") for i in range(2)]
    kh = [P.sb([128, 512], BF16, f"kh{i}") for i in range(2)]
    qb16 = P.sb([128, 512], BF16, "qb16")
    kTok = [P.sb([128, 4, 128], BF16, f"kTok{i}") for i in range(2)]
    ATm = [P.sb([128, 4, 128], BF16, f"ATm{i}") for i in range(2)]
    Vbd = P.sb([128, 4, 512], BF16, "Vbd")
    ebt = [P.sb([128, 16], F32, f"ebt{i}") for i in range(2)]
    ebt1 = [P.sb([128, 16], F32, f"ebtp{i}") for i in range(2)]
    Vbd1 = P.sb([128, 4, 512], BF16, "Vbd1")
    _r = lambda k: arena[:, k * 512:(k + 1) * 512]
    hqs1 = T(_r(0), "hqs1")
    hqt1 = [T(_r(1), "hqt10"), T(_r(2), "hqt11")]
    hkt1 = [T(_r(3), "hkt10"), T(_r(4), "hkt11")]
    hkh1 = [T(_r(5), "hkh10"), T(_r(6), "hkh11")]
    _r3 = lambda k: arena[:, 20480 + k * 512:20480 + (k + 1) * 512].rearrange("p (i n) -> p i n", i=4)
    hkTok1 = [T(_r3(0), "hkTok10"), T(_r3(1), "hkTok11")]
    hATm1 = [T(_r3(2), "hATm10"), T(_r3(3), "hATm11")]
    HGP_QT = [hqs1] + hqt1 + hkt1 + hkh1
    HGP_SP = hkTok1 + hATm1
    ptr = Ring([P.sb([128, 512], BF16, f"pt{i}") for i in range(6)])
    kqr = Ring([P.sb([128, 128], BF16, f"kq{i}") for i in range(8)])
    vqr = Ring([P.sb([128, 128], BF16, f"vq{i}") for i in range(10)])
    kchr = Ring([P.sb([128, 2, 128], BF16, f"kch{i}") for i in range(3)])
    vchr = Ring([P.sb([128, 256], BF16, f"vch{i}") for i in range(3)])
    KTp = P.sb([128, 2, 512], BF16, "KTp")
    Vp = P.sb([128, 4, 256], BF16, "Vp")
    SX = [[P.sb([128, 128], F32, f"S{x}{h}") for h in range(8)] for x in range(2)]
    SXb = [[P.sb([128, 128], BF16, f"Sb{x}{h}") for h in range(8)] for x in range(2)]
    ropet = P.sb([128, 4, 128], F32, "ropet")
    st = {n: P.sb([128, 16], F32, "st_" + n) for n in ("a", "b", "c", "d", "ssq", "e", "f", "g")}
    kvf = P.sb([128, 512], F32, "kvf")

    mm = [P.ps([128, 512], F32, f"mm{i}") for i in range(4)]
    mmr = Ring(mm)
    acc = [P.ps([128, 512], F32, f"acc{i}") for i in range(2)]
    tb = [P.ps([128, 1024], BF16, f"tb{i}") for i in range(2)]
    tbr = Ring(tb)
    HB = [dict(qs=qs, qt=qt, kt=kt, kh=kh, kTok=kTok, ATm=ATm, Vbd=Vbd, ebt=ebt, obank=acc[0]),
          dict(qs=hqs1, qt=hqt1, kt=hkt1, kh=hkh1, kTok=hkTok1, ATm=hATm1, Vbd=Vbd1, ebt=ebt1, obank=acc[1])]

    def act(out, in_, func, reads, writes, pw=(), **kw):
        P.op("act", lambda e: e.activation(out=out, in_=in_, func=func, **kw), reads, writes, pw)

    def tt(out, in0, in1, op, reads, writes, eng="dve", pw=()):
        P.op(eng, lambda e: e.tensor_tensor(out=out, in0=in0, in1=in1, op=op), reads, writes, pw)

    def ts(out, in0, s1, s2, op0, op1, reads, writes, eng="dve", pw=()):
        if s2 is None:
            P.op(eng, lambda e: e.tensor_scalar(out=out, in0=in0, scalar1=s1, scalar2=None, op0=op0), reads, writes, pw)
        else:
            P.op(eng, lambda e: e.tensor_scalar(out=out, in0=in0, scalar1=s1, scalar2=s2, op0=op0, op1=op1), reads, writes, pw)

    def stt(out, in0, sc, in1, op0, op1, reads, writes):
        P.op("dve", lambda e: e.scalar_tensor_tensor(out=out, in0=in0, scalar=sc, in1=in1, op0=op0, op1=op1), reads, writes)

    def rstd_of(ss_ap, n, out_ap, tmp_ap, tiles_r, tiles_w):
        act(tmp_ap, ss_ap, AF.Ln, tiles_r, tiles_w[:1], scale=1.0 / n, bias=EPS)
        act(out_ap, tmp_ap, AF.Exp, tiles_w[:1], tiles_w[1:], scale=-0.5)

    def sigmoid_inplace_from(dst_t, dst_ap, src_ap, src_tiles):
        act(dst_ap, src_ap, AF.Exp, src_tiles, [dst_t], scale=-1.0)
        act(dst_ap, dst_ap, AF.Ln, [dst_t], [dst_t], bias=1.0)
        act(dst_ap, dst_ap, AF.Exp, [dst_t], [dst_t], scale=-1.0)

    wcache = {}

    def wload_multi(parts, kcn, width, cache=True):
        key = tuple((p[0].name, p[1], p[2], p[3], p[4]) for p in parts) + (kcn,)
        slab = slabs.next()
        if cache and key in wcache:
            scr = wcache[key]
            P.dma("pool", slab[:, 0:kcn, 0:width], scr[:, :].rearrange("p (k n) -> p k n", k=kcn), reads=[scr], writes=[slab])
            return slab
        for (Wd, r0, c0, ncols, dst_c0) in parts:
            src = Wd[r0:r0 + kcn * 128, c0:c0 + ncols].rearrange("(k p) n -> p k n", p=128)
            P.dma("pool", slab[:, 0:kcn, dst_c0:dst_c0 + ncols], src, writes=[slab])
        if cache:
            scr = dscr(f"wc{len(wcache)}", [128, kcn * width], BF16)
            P.dma("sp", scr[:, :].rearrange("p (k n) -> p k n", k=kcn), slab[:, 0:kcn, 0:width], reads=[slab], writes=[scr])
            wcache[key] = scr
        return slab

    def wload(Wd, r0, kcn, c0, ncols, cache=True):
        return wload_multi([(Wd, r0, c0, ncols, 0)], kcn, ncols, cache)

    def gemm_a(slab, kcn, lhs, lhs_tiles, bank, ncols=512, k0=0, first=True, last=True):
        def f(e):
            for kc in range(kcn):
                ins = e.matmul(bank[:, 0:ncols], lhsT=lhs(k0 + kc), rhs=slab[:, kc, 0:ncols],
                               start=(first and kc == 0), stop=(last and kc == kcn - 1))
            return ins
        P.op("pe", f, [slab] + lhs_tiles, [bank])

    def gemm_b(slab, kcn, c0, rhs, rhs_tiles, bank):
        def f(e):
            for kc in range(kcn):
                ins = e.matmul(bank[:, 0:512], lhsT=slab[:, kc, c0:c0 + 128], rhs=rhs(kc),
                               start=(kc == 0), stop=(kc == kcn - 1))
            return ins
        P.op("pe", f, [slab] + rhs_tiles, [bank])

    def transposes(src_t, src_aps, n):
        bank = tbr.next()

        def f(e):
            for k in range(n):
                ins = e.transpose(bank[:, k * 128:(k + 1) * 128], src_aps(k), identb[:])
            return ins
        P.op("pe", f, [src_t, identb], [bank])
        return bank

    P.dma("sp", cF[:], cF_d[:, :], writes=[cF])
    P.dma("sp", rowb[:], rowb_d[:, :, :], writes=[rowb])
    P.dma("sp", vecT[:], vecT_d[:, :], writes=[vecT])
    P.dma("sp", condT[:], condT_d.rearrange("p k g -> p (k g)"), writes=[condT])
    P.op("dve", lambda e: e.tensor_copy(out=identb[:], in_=identf), [cF], [identb])
    P.op("dve", lambda e: e.memset(onesb[:], 1.0), [], [onesb])
    P.op("dve", lambda e: e.memset(onesf[:], 1.0), [], [onesf])
    act(rowb[:, 0, :], rowb[:, 0, :], AF.Copy, [rowb], [rowb], scale=float(128 ** -0.5))
    for h in range(8):
        P.dma("sp", SX[1][h][:], stB0[h, :, :], writes=[SX[1][h]])

    tt(oml[:, 0:8], vecT[:, 160:168], vecT[:, 168:176], ALU.subtract, [vecT], [oml])
    tt(oml[:, 8:16], vecT[:, 176:184], vecT[:, 184:192], ALU.subtract, [vecT], [oml])
    act(oml[:], oml[:], AF.Exp, [oml], [oml])
    act(oml[:], oml[:], AF.Ln, [oml], [oml], bias=1.0)
    act(oml[:], oml[:], AF.Exp, [oml], [oml], scale=-1.0)
    ts(noml[:], oml[:], -1.0, None, ALU.mult, None, [oml], [noml])

    t0 = tf[0]
    act(t0[:, 0:32], condT[:], AF.Exp, [condT], [t0], scale=-1.0)
    ts(t0[:, 0:32], t0[:, 0:32], 1.0, None, ALU.add, None, [t0], [t0])
    P.op("dve", lambda e: e.reciprocal(out=t0[:, 0:32], in_=t0[:, 0:32]), [t0], [t0])
    tt(scT[:].rearrange("p k g -> p (k g)"), condT[:], t0[:, 0:32], ALU.mult, [condT, t0], [scT])
    pada = acc[0]

    def ada_slabs(cbs):
        for cb in cbs:
            cur = wload(w_ada, 0, 16, cb * 512, 512, cache=False)

            def f(e, cb=cb, cur=cur):
                for m in range(4):
                    c = cb * 4 + m
                    for kc in range(16):
                        ins = e.matmul(pada[:, 2 * c:2 * c + 2], lhsT=cur[:, kc, m * 128:(m + 1) * 128], rhs=scT[:, kc, :],
                                       start=(kc == 0), stop=(kc == 15))
                return ins
            P.op("pe", f, [cur, scT], [pada])

    def ada_finish(c0, c1):
        tt(modT[:, c0:c1, :], pada[:, 2 * c0:2 * c1].rearrange("p (c g) -> p c g", g=2),
           vecT[:, c0:c1].unsqueeze(2).broadcast_to([128, c1 - c0, 2]), ALU.add, [pada, vecT], [modT])

    ada_slabs(range(8))
    ada_finish(0, 32)
    for g in range(2):
        stt(AT1[:, :, g], modT[:, 16:32, g], 1.0, vecT[:, 96:112], ALU.add, ALU.mult, [modT, vecT], [AT1])

    def ada_rest():
        ada_finish(32, 96)
        for g in range(2):
            stt(AT2[:, :, g], modT[:, 64:80, g], 1.0, vecT[:, 128:144], ALU.add, ALU.mult, [modT, vecT], [AT2])
            tt(GT1[:, :, g], modT[:, 32:48, g], vecT[:, 112:128], ALU.mult, [modT, vecT], [GT1])
            tt(GT2[:, :, g], modT[:, 80:96, g], vecT[:, 144:160], ALU.mult, [modT, vecT], [GT2])
    BT1 = lambda kc, g: modT[:, kc, g:g + 1]
    BT2 = lambda kc, g: modT[:, 48 + kc, g:g + 1]

    def build_Cb(g):
        for GT, Cb in ((GT1, C1b), (GT2, C2b)):
            for q4 in range(4):
                bank = mmr.next()
                for k in range(4):
                    kc = q4 * 4 + k
                    d_ = dgt.next()
                    ts(d_[:], identf, GT[:, kc, g:g + 1], None, ALU.mult, None, [cF, GT], [d_])
                    P.op("pe", lambda e, d_=d_, bank=bank, k=k: e.matmul(bank[:, k * 128:(k + 1) * 128], lhsT=onesf[:], rhs=d_[:],
                                                                       start=True, stop=True), [onesf, d_], [bank])
                act(Cb[:, q4 * 512:(q4 + 1) * 512], bank[:, 0:512], AF.Copy, [bank], [Cb])

    def norm_tile(xt, i, g, AT, BT):
        norm_B(norm_A(xt), i, g, AT, BT)

    def norm_A(xt):
        xs = xsr.next()
        act(xs[:], xt[:], AF.Square, [xt], [xs, st["a"]], accum_out=st["a"][:, 0:1])
        rstd_of(st["a"][:, 0:1], D, st["a"][:, 2:3], st["a"][:, 1:2], [st["a"]], [st["a"], st["a"]])
        ts(xs[:], xt[:], st["a"][:, 2:3], None, ALU.mult, None, [xt, st["a"]], [xs])
        return xs

    def norm_B(xs, i, g, AT, BT):
        for half in range(2):
            bank = transposes(xs, lambda k, half=half: xs[:, (half * 8 + k) * 128:(half * 8 + k + 1) * 128], 8)
            for k in range(8):
                kc = half * 8 + k
                o = hT[:, kc, i * 128:(i + 1) * 128]
                src = bank[:, k * 128:(k + 1) * 128]
                if half == 0:
                    act(o, src, AF.Identity, [bank, AT, modT], [], pw=[hT], scale=AT[:, kc, g:g + 1], bias=BT(kc, g))
                else:
                    ts(o, src, AT[:, kc, g:g + 1], BT(kc, g), ALU.mult, ALU.add, [bank, AT, modT], [], pw=[hT])

    def norm_one_A(row0, i):
        xt = xtr.next()
        P.dma("sp", xt[:], x_all[row0 + i * 128:row0 + (i + 1) * 128, :], writes=[xt])
        return norm_A(xt)

    def load_norm_block(row0, g):
        xts = []
        xt = xtr.next()
        P.dma("sp", xt[:], x_all[row0:row0 + 128, :], writes=[xt])
        for i in range(4):
            nxt = None
            if i < 3:
                nxt = xtr.next()
                P.dma("sp", nxt[:], x_all[row0 + (i + 1) * 128:row0 + (i + 2) * 128, :], writes=[nxt])
            norm_tile(xt, i, g, AT1, BT1)
            xt = nxt

    def headnorm(bank, c0, nh, nrow, out_ap, out_t):
        w = nh * 128
        t_sq = tfr.next()
        act(t_sq[:, 0:w], bank[:, c0:c0 + w], AF.Square, [bank], [t_sq])
        P.op("dve", lambda e: e.tensor_reduce(out=st["b"][:, 0:nh], in_=t_sq[:, 0:w].rearrange("p (h d) -> p h d", h=nh),
                                             axis=AX.X, op=ALU.add), [t_sq], [st["b"]])
        rstd_of(st["b"][:, 0:nh], 128, st["b"][:, 8:8 + nh], st["b"][:, 4:4 + nh], [st["b"]], [st["b"], st["b"]])
        tt(t_sq[:, 0:w].rearrange("p (h d) -> p h d", h=nh), bank[:, c0:c0 + w].rearrange("p (h d) -> p h d", h=nh),
           st["b"][:, 8:8 + nh].unsqueeze(2).broadcast_to([128, nh, 128]), ALU.mult, [bank, st["b"]], [t_sq])
        tt(out_ap.rearrange("p (h d) -> p h d", h=nh), t_sq[:, 0:w].rearrange("p (h d) -> p h d", h=nh),
           rowb[:, nrow, :].unsqueeze(1).broadcast_to([128, nh, 128]), ALU.mult, [t_sq, rowb], [out_t])

    def rope(src_t, src_ap, nh, i, out_t, out_ap):
        v = src_ap.rearrange("p (h a f d) -> p h a f d", h=nh, a=2, f=2)
        o = out_ap.rearrange("p (h a f d) -> p h a f d", h=nh, a=2, f=2)
        tbl = ropet[:, i, :].rearrange("p (c a d) -> p c a d", c=2, a=2)
        cos = tbl[:, 0, :, :].unsqueeze(1).broadcast_to([128, nh, 2, 32])
        sin = tbl[:, 1, :, :].unsqueeze(1).broadcast_to([128, nh, 2, 32])
        a1, a2 = v[:, :, :, 0, :], v[:, :, :, 1, :]
        t1, t2 = tfr.next(), tfr.next()
        w = nh * 64
        t1v = t1[:, 0:w].rearrange("p (h a d) -> p h a d", h=nh, a=2)
        t2v = t2[:, 0:w].rearrange("p (h a d) -> p h a d", h=nh, a=2)
        tt(t1v, a1, cos, ALU.mult, [src_t, ropet], [t1])
        tt(t2v, a2, sin, ALU.mult, [src_t, ropet], [t2])
        tt(o[:, :, :, 0, :], t1v, t2v, ALU.subtract, [t1, t2], [out_t])
        t3, t4 = tfr.next(), tfr.next()
        t3v = t3[:, 0:w].rearrange("p (h a d) -> p h a d", h=nh, a=2)
        t4v = t4[:, 0:w].rearrange("p (h a d) -> p h a d", h=nh, a=2)
        tt(t3v, a2, cos, ALU.mult, [src_t, ropet], [t3])
        tt(t4v, a1, sin, ALU.mult, [src_t, ropet], [t4])
        tt(o[:, :, :, 1, :], t3v, t4v, ALU.add, [t3, t4], [out_t])

    def q_stage(sample):
        P.handoff(HGP_QT, [QT])
        for s in range(2):
            slab = wload(w_in, 0, 16, C_QA + s * 512, 512)
            banks = {}

            def issue(i):
                banks[i] = mmr.next()
                gemm_a(slab, 16, lambda kc, i=i: hT[:, kc, i * 128:(i + 1) * 128], [hT], banks[i])
            issue(0)
            for i in range(4):
                if i + 1 < 4:
                    issue(i + 1)
                bank = banks[i]
                if sample:
                    qn = tfr.next()
                    headnorm(bank, 0, 4, 0, qn[:], qn)
                    rope(qn, qn[:], 4, i, qb16, qb16[:])
                else:
                    headnorm(bank, 0, 4, 0, qb16[:], qb16)
                tbk = transposes(qb16, lambda k: qb16[:, k * 128:(k + 1) * 128], 4)
                act(QTv[:, s, i, :], tbk[:, 0:512], AF.Copy, [tbk], [QT])

    def kv_stage(sample, blk_local_tok0, out_row0):
        slab = wload(w_in, 0, 16, C_KA, 512)
        banks = {}

        def issue(i):
            banks[i] = mmr.next()
            gemm_a(slab, 16, lambda kc, i=i: hT[:, kc, i * 128:(i + 1) * 128], [hT], banks[i])
        issue(0)
        for i in range(4):
            if i + 1 < 4:
                issue(i + 1)
            bank = banks[i]
            kb = ptr.next()
            if sample:
                kn = tfr.next()
                headnorm(bank, 0, 2, 1, kn[:, 0:256], kn)
                rope(kn, kn[:, 0:256], 2, i, kb, kb[:, 0:256])
            else:
                headnorm(bank, 0, 2, 1, kvf[:, 0:256], kvf)
                P.op("dve", lambda e, kb=kb: e.tensor_copy(out=kb[:, 0:256], in_=kvf[:, 0:256]), [kvf], [kb])
            act(kvf[:, 256:512], bank[:, 256:512], AF.Copy, [bank], [kvf])
            if not sample:
                P.dma("sp", kc_out[out_row0 + i * 128:out_row0 + (i + 1) * 128, :], kvf[:, 0:256], reads=[kvf], is_output=True)
                P.dma("sp", vc_out[out_row0 + i * 128:out_row0 + (i + 1) * 128, :], kvf[:, 256:512], reads=[kvf], is_output=True)
            tbk = transposes(kb, lambda k, kb=kb: kb[:, k * 128:(k + 1) * 128], 2)
            if sample:
                kch = kchr.next()
                act(kch[:].rearrange("p g n -> p (g n)"), tbk[:, 0:256], AF.Copy, [tbk], [kch])
                t0_ = blk_local_tok0 + i * 128
                P.dma("sp", KT_scr[:, :, t0_:t0_ + 128].rearrange("g d n -> d g n"), kch[:], reads=[kch], writes=[KT_scr])
                vch = vchr.next()
                P.op("dve", lambda e, vch=vch: e.tensor_copy(out=vch[:], in_=kvf[:, 256:512]), [kvf], [vch])
                P.dma("sp", V_scr[t0_:t0_ + 128, :], vch[:], reads=[vch], writes=[V_scr])
            else:
                act(KTp[:, :, i * 128:(i + 1) * 128], tbk[:, 0:256].rearrange("p (g n) -> p g n", g=2), AF.Copy, [tbk], [KTp])
                P.op("dve", lambda e, i=i: e.tensor_copy(out=Vp[:, i, :], in_=kvf[:, 256:512]), [kvf], [Vp])

    def attention(qtiles, nchunks, chunk_src):
        LOOK, PF = 3, 4
        Sbank = Ring(mm)
        ob, db = acc[0], acc[1]
        for i in qtiles:
            for g in range(2):
                srcs = {}

                def get(j):
                    if j not in srcs and j < nchunks:
                        srcs[j] = chunk_src(j, g)
                    return srcs.get(j)
                pend = []

                def flush_one():
                    j, pt, vv, tiles = pend.pop(0)

                    def f(e, pt=pt, vv=vv, j=j):
                        e.matmul(ob[:, 0:512], lhsT=vv, rhs=pt[:], start=(j == 0), stop=(j == nchunks - 1))
                        return e.matmul(db[:, 0:512], lhsT=onesb[:], rhs=pt[:], start=(j == 0), stop=(j == nchunks - 1))
                    P.op("pe", f, tiles + [pt, onesb], [ob, db])
                for j in range(nchunks):
                    for jj in range(j, min(j + PF, nchunks)):
                        get(jj)
                    kT, vv, tiles = get(j)
                    sb_ = Sbank.next()
                    P.op("pe", lambda e, sb_=sb_, kT=kT, g=g, i=i: e.matmul(sb_[:, 0:512], lhsT=kT, rhs=QTv[:, g, i, :], start=True, stop=True),
                         tiles + [QT], [sb_])
                    pt = ptr.next()
                    act(pt[:], sb_[:, 0:512], AF.Exp, [sb_], [pt])
                    pend.append((j, pt, vv, tiles))
                    if len(pend) > LOOK:
                        flush_one()
                while pend:
                    flush_one()
                rD = tfr.next()
                act(rD[:], db[:, 0:512], AF.Ln, [db], [rD])
                act(rD[:], rD[:], AF.Exp, [rD], [rD], scale=-1.0)
                tt(attTv[:, 4 * g:4 * g + 4, i * 128:(i + 1) * 128], ob[:, 0:512].rearrange("p (h n) -> p h n", h=4),
                   rD[:].rearrange("p (h n) -> p h n", h=4), ALU.mult, [ob, rD], [attT])
            if deferred:
                deferred.pop(0)()

    def hg_gates(zbank, x, h, need_q, B):
        c = x * 8 + h
        e0, s1, lf, bc, t4, t5 = tf
        ebt_, qt_, kt_, kh_, kTok_, qs_ = B["ebt"], B["qt"], B["kt"], B["kh"], B["kTok"], B["qs"]
        act(e0[:], zbank[:, 0:512], AF.Exp, [zbank], [e0]); yield
        act(e0[:], e0[:], AF.Ln, [e0], [e0], bias=1.0); yield
        act(s1[:], e0[:], AF.Exp, [e0], [s1], scale=-1.0); yield
        act(lf[:], s1[:], AF.Ln, [s1, noml], [lf], scale=noml[:, c:c + 1], bias=1.0); yield
        P.op("dve", lambda e: e.tensor_tensor_scan(out=bc[:], data0=resetm, data1=lf[:], initial=0.0, op0=ALU.mult, op1=ALU.add),
             [cF, lf], [bc]); yield
        act(ebt_[x][:], bc[:, 31::32], AF.Exp, [bc], [ebt_[x]]); yield
        btot = bc[:, 31::32].unsqueeze(2).broadcast_to([128, 16, 32])
        v3 = lambda t: t[:].rearrange("p (c s) -> p c s", s=32)
        if x == 0:
            cum = bc
            tt(v3(t4), btot, v3(bc), ALU.subtract, [bc], [t4]); yield
            rem = t4
        else:
            tt(t4[:], bc[:], lf[:], ALU.subtract, [bc, lf], [t4]); yield
            rem = t4
            tt(v3(e0), btot, v3(t4), ALU.subtract, [bc, t4], [e0]); yield
            cum = e0
        if need_q:
            act(t5[:], cum[:], AF.Exp, [cum], [t5]); yield
            tt(qt_[x][:], qs_[:], t5[:], ALU.mult, [qs_, t5], [qt_[x]]); yield
            act(t5[:], cum[:], AF.Exp, [cum], [t5], scale=-1.0); yield
            stt(kt_[x][:], s1[:], oml[:, c:c + 1], t5[:], ALU.mult, ALU.mult, [s1, oml, t5], [kt_[x]]); yield
        act(t5[:], rem[:], AF.Exp, [rem], [t5]); yield
        stt(kh_[x][:], s1[:], oml[:, c:c + 1], t5[:], ALU.mult, ALU.mult, [s1, oml, t5], [kh_[x]]); yield
        tbk = transposes(kh_[x], lambda k: kh_[x][:, k * 128:(k + 1) * 128], 4); yield
        act(kTok_[x][:].rearrange("p i n -> p (i n)"), tbk[:, 0:512], AF.Copy, [tbk], [kTok_[x]]); yield

    def make_vbd(h, B):
        Vb = B["Vbd"]
        for i in range(4):
            tt(Vb[:, i, :].rearrange("p (c n) -> p c n", c=4),
               vtokv[:, i, h * 128:(h + 1) * 128].unsqueeze(1).broadcast_to([128, 4, 128]),
               mask4.unsqueeze(2).broadcast_to([128, 4, 128]), ALU.mult, [vtok, cF], [Vb])
            yield

    def hg_G(h, B, main, slab, c0):
        rhs = lambda kc: hT[:, kc, :]
        gbank = Ring([mm[0], mm[1]])
        if main:
            bq = gbank.next()
            gemm_b(slab, 16, c0, rhs, [hT], bq); yield
            t_ = tf[5]
            act(t_[:], bq[:, 0:512], AF.Exp, [bq], [t_], scale=-1.0); yield
            act(t_[:], t_[:], AF.Ln, [t_], [t_], bias=1.0); yield
            act(t_[:], t_[:], AF.Exp, [t_], [t_], scale=-1.0); yield
            tt(B["qs"][:], bq[:, 0:512], t_[:], ALU.mult, [bq, t_], [B["qs"]]); yield
            for x in range(2):
                bz = gbank.next()
                gemm_b(slab, 16, c0 + 128 + 128 * x, rhs, [hT], bz); yield
                yield from hg_gates(bz, x, h, True, B)
        else:
            bz = gbank.next()
            gemm_b(slab, 16, c0, rhs, [hT], bz); yield
            yield from hg_gates(bz, 1, h, False, B)
        yield from make_vbd(h, B)
        if main:
            obank = B["obank"]
            for x in range(2):
                bank = gbank.next()

                def f(e, x=x, bank=bank):
                    for i in range(4):
                        ins = e.matmul(bank[:, i * 128:(i + 1) * 128], lhsT=B["kt"][x][:, i * 128:(i + 1) * 128],
                                       rhs=B["qt"][x][:, i * 128:(i + 1) * 128], start=True, stop=True)
                    return ins
                P.op("pe", f, [B["kt"][x], B["qt"][x]], [bank]); yield
                msk = (maskA if x == 0 else maskB).unsqueeze(1).broadcast_to([128, 4, 128])
                tt(B["ATm"][x][:], bank[:, 0:512].rearrange("p (i n) -> p i n", n=128), msk, ALU.mult, [bank, cF], [B["ATm"][x]]); yield

            def f2(e):
                for i in range(4):
                    e.matmul(obank[:, i * 128:(i + 1) * 128], lhsT=B["ATm"][0][:, i, :], rhs=vtokv[:, i, h * 128:(h + 1) * 128],
                             start=(i == 0), stop=False, skip_group_check=True)
                    ins = e.matmul(obank[:, i * 128:(i + 1) * 128], lhsT=B["ATm"][1][:, i, :], rhs=vtokv[:, i, h * 128:(h + 1) * 128],
                                   start=False, stop=False, skip_group_check=True)
                return ins
            P.op("pe", f2, [B["ATm"][0], B["ATm"][1], vtok], [obank]); yield

    def hg_C(h, B, dirs, segs, with_o, pre_blk=None, sample_blk=None, seq0=None):
        obank = B["obank"]
        if pre_blk is not None and pre_blk <= 3:
            P.dma("sp", SB_scr[pre_blk, h, :, :], SX[1][h][:], reads=[SX[1][h]], writes=[SB_scr])
        if sample_blk is not None:
            P.dma("sp", SX[1][h][:], SB_scr[sample_blk, h, :, :], reads=[SB_scr], writes=[SX[1][h]])
            act(SXb[1][h][:], SX[1][h][:], AF.Copy, [SX[1][h]], [SXb[1][h]])
        pbank = {0: mm[2], 1: mm[3]}
        pring = Ring([mm[2], mm[3]])
        for si, seg in enumerate(segs):
            if seq0 is not None:
                for x in dirs:
                    P.op("dve", lambda e, x=x: e.memset(SX[x][h][:], 0.0), [], [SX[x][h]])
                    P.op("dve", lambda e, x=x: e.memset(SXb[x][h][:], 0.0), [], [SXb[x][h]])
            order = {0: [(i, c) for i in seg for c in range(4)], 1: [(i, c) for i in reversed(seg) for c in reversed(range(4))]}
            pb = {}
            nsteps = len(seg) * 4
            for k in range(nsteps):
                for x in dirs:
                    i, c = order[x][k]
                    if k % 4 == 0:
                        bank = pbank[x] if len(dirs) == 2 else pring.next()
                        P.op("pe", lambda e, bank=bank, x=x, i=i: e.matmul(bank[:, 0:512], lhsT=B["kTok"][x][:, i, :], rhs=B["Vbd"][:, i, :],
                                                                          start=True, stop=True), [B["kTok"][x], B["Vbd"]], [bank])
                        pb[x] = bank
                    S, Sb = SX[x][h], SXb[x][h]
                    if with_o:
                        P.op("pe", lambda e, x=x, i=i, c=c, Sb=Sb: e.matmul(obank[32 * c:32 * c + 32, i * 128:(i + 1) * 128],
                                                                          lhsT=B["qt"][x][:, i * 128 + 32 * c:i * 128 + 32 * c + 32], rhs=Sb[:],
                                                                          start=False, stop=True, tile_position=(0, 32 * c), skip_group_check=True),
                             [B["qt"][x], Sb], [obank])
                    bank = pb[x]
                    stt(S[:], S[:], B["ebt"][x][:, i * 4 + c:i * 4 + c + 1], bank[:, c * 128:(c + 1) * 128], ALU.mult, ALU.add,
                        [S, B["ebt"][x], bank], [S])
                    if with_o:
                        act(Sb[:], S[:], AF.Copy, [S], [Sb])
                yield
            if seq0 is not None:
                P.dma("sp", sA_out[seq0 + si, h, :, :], SX[0][h][:], reads=[SX[0][h]], is_output=True)
                P.dma("sp", sB_out[seq0 + si, h, :, :], SX[1][h][:], reads=[SX[1][h]], is_output=True)

    def hg_E(h, B):
        obank = B["obank"]
        t_sq = kvf
        act(t_sq[:], obank[:, 0:512], AF.Square, [obank], [t_sq]); yield
        P.op("dve", lambda e: e.tensor_reduce(out=st["c"][:, 0:4], in_=t_sq[:].rearrange("p (h d) -> p h d", h=4), axis=AX.X, op=ALU.add),
             [t_sq], [st["c"]]); yield
        rstd_of(st["c"][:, 0:4], 128, st["c"][:, 8:12], st["c"][:, 4:8], [st["c"]], [st["c"], st["c"]]); yield
        v4 = lambda ap: ap.rearrange("p (h d) -> p h d", h=4)
        tt(v4(t_sq[:]), v4(obank[:, 0:512]), st["c"][:, 8:12].unsqueeze(2).broadcast_to([128, 4, 128]), ALU.mult, [obank, st["c"]], [t_sq]); yield
        tt(v4(t_sq[:]), v4(t_sq[:]), rowb[:, 2, :].unsqueeze(1).broadcast_to([128, 4, 128]), ALU.mult, [t_sq, rowb], [t_sq]); yield
        tt(v4(qb16[:]), v4(t_sq[:]), sogv[:, :, h * 128:(h + 1) * 128], ALU.mult, [t_sq, sog], [qb16]); yield
        tbk = transposes(qb16, lambda k: qb16[:, k * 128:(k + 1) * 128], 4); yield
        act(hgTv[:, h, :], tbk[:, 0:512], AF.Copy, [tbk], [hgT]); yield

    def drain(g):
        if g is not None:
            for _ in g:
                pass

    def pipeline_heads(makeG, makeC, doE, ratio):
        g = makeG(0)
        drain(g)
        pe_ = None
        for h in range(8):
            c = makeC(h)
            gn = makeG(h + 1) if h < 7 else None
            for _ in c:
                if pe_ is not None and next(pe_, "end") == "end":
                    pe_ = None
                if gn is not None:
                    for _k in range(ratio):
                        if next(gn, "end") == "end":
                            gn = None
                            break
            drain(pe_)
            drain(gn)
            pe_ = doE(h) if doE is not None else None
        drain(pe_)

    def hgrn_main(segs, sample, blk, seq0):
        P.handoff([QT], HGP_QT)
        for s in range(2):
            slab = wload(w_in, 0, 16, C_VH + s * 512, 512)
            for i in range(4):
                bank = mmr.next()
                gemm_a(slab, 16, lambda kc, i=i: hT[:, kc, i * 128:(i + 1) * 128], [hT], bank)
                act(vtokv[:, i, s * 512:(s + 1) * 512], bank[:, 0:512], AF.Copy, [bank], [vtok])
        for s in range(2):
            slab = wload(w_in, 0, 16, C_OG + s * 512, 512)
            for i in range(4):
                bank = mmr.next()
                gemm_a(slab, 16, lambda kc, i=i: hT[:, kc, i * 128:(i + 1) * 128], [hT], bank)
                t_ = tfr.next()
                sigmoid_inplace_from(t_, t_[:], bank[:, 0:512], [bank])
                tt(sogv[:, i, s * 512:(s + 1) * 512], bank[:, 0:512], t_[:], ALU.mult, [bank, t_], [sog])

        def makeG(h):
            slab = wload_multi([(w_in, 0, C_QH + h * 128, 128, 0), (w_in, 0, C_ZA + h * 128, 128, 128),
                                (w_in, 0, C_ZB + h * 128, 128, 256)], 16, 384)
            return hg_G(h, HB[h % 2], True, slab, 0)

        def makeC(h):
            return hg_C(h, HB[h % 2], [0, 1], segs, True, sample_blk=(blk if sample else None), seq0=(None if sample else seq0))
        pipeline_heads(makeG, makeC, lambda h: hg_E(h, HB[h % 2]), 4)

    def hgrn_pre(blk):
        for s_ in range(2):
            slab = wload(w_in, 0, 16, C_VH + s_ * 512, 512)
            for i in range(4):
                bank = mmr.next()
                gemm_a(slab, 16, lambda kc, i=i: hT[:, kc, i * 128:(i + 1) * 128], [hT], bank)
                act(vtokv[:, i, s_ * 512:(s_ + 1) * 512], bank[:, 0:512], AF.Copy, [bank], [vtok])
        gbank = Ring([mm[0], mm[1]])
        zslab, zb = {}, {}

        def issue(h):
            hs = h // 4
            if hs not in zslab:
                zslab[hs] = wload(w_in, 0, 16, C_ZB + hs * 512, 512)
            zb[h] = gbank.next()
            gemm_b(zslab[hs], 16, (h % 4) * 128, lambda kc: hT[:, kc, :], [hT], zb[h])
        def head(h):
            B = HB[h % 2]
            c = 8 + h
            if blk <= 3:
                P.dma("sp", SB_scr[blk, h, :, :], SX[1][h][:], reads=[SX[1][h]], writes=[SB_scr])
            if h % 2 == 1:
                e0, s1, lf = tf[3], tf[4], tf[5]
                bc = tf[3]
            else:
                e0, s1, lf = tf[0], tf[1], tf[2]
                bc = tf[0]
            zbank = zb[h]
            act(e0[:], zbank[:, 0:512], AF.Exp, [zbank], [e0])
            if h + 1 < 8:
                issue(h + 1)
            yield
            act(e0[:], e0[:], AF.Ln, [e0], [e0], bias=1.0); yield
            act(s1[:], e0[:], AF.Exp, [e0], [s1], scale=-1.0); yield
            act(lf[:], s1[:], AF.Ln, [s1, noml], [lf], scale=noml[:, c:c + 1], bias=1.0); yield
            P.op("dve", lambda e, bc=bc, lf=lf: e.tensor_tensor_scan(out=bc[:], data0=ones512, data1=lf[:], initial=0.0, op0=ALU.mult, op1=ALU.add),
                 [cF, lf], [bc]); yield
            act(B["ebt"][1][:, 0:1], bc[:, 511:512], AF.Exp, [bc], [B["ebt"][1]]); yield
            tt(lf[:], bc[:], lf[:], ALU.subtract, [bc, lf], [lf]); yield
            act(lf[:], lf[:], AF.Exp, [lf], [lf]); yield
            kh_ = B["kh"][1]
            stt(kh_[:], s1[:], oml[:, c:c + 1], lf[:], ALU.mult, ALU.mult, [s1, oml, lf], [kh_]); yield
            tbk = transposes(kh_, lambda k, kh_=kh_: kh_[:, k * 128:(k + 1) * 128], 4); yield
            kT_ = B["kTok"][1]
            act(kT_[:].rearrange("p i n -> p (i n)"), tbk[:, 0:512], AF.Copy, [tbk], [kT_]); yield
            pbk = mm[2 + h % 2]

            def f(e, kT_=kT_, pbk=pbk, h=h):
                for i in range(4):
                    ins = e.matmul(pbk[:, 0:128], lhsT=kT_[:, i, :], rhs=vtokv[:, i, h * 128:(h + 1) * 128], start=(i == 0), stop=(i == 3))
                return ins
            P.op("pe", f, [kT_, vtok], [pbk]); yield
            S = SX[1][h]
            stt(S[:], S[:], B["ebt"][1][:, 0:1], pbk[:, 0:128], ALU.mult, ALU.add, [S, B["ebt"][1], pbk], [S]); yield

        issue(0)
        SKEW = 6
        gens = [head(h) for h in range(8)]
        cur, nxt_, started = gens[0], None, 1
        steps = 0
        while cur is not None:
            alive = next(cur, "end") != "end"
            steps += 1
            if nxt_ is None and started < 8 and steps >= SKEW:
                nxt_ = gens[started]
                started += 1
            if nxt_ is not None and next(nxt_, "end") == "end":
                nxt_ = None
            if not alive:
                cur, nxt_ = nxt_, None
                steps = SKEW
                if cur is None and started < 8:
                    cur = gens[started]
                    started += 1
                    steps = 0

    def merge_block():
        P.handoff([QT, vtok] + HGP_QT, [mT])
        Wga = wload(w_in, 0, 16, C_GA, 512)
        for cb in range(4):
            Wa = wload(w_ba, 0, 8, cb * 512, 512)
            for m in range(4):
                b3 = mmr.next()
                gemm_b(Wga, 16, m * 128, lambda kc: hT[:, kc, :], [hT], b3)
                sigmoid_inplace_from(tf[m], tf[m][:], b3[:, 0:512], [b3])
            Wgb = wload(w_in, 0, 16, C_GB + cb * 512, 512)
            for m in range(4):
                b1 = mmr.next()
                gemm_b(Wa, 8, m * 128, lambda kc: attTv[:, kc, :], [attT], b1)
                tt(tf[m][:], b1[:, 0:512], tf[m][:], ALU.mult, [b1, tf[m]], [tf[m]])
            Wh = wload(w_bh, 0, 8, cb * 512, 512)
            for m in range(4):
                b4 = mmr.next()
                gemm_b(Wgb, 16, m * 128, lambda kc: hT[:, kc, :], [hT], b4)
                sigmoid_inplace_from(tf[4], tf[4][:], b4[:, 0:512], [b4])
                b2 = mmr.next()
                gemm_b(Wh, 8, m * 128, lambda kc: hgTv[:, kc, :], [hgT], b2)
                tt(tf[5][:], b2[:, 0:512], tf[4][:], ALU.mult, [b2, tf[4]], [tf[5]])
                tt(mTv[:, cb * 4 + m, :], tf[5][:], tf[m][:], ALU.add, [tf[5], tf[m]], [], pw=[mT])
                if m == 1 and cb + 1 < 4:
                    pass
            if cb + 1 < 4:
                Wga = wload(w_in, 0, 16, C_GA + (cb + 1) * 512, 512)

    def proj_out_stage(Wd, kparts, lhs, lhs_tiles, scr, hook=None):
        for cb in range(4):
            if hook is not None:
                hook(cb)
            for pi, (k0, kcn) in enumerate(kparts):
                slab = wload(Wd, k0 * 128, kcn, cb * 512, 512)
                for i in range(4):
                    gemm_a(slab, kcn, lambda kc, i=i: lhs(kc, i), lhs_tiles, mm[i], k0=k0, first=(pi == 0), last=(pi == len(kparts) - 1))
            for i in range(4):
                t_ = tfr.next()
                act(t_[:], mm[i][:, 0:512], AF.Copy, [mm[i]], [t_])
                P.dma("sp", scr[i * 128:(i + 1) * 128, cb * 512:(cb + 1) * 512], t_[:], reads=[t_], writes=[scr])
                act(sqj[:], mm[i][:, 0:512], AF.Square, [mm[i]], [], pw=[st["ssq"]], accum_out=st["ssq"][:, i * 4 + cb:i * 4 + cb + 1])

    def tile_rstd(i):
        P.op("dve", lambda e: e.tensor_reduce(out=st["d"][:, 0:1], in_=st["ssq"][:, i * 4:(i + 1) * 4], axis=AX.X, op=ALU.add), [st["ssq"]], [st["d"]])
        rstd_of(st["d"][:, 0:1], D, st["d"][:, 2:3], st["d"][:, 1:2], [st["d"]], [st["d"], st["d"]])
        return st["d"][:, 2:3]

    deferred = []

    def block_tail(row0, g, par, yrow0, nxt=None, defer=False):
        proj_out_stage(w_out, [(0, 16)], lambda kc, i: mTv[:, kc, i * 128:(i + 1) * 128], [mT], mo_scr[par])
        for i in range(4):
            r = tile_rstd(i)
            mo_t, xt = xtr.next(), xtr.next()
            P.dma("sp", mo_t[:], mo_scr[par][i * 128:(i + 1) * 128, :], reads=[mo_scr[par]], writes=[mo_t])
            P.dma("sp", xt[:], x_all[row0 + i * 128:row0 + (i + 1) * 128, :], writes=[xt])
            stt(mo_t[:], mo_t[:], r, C1b[:], ALU.mult, ALU.mult, [mo_t, st["d"], C1b], [mo_t])
            tt(xt[:], xt[:], mo_t[:], ALU.add, [xt, mo_t], [xt])
            P.dma("sp", x1_scr[par][i * 128:(i + 1) * 128, :], xt[:], reads=[xt], writes=[x1_scr[par]])
            norm_tile(xt, i, g, AT2, BT2)
        P.handoff([QT, vtok, mT, attT, sog, hgT] + HGP_QT + HGP_SP, [fT])
        Wg = wload(w_fi, 0, 16, 0, 512)
        for jb in range(11):
            Wu = wload(w_fi, 0, 16, DFF + jb * 512, 512)
            for m in range(4):
                bg = mmr.next()
                gemm_b(Wg, 16, m * 128, lambda kc: hT[:, kc, :], [hT], bg)
                sigmoid_inplace_from(tf[m], tf[m][:], bg[:, 0:512], [bg])
                tt(tf[m][:], bg[:, 0:512], tf[m][:], ALU.mult, [bg, tf[m]], [tf[m]])
            if jb + 1 < 11:
                Wg = wload(w_fi, 0, 16, (jb + 1) * 512, 512)
            for m in range(4):
                bu = mmr.next()
                gemm_b(Wu, 16, m * 128, lambda kc: hT[:, kc, :], [hT], bu)
                tt(fTv[:, jb * 4 + m, :], tf[m][:], bu[:, 0:512], ALU.mult, [tf[m], bu], [], pw=[fT])
        pend_xs = {}

        def hook(cb):
            if nxt is None:
                return
            if cb >= 1:
                norm_B(pend_xs[cb - 1], cb - 1, nxt[1], AT1, BT1)
            pend_xs[cb] = norm_one_A(nxt[0], cb)
        proj_out_stage(w_fo, [(0, 16), (16, 16), (32, 12)], lambda kc, i: fTv[:, kc, i * 128:(i + 1) * 128], [fT], mo_scr[par], hook)
        if nxt is not None:
            norm_B(pend_xs[3], 3, nxt[1], AT1, BT1)
        def finish(i):
            r = tile_rstd(i)
            fo_t, x1t = xtr.next(), xtr.next()
            P.dma("sp", fo_t[:], mo_scr[par][i * 128:(i + 1) * 128, :], reads=[mo_scr[par]], writes=[fo_t])
            P.dma("sp", x1t[:], x1_scr[par][i * 128:(i + 1) * 128, :], reads=[x1_scr[par]], writes=[x1t])
            stt(fo_t[:], fo_t[:], r, C2b[:], ALU.mult, ALU.mult, [fo_t, st["d"], C2b], [fo_t])
            tt(x1t[:], x1t[:], fo_t[:], ALU.add, [x1t, fo_t], [x1t])
            P.dma("sp", y_own[yrow0 + i * 128:yrow0 + (i + 1) * 128, :], x1t[:], reads=[x1t], is_output=True)
        for i in range(4):
            if defer:
                deferred.append(lambda i=i: finish(i))
            else:
                finish(i)
        P.handoff([fT], [QT, vtok, attT, sog, hgT, mT] + HGP_QT + HGP_SP)

    def _program():
        ck("ada")
        for t in range(2):
            xt = xtr.next()
            P.dma("sp", xt[:, 0:256], cache_k[t * 128:(t + 1) * 128, :], writes=[xt])
            P.dma("sp", xt[:, 256:512], cache_v[t * 128:(t + 1) * 128, :], writes=[xt])
            kb = ptr.next()
            P.op("dve", lambda e, kb=kb, xt=xt: e.tensor_copy(out=kb[:], in_=xt[:, 0:512]), [xt], [kb])
            tbk = transposes(kb, lambda k, kb=kb: kb[:, k * 128:(k + 1) * 128], 2)
            kch = kchr.next()
            act(kch[:].rearrange("p g n -> p (g n)"), tbk[:, 0:256], AF.Copy, [tbk], [kch])
            t0_ = 4096 + t * 128
            P.dma("sp", KT_scr[:, :, t0_:t0_ + 128].rearrange("g d n -> d g n"), kch[:], reads=[kch], writes=[KT_scr])
            P.dma("sp", V_scr[t0_:t0_ + 128, :], kb[:, 256:512], reads=[kb], writes=[V_scr])

        ck("cache")
        import os
        _pb = [int(v) for v in os.environ.get("DBG_PRE_BLOCKS", "7,6,5,4,3,2,1,0").split(",")]
        _phg = os.environ.get("DBG_PRE_HG", "1") == "1"
        _pkv = os.environ.get("DBG_PRE_KV", "1") == "1"
        _ada_done = set()
        for blk in _pb:
            row0 = ROW_S + blk * 512
            load_norm_block(row0, 1)
            ck("pre_norm")
            P.dma("sp", ropet[:], rope_d[blk * 512:(blk + 1) * 512].rearrange("(i p) c a d -> p i (c a d)", p=128), writes=[ropet])
            if _pkv:
                kv_stage(True, blk * 512, 0)
            ck("pre_kv")
            if blk >= 1 and _phg:
                hgrn_pre(blk)
                ck("pre_hg")
            else:
                for h in range(8):
                    P.dma("sp", SB_scr[0, h, :, :], SX[1][h][:], reads=[SX[1][h]], writes=[SB_scr])
            _nx = [cb for cb in range(8, 24) if cb not in _ada_done][:2]
            _ada_done.update(_nx)
            ada_slabs(_nx)
            ck(f"pre_blk{blk}")

        ada_slabs([cb for cb in range(8, 24) if cb not in _ada_done])
        ada_rest()
        ck("pre")
        build_Cb(0)
        for pb_ in range(NPB):
            row0 = pb_ * 512
            if pb_ == 0 or stop_after is not None:
                load_norm_block(row0, 0)
            ck("p_norm")
            q_stage(False)
            ck("p_q")
            kv_stage(False, 0, row0)
            ck("p_kv")
            for si, seg in enumerate([[0, 1], [2, 3]]):
                attention(seg, 2, lambda j, g, si=si: (KTp[:, g, (si * 2 + j) * 128:(si * 2 + j + 1) * 128],
                                                        Vp[:, si * 2 + j, g * 128:(g + 1) * 128], [KTp, Vp]))
            while deferred:
                deferred.pop(0)()
            if pb_ == 0:
                dump("d_attT", arena[:, 8192:12288], [128, 4096], [attT], BF16)
                dump("d_QT", arena[:, 0:4096], [128, 4096], [QT], BF16)
            ck("p_att")
            hgrn_main([[0, 1], [2, 3]], False, 0, pb_ * 2)
            if pb_ == 0:
                dump("d_hgT", arena[:, 16384:20480], [128, 4096], [hgT], BF16)
            ck("p_hg")
            merge_block()
            if pb_ == 0:
                dump("d_mT", arena[:, 0:8192], [128, 8192], [mT], BF16)
            ck("p_merge")
            block_tail(row0, 0, pb_ % 2, row0, None if stop_after is not None else ((row0 + 512, 0) if pb_ + 1 < NPB else (ROW_S, 1)),
                       defer=(stop_after is None and pb_ + 1 < NPB))
            if pb_ == 0:
                dump("d_x1", x1_scr[0][:, :], [512, 2048], [x1_scr[0]])
                dump("d_fo", mo_scr[0][:, :], [512, 2048], [mo_scr[0]])
                dump("d_fT", arena[:, 0:22528], [128, 22528], [fT], BF16)
            ck("p_tail")

        ck("prompt")
        build_Cb(1)
        for h in range(8):
            P.dma("sp", SX[0][h][:], stA0[h, :, :], writes=[SX[0][h]])
            act(SXb[0][h][:], SX[0][h][:], AF.Copy, [SX[0][h]], [SXb[0][h]])
        for blk in range(NSB):
            row0 = ROW_S + blk * 512
            if stop_after is not None:
                load_norm_block(row0, 1)
            P.dma("sp", ropet[:], rope_d[blk * 512:(blk + 1) * 512].rearrange("(i p) c a d -> p i (c a d)", p=128), writes=[ropet])
            q_stage(True)
            ck("s_q")

            def chunk_src(j, g):
                kch, vch = kqr.next(), vqr.next()
                P.dma("sp", kch[:], KT_scr[g, :, j * 128:(j + 1) * 128], reads=[KT_scr], writes=[kch])
                P.dma("sp", vch[:], V_scr[j * 128:(j + 1) * 128, g * 128:(g + 1) * 128], reads=[V_scr], writes=[vch])
                return kch[:], vch[:], [kch, vch]
            attention([0, 1, 2, 3], 34, chunk_src)
            while deferred:
                deferred.pop(0)()
            ck("s_att")
            hgrn_main([[0, 1, 2, 3]], True, blk, 0)
            ck("s_hg")
            merge_block()
            block_tail(row0, 1, blk % 2, 1024 + blk * 512, None if stop_after is not None else ((row0 + 512, 1) if blk + 1 < NSB else None),
                       defer=(stop_after is None and blk + 1 < NSB))
            ck("s_tail")


    try:
        _program()
    except _Stop:
        pass
    P.emit()
    return nc


def _host_inputs(inp, cores=range(8)):
    f32 = np.float32
    x_prompt, x_sample = inp["x_prompt"], inp["x_sample"]
    w_in0 = inp["w_in"][0]
    half = 32
    inv = (10000.0 ** (-np.arange(half, dtype=f32) / half)).astype(f32)
    pos = np.arange(4096)
    r = (pos // 64).astype(f32)
    c = (pos % 64).astype(f32)
    ang = np.stack([r[:, None] * inv[None, :], c[:, None] * inv[None, :]], axis=1).astype(f32)
    rope_g = np.stack([np.cos(ang), np.sin(ang)], axis=1).astype(f32)
    s_idx = np.arange(128)
    same = (s_idx[:, None] // 32) == (s_idx[None, :] // 32)
    maskA = (same & (s_idx[:, None] <= s_idx[None, :])).astype(f32)
    maskB = (same & (s_idx[:, None] >= s_idx[None, :])).astype(f32)
    resetm = np.ones((128, 512), f32)
    resetm[:, ::32] = 0
    mask4 = (s_idx[:, None] // 32 == np.arange(4)[None, :]).astype(f32)
    cF = np.concatenate([np.eye(128, dtype=f32), maskA, maskB, resetm, mask4, np.ones((128, 512), f32)], axis=1)

    def fm(v, n):
        return np.ascontiguousarray(v.reshape(n, 128).T)

    cols = lambda a, b: w_in0[:, a:b]
    w_in_sw = np.ascontiguousarray(np.concatenate([cols(0, C_ZA), cols(C_ZB, C_VH), cols(C_ZA, C_ZB), cols(C_VH, NIN)], axis=1))
    rowb = np.ascontiguousarray(np.broadcast_to(np.stack([inp["q_norm"][0], inp["k_norm"][0], inp["hg_norm"][0]])[None], (128, 3, 128))).astype(f32)
    maps = []
    for j in cores:
        b, hf = j // 2, j % 2
        flip = hf == 1
        xp = x_prompt[4 * j:4 * j + 4]
        xs = x_sample[b]
        rp = rope_g
        if flip:
            xp = xp[:, ::-1]
            xs = xs[::-1]
            rp = rope_g[::-1]
        x_all = np.ascontiguousarray(np.concatenate([xp.reshape(1024, D), xs], axis=0))
        lbA, lbB = (inp["lb_bwd"], inp["lb_fwd"]) if flip else (inp["lb_fwd"], inp["lb_bwd"])
        vecT = np.concatenate([fm(inp["b_ada"][0], 96), fm(inp["norm_pre_mix"][0], 16), fm(inp["norm_post_mix"][0], 16),
                               fm(inp["norm_pre_ffn"][0], 16), fm(inp["norm_post_ffn"][0], 16),
                               fm(lbA[0], 8), fm(lbA[1], 8), fm(lbB[0], 8), fm(lbB[1], 8)], axis=1).astype(f32)
        condT = np.stack([fm(inp["c_ctx"], 16), fm(inp["c"][b], 16)], axis=2).astype(f32)
        stA, stB = (inp["state_bwd"], inp["state_fwd"]) if flip else (inp["state_fwd"], inp["state_bwd"])
        maps.append({
            "x_all": x_all, "rope": np.ascontiguousarray(rp),
            "cache_k": np.ascontiguousarray(inp["cache_k"][b, 0].reshape(256, 256)),
            "cache_v": np.ascontiguousarray(inp["cache_v"][b, 0].reshape(256, 256)),
            "stA0": np.ascontiguousarray(stA[b, 0]), "stB0": np.ascontiguousarray(stB[b, 0]),
            "condT": np.ascontiguousarray(condT), "vecT": np.ascontiguousarray(vecT), "rowb": rowb, "cF": cF,
            "w_ada": inp["w_ada"][0], "w_in": w_in_sw if flip else w_in0,
            "w_br_att": inp["w_br_att"][0], "w_br_hg": inp["w_br_hg"][0], "w_out": inp["w_out"][0],
            "w_ffn_in": inp["w_ffn_in"][0], "w_ffn_out": inp["w_ffn_out"][0],
        })
    return maps


def _assemble(res):
    f32 = np.float32
    y_prompt = np.zeros((32, 256, D), f32)
    y_sample = np.zeros((4, 4096, D), f32)
    nk = np.zeros((32, 1, 256, 2, 128), f32)
    nv = np.zeros((32, 1, 256, 2, 128), f32)
    sf = np.zeros((32, 1, 8, 128, 128), f32)
    sb = np.zeros((32, 1, 8, 128, 128), f32)
    for j in range(8):
        r = res[j]
        b, hf = j // 2, j % 2
        flip = hf == 1
        yp = r["y_own"][0:1024].reshape(4, 256, D)
        ys = r["y_own"][1024:3072]
        kc = r["kc_out"].reshape(4, 256, 2, 128)
        vc = r["vc_out"].reshape(4, 256, 2, 128)
        sA, sB = r["sA_out"], r["sB_out"]
        if flip:
            yp, ys, kc, vc = yp[:, ::-1], ys[::-1], kc[:, ::-1], vc[:, ::-1]
            sA, sB = sB, sA
        y_prompt[4 * j:4 * j + 4] = yp
        y_sample[b, hf * 2048:(hf + 1) * 2048] = ys
        nk[4 * j:4 * j + 4, 0] = kc
        nv[4 * j:4 * j + 4, 0] = vc
        sf[4 * j:4 * j + 4, 0] = sA
        sb[4 * j:4 * j + 4, 0] = sB
    return (y_prompt, y_sample, nk, nv, sf, sb)


def kernel(**inputs):
    inp = {k: np.asarray(v) for k, v in inputs.items()}
    nc = build()
    maps = _host_inputs(inp)
    res = run_bass_kernel_spmd(nc, maps, core_ids=list(range(8)))
    return _assemble(res.results)
```
